# Optimizing a Trainium2 kernel written in Bass

```python
import functools
import jax, jax.numpy as jnp
from jax import lax
import numpy as np

D_MODEL = 1024
BATCH = 32
SEQ = 2048
DEPTH = 1
DEC_BATCH = 32
DEC_SEQ = 16
PAST_LEN = 2048

CHUNK = 64
LRU_WIDTH = D_MODEL // 2
ATTN_WIDTH = D_MODEL - LRU_WIDTH
HEAD_DIM = 64
N_HEADS = ATTN_WIDTH // HEAD_DIM
N_LRU_BLOCKS = 8
LRU_BLOCK = LRU_WIDTH // N_LRU_BLOCKS
CONV_WIDTH = 4
LRU_C = 8.0
LEFT_CHUNKS = 8
BAND = (LEFT_CHUNKS + 1) * CHUNK
MAX_REL = 128
D_FF = ((8 * D_MODEL // 3 + 255) // 256) * 256
IN_COLS = 2 * LRU_WIDTH + 3 * ATTN_WIDTH
EPS = 1e-6
CACHE_ROWS = min(LEFT_CHUNKS * CHUNK, PAST_LEN)

kernel_name = 'hymba_rglru_chunkattn_stream_step'


def rms_norm(x, g):
    xf = x.astype(jnp.float32)
    y = xf * lax.rsqrt(jnp.mean(xf * xf, axis=-1, keepdims=True) + EPS)
    return (y * g.astype(jnp.float32)).astype(x.dtype)


def causal_conv(u, buf, w, b):
    T = u.shape[1]
    up = jnp.concatenate([buf.astype(u.dtype), u], axis=1)
    y = b + up[:, 0:T] * w[0]
    for k in range(1, CONV_WIDTH):
        y = y + up[:, k:k + T] * w[k]
    return y, up[:, -(CONV_WIDTH - 1):]


def rg_lru(u, h0, w_a, b_a, w_x, b_x, lam):
    B, T, _ = u.shape
    ub = u.reshape(B, T, N_LRU_BLOCKS, LRU_BLOCK)
    r = jax.nn.sigmoid(jnp.einsum('btnc,ncd->btnd', ub, w_a).reshape(B, T, LRU_WIDTH) + b_a)
    i = jax.nn.sigmoid(jnp.einsum('btnc,ncd->btnd', ub, w_x).reshape(B, T, LRU_WIDTH) + b_x)
    log_a = (LRU_C * r.astype(jnp.float32)) * jax.nn.log_sigmoid(lam.astype(jnp.float32))
    a = jnp.exp(log_a)
    gain = jnp.sqrt(-jnp.expm1(2.0 * log_a))
    bterm = gain * (i * u).astype(jnp.float32)
    bterm = bterm.at[:, 0].add(a[:, 0] * h0.astype(jnp.float32))

    def combine(c1, c2):
        a1, b1 = c1
        a2, b2 = c2
        return a1 * a2, a2 * b1 + b2

    _, h = lax.associative_scan(combine, (a, bterm), axis=1)
    return h.astype(u.dtype), h[:, -1].astype(h0.dtype)


def attend_prompt(q, k, v, table):
    B, T, H, Dh = q.shape
    n_chunks = T // CHUNK
    pad = LEFT_CHUNKS * CHUNK
    kp = jnp.pad(k, ((0, 0), (pad, 0), (0, 0), (0, 0)))
    vp = jnp.pad(v, ((0, 0), (pad, 0), (0, 0), (0, 0)))
    qc = q.reshape(B, n_chunks, CHUNK, H, Dh).transpose(1, 0, 2, 3, 4)
    dist = jnp.arange(CHUNK)[:, None] + pad - jnp.arange(BAND)[None, :]
    bias = table[:, jnp.clip(dist, -MAX_REL, MAX_REL) + MAX_REL].astype(jnp.float32)
    scale = HEAD_DIM ** -0.5

    def one_chunk(args):
        c, qb = args
        kb = lax.dynamic_slice_in_dim(kp, c * CHUNK, BAND, axis=1)
        vb = lax.dynamic_slice_in_dim(vp, c * CHUNK, BAND, axis=1)
        s = jnp.einsum('bqhd,bkhd->bhqk', qb, kb).astype(jnp.float32) * scale + bias
        valid = jnp.arange(BAND) >= pad - c * CHUNK
        s = jnp.where(valid[None, None, None, :], s, -1e30)
        p = jax.nn.softmax(s, axis=-1).astype(vb.dtype)
        return jnp.einsum('bhqk,bkhd->bqhd', p, vb)

    o = lax.map(one_chunk, (jnp.arange(n_chunks), qc))
    return o.transpose(1, 0, 2, 3, 4).reshape(B, T, H * Dh)


def attend_sample(q, k, v, k_cache, v_cache, table):
    B, T, H, Dh = q.shape
    R = k_cache.shape[1]
    kk = jnp.concatenate([k_cache.astype(k.dtype), k], axis=1)
    vv = jnp.concatenate([v_cache.astype(v.dtype), v], axis=1)
    qpos = PAST_LEN + jnp.arange(T)
    kpos = jnp.concatenate([PAST_LEN - R + jnp.arange(R), qpos])
    dist = jnp.clip(qpos[:, None] - kpos[None, :], -MAX_REL, MAX_REL) + MAX_REL
    bias = table[:, dist].astype(jnp.float32)
    s = jnp.einsum('bqhd,bkhd->bhqk', q, kk).astype(jnp.float32) * (HEAD_DIM ** -0.5) + bias
    p = jax.nn.softmax(s, axis=-1).astype(vv.dtype)
    return jnp.einsum('bhqk,bkhd->bqhd', p, vv).reshape(B, T, H * Dh)


def layer(x, conv_buf, h0, attend, norm_mix, w_in, conv_w, conv_b, lru_wa, lru_ba, lru_wx, lru_bx,
          lru_lambda, norm_lru_out, norm_attn_out, w_out, norm_ffn, w_gate, w_up, w_down):
    B, T, _ = x.shape
    xn = rms_norm(x, norm_mix)
    proj = xn @ w_in
    splits = [LRU_WIDTH, 2 * LRU_WIDTH, 2 * LRU_WIDTH + ATTN_WIDTH, 2 * LRU_WIDTH + 2 * ATTN_WIDTH]
    u, g, q, k, v = jnp.split(proj, splits, axis=-1)
    uc, conv_new = causal_conv(u, conv_buf, conv_w, conv_b)
    h, h_last = rg_lru(uc, h0, lru_wa, lru_ba, lru_wx, lru_bx, lru_lambda)
    y_lru = jax.nn.gelu(g) * h
    q = q.reshape(B, T, N_HEADS, HEAD_DIM)
    k = k.reshape(B, T, N_HEADS, HEAD_DIM)
    v = v.reshape(B, T, N_HEADS, HEAD_DIM)
    y_att = attend(q, k, v)
    mix = jnp.concatenate([rms_norm(y_lru, norm_lru_out), rms_norm(y_att, norm_attn_out)], axis=-1) @ w_out
    x = x + mix
    xn = rms_norm(x, norm_ffn)
    x = x + (jax.nn.silu(xn @ w_gate) * (xn @ w_up)) @ w_down
    return x, conv_new, h_last, k, v


def setup_inputs(seed: int = 0) -> dict:
    key = jax.random.key(seed)
    ks = jax.random.split(key, 24)
    f32 = jnp.float32

    def nrm(k, shape, s):
        return s * jax.random.normal(k, shape, f32)

    u = jax.random.uniform(ks[14], (DEPTH, LRU_WIDTH), f32, 0.9, 0.999)
    s = u ** (1.0 / LRU_C)
    lam = jnp.log(s) - jnp.log1p(-s)
    return {
        'x_prompt': nrm(ks[0], (BATCH, SEQ, D_MODEL), 1.0),
        'x_sample': nrm(ks[1], (DEC_BATCH, DEC_SEQ, D_MODEL), 1.0),
        'state_conv': nrm(ks[2], (DEPTH, DEC_BATCH, CONV_WIDTH - 1, LRU_WIDTH), 1.0),
        'state_lru': nrm(ks[3], (DEPTH, DEC_BATCH, LRU_WIDTH), 0.5),
        'cache_k': nrm(ks[4], (DEPTH, DEC_BATCH, CACHE_ROWS, N_HEADS, HEAD_DIM), 1.0),
        'cache_v': nrm(ks[5], (DEPTH, DEC_BATCH, CACHE_ROWS, N_HEADS, HEAD_DIM), 1.0),
        'norm_mix': 1.0 + nrm(ks[6], (DEPTH, D_MODEL), 0.05),
        'w_in': nrm(ks[7], (DEPTH, D_MODEL, IN_COLS), D_MODEL ** -0.5),
        'conv_w': nrm(ks[8], (DEPTH, CONV_WIDTH, LRU_WIDTH), CONV_WIDTH ** -0.5),
        'conv_b': nrm(ks[9], (DEPTH, LRU_WIDTH), 0.02),
        'lru_wa': nrm(ks[10], (DEPTH, N_LRU_BLOCKS, LRU_BLOCK, LRU_BLOCK), LRU_BLOCK ** -0.5),
        'lru_ba': nrm(ks[11], (DEPTH, LRU_WIDTH), 0.02),
        'lru_wx': nrm(ks[12], (DEPTH, N_LRU_BLOCKS, LRU_BLOCK, LRU_BLOCK), LRU_BLOCK ** -0.5),
        'lru_bx': nrm(ks[13], (DEPTH, LRU_WIDTH), 0.02),
        'lru_lambda': lam,
        'rel_bias': nrm(ks[15], (DEPTH, N_HEADS, 2 * MAX_REL + 1), 0.1),
        'norm_lru_out': 1.0 + nrm(ks[16], (DEPTH, LRU_WIDTH), 0.05),
        'norm_attn_out': 1.0 + nrm(ks[17], (DEPTH, ATTN_WIDTH), 0.05),
        'w_out': nrm(ks[18], (DEPTH, D_MODEL, D_MODEL), D_MODEL ** -0.5),
        'norm_ffn': 1.0 + nrm(ks[19], (DEPTH, D_MODEL), 0.05),
        'w_gate': nrm(ks[20], (DEPTH, D_MODEL, D_FF), D_MODEL ** -0.5),
        'w_up': nrm(ks[21], (DEPTH, D_MODEL, D_FF), D_MODEL ** -0.5),
        'w_down': nrm(ks[22], (DEPTH, D_FF, D_MODEL), D_FF ** -0.5),
        'norm_final': 1.0 + nrm(ks[23], (D_MODEL,), 0.05),
    }


def reference(x_prompt, x_sample, state_conv, state_lru, cache_k, cache_v, norm_mix, w_in, conv_w,
              conv_b, lru_wa, lru_ba, lru_wx, lru_bx, lru_lambda, rel_bias, norm_lru_out,
              norm_attn_out, w_out, norm_ffn, w_gate, w_up, w_down, norm_final):
    xp, xs = x_prompt, x_sample
    B, T, _ = xp.shape
    keep = min(LEFT_CHUNKS * CHUNK, T)
    p_conv, p_lru, p_k, p_v = [], [], [], []
    s_conv, s_lru, s_k, s_v = [], [], [], []
    for l in range(DEPTH):
        params = (norm_mix[l], w_in[l], conv_w[l], conv_b[l], lru_wa[l], lru_ba[l], lru_wx[l],
                  lru_bx[l], lru_lambda[l], norm_lru_out[l], norm_attn_out[l], w_out[l],
                  norm_ffn[l], w_gate[l], w_up[l], w_down[l])
        conv0 = jnp.zeros((B, CONV_WIDTH - 1, LRU_WIDTH), xp.dtype)
        h0 = jnp.zeros((B, LRU_WIDTH), xp.dtype)
        att_p = functools.partial(attend_prompt, table=rel_bias[l])
        xp, cp, hp, kp, vp = layer(xp, conv0, h0, att_p, *params)
        p_conv.append(cp)
        p_lru.append(hp)
        p_k.append(kp[:, T - keep:])
        p_v.append(vp[:, T - keep:])
        att_s = functools.partial(attend_sample, k_cache=cache_k[l], v_cache=cache_v[l], table=rel_bias[l])
        xs, cs, hs, ksn, vsn = layer(xs, state_conv[l], state_lru[l], att_s, *params)
        s_conv.append(cs)
        s_lru.append(hs)
        s_k.append(ksn)
        s_v.append(vsn)
    y_prompt = rms_norm(xp, norm_final)
    y_sample = rms_norm(xs, norm_final)
    return (y_prompt, y_sample,
            jnp.stack(p_conv), jnp.stack(p_lru), jnp.stack(p_k), jnp.stack(p_v),
            jnp.stack(s_conv), jnp.stack(s_lru), jnp.stack(s_k), jnp.stack(s_v))
```

```python
import numpy as np
from contextlib import ExitStack
import concourse.bass as bass
import concourse.mybir as mybir
from concourse.bass_utils import run_bass_kernel_spmd

F32 = mybir.dt.float32
BF16 = mybir.dt.bfloat16
AF = mybir.ActivationFunctionType
ALU = mybir.AluOpType

D = 1024
KC = 8
LW = 512
DFF = 2816
NF = 22
INC = 2560
EPS = 1e-6
NEG = -30000.0
NCD = dict(allow_slow_non_contiguous=True)
C0 = 0.7978845608028654
C1 = 0.7978845608028654 * 0.044715
B_IN = [0, 1, 2, 3]
B_INV = 4
B_INK = 5
B_OUT = [6, 7]
B_GU = list(range(8, 19))
B_DN = list(range(19, 25))
NBLK = 25
NSLOT = 6
OVERLAP_N1 = True


class Buf:
    def __init__(self, name=""):
        self.name = name
        self.w = {}
        self.r = {}


class Chan:
    def __init__(self, sem):
        self.sem = sem
        self.count = 0


class Prog:
    ENG = ("pe", "act", "dve", "pool", "sp")

    def __init__(self, nc, es):
        self.nc = nc
        self.streams = {e: [] for e in self.ENG}
        self.sem = {e: es.enter_context(nc.semaphore("sem_" + e)) for e in self.ENG if e != "sp"}
        self.cnt = {e: 0 for e in self.ENG}
        self.waited = {e: {} for e in self.ENG}
        self.semid = {}
        self.es = es
        self.nchan = 0

    def chan(self):
        self.nchan += 1
        return Chan(self.es.enter_context(self.nc.semaphore("ch%d" % self.nchan)))

    def _key(self, sem):
        k = id(sem)
        self.semid[k] = sem
        return k

    def _deps(self, eng, reads, writes, skip=None):
        deps = {}
        for b in reads:
            for k, v in b.w.items():
                deps[k] = max(deps.get(k, 0), v)
        for b in writes:
            for k, v in b.w.items():
                deps[k] = max(deps.get(k, 0), v)
            for k, v in b.r.items():
                deps[k] = max(deps.get(k, 0), v)
        pek = self._key(self.sem["pe"])
        for k, v in deps.items():
            if eng == "pe" and k == pek:
                continue
            if skip is not None and k == skip[0] and v < skip[1]:
                continue
            if self.waited[eng].get(k, 0) >= v:
                continue
            self.waited[eng][k] = v
            sem = self.semid[k]
            self.streams[eng].append(lambda e, s=sem, vv=v: e.wait_ge(s, vv))

    def _mark(self, k, v, reads, writes):
        for b in reads:
            b.r[k] = max(b.r.get(k, 0), v)
        for b in writes:
            b.w = {k: v}
            b.r = {}

    mute = False

    def op(self, eng, fn, reads=(), writes=(), signal=True):
        if self.mute:
            return
        reads = [b for b in reads if b is not None]
        writes = [b for b in writes if b is not None]
        self._deps(eng, reads, writes)
        sem = self.sem[eng]
        k = self._key(sem)
        if signal:
            self.cnt[eng] += 1
            v = self.cnt[eng]
            self.streams[eng].append(lambda e, f=fn, s=sem: f(e).then_inc(s, 1))
        else:
            v = self.cnt[eng] + 1
            self.streams[eng].append(lambda e, f=fn: f(e))
        self._mark(k, v, reads, writes)

    def dma(self, q, ch, out, in_, reads=(), writes=(), **kw):
        if self.mute:
            return
        reads = [b for b in reads if b is not None]
        writes = [b for b in writes if b is not None]
        self._deps(q, reads, writes, skip=(self._key(ch.sem), ch.count))
        ch.count += 16
        sem = ch.sem
        self.streams[q].append(
            lambda e, o=out, i=in_, s=sem, kw=kw: e.dma_start(out=o, in_=i, **kw).then_inc(s, 16))
        self._mark(self._key(sem), ch.count, reads, writes)

    def seal(self, ch, bufs):
        k = self._key(ch.sem)
        for b in bufs:
            if k in b.w:
                b.w[k] = ch.count

    def wait_all(self, eng, chans):
        for ch in chans:
            if ch.count:
                k = self._key(ch.sem)
                if self.waited[eng].get(k, 0) < ch.count:
                    self.waited[eng][k] = ch.count
                    self.streams[eng].append(lambda e, s=ch.sem, v=ch.count: e.wait_ge(s, v))

    def flush(self, block):
        m = {"sp": block.sync, "act": block.scalar, "dve": block.vector, "pool": block.gpsimd,
             "pe": block.tensor}
        for e in self.ENG:
            lst = self.streams[e]
            if not lst:
                continue
            self.streams[e] = []

            def body(eng, lst=lst):
                for f in lst:
                    f(eng)
            m[e](body)


def build(NSEQ, SEQ, NS, DSEQ=16, dbg=None):
    NT = SEQ // 512
    nc = bass.Bass("TRN2", target_bir_lowering=False)
    dt = lambda n, s, k="ExternalInput", d=F32: nc.dram_tensor(n, s, d, kind=k).ap()
    xp = dt("x_prompt", [NSEQ * SEQ, D])
    xs = dt("x_sample", [NS * DSEQ, D])
    state_conv = dt("state_conv", [NS, 3, LW])
    state_lru = dt("state_lru", [NS, LW])
    cache_k = dt("cache_k", [NS, 512, 512])
    cache_v = dt("cache_v", [NS, 512, 512])
    norm_mix = dt("norm_mix", [D])
    w_in = dt("w_in", [D, INC])
    conv_w = dt("conv_w", [4, LW])
    conv_b = dt("conv_b", [LW])
    lru_wa = dt("lru_wa", [8, 64, 64])
    lru_ba = dt("lru_ba", [LW])
    lru_wx = dt("lru_wx", [8, 64, 64])
    lru_bx = dt("lru_bx", [LW])
    lru_lambda = dt("lru_lambda", [LW])
    rel_bias = dt("rel_bias", [8, 257])
    norm_lru_out = dt("norm_lru_out", [LW])
    norm_attn_out = dt("norm_attn_out", [LW])
    w_out = dt("w_out", [D, D])
    norm_ffn = dt("norm_ffn", [D])
    w_gate = dt("w_gate", [D, DFF])
    w_up = dt("w_up", [D, DFF])
    w_down = dt("w_down", [DFF, D])
    norm_final = dt("norm_final", [D])
    O = "ExternalOutput"
    y_prompt = dt("y_prompt", [NSEQ * SEQ, D], O)
    y_sample = dt("y_sample", [NS * DSEQ, D], O)
    prompt_conv = dt("prompt_conv", [NSEQ, 3, LW], O)
    prompt_lru = dt("prompt_lru", [NSEQ, LW], O)
    prompt_k = dt("prompt_k", [NSEQ, 512, 512], O)
    prompt_v = dt("prompt_v", [NSEQ, 512, 512], O)
    sample_conv = dt("sample_conv", [NS, 3, LW], O)
    sample_lru = dt("sample_lru", [NS, LW], O)
    sample_k = dt("sample_k", [NS * DSEQ, 512], O)
    sample_v = dt("sample_v", [NS * DSEQ, 512], O)
    wsc = dt("wsc", [NBLK, 128, 4096], "Internal", BF16)
    ext = dt("ext", [8, 640], "Internal")
    ext2 = dt("ext2", [8, 128, 640], "Internal")

    with ExitStack() as es:
        P = Prog(nc, es)
        sb = lambda n, s, d=F32: es.enter_context(nc.sbuf_tensor(n, s, d))
        wring = sb("wring", [128, NSLOT, 4096], BF16)
        X = sb("X", [128, 4, D])
        xin = sb("xin", [128, 4, D])
        actT = sb("actT", [128, 8, 512], BF16)
        upre = sb("upre", [128, 4, 515])
        gbuf = sb("gbuf", [128, 4, 512])
        qT = sb("qT", [128, 8, 512], BF16)
        kT = sb("kT", [128, 4, 1024], BF16)
        vext = sb("vext", [128, 8, 8, 66], BF16)
        big = sb("big", [128, NF * 512], BF16)
        ucb = sb("ucb", [128, 2, 512], BF16)
        pT = sb("pT", [128, 4, 512], BF16)
        ya = sb("ya", [128, 4, 512])
        yab = sb("yab", [128, 2, 512], BF16)
        xsb = sb("xsb", [128, 2, D], BF16)
        biasT = sb("biasT", [128, 3, 8, 128], BF16)
        biasN = sb("biasN", [64, 8, 64], BF16)
        scrF = sb("scrF", [128, 4, 512])
        gfin = sb("gfin", [128, D])
        gw = sb("gw", [128, 2, 4, 128], BF16)
        ident = sb("ident", [128, 128], BF16)
        ones = sb("ones", [128, 2], BF16)
        cst = sb("cst", [128, 64])
        st = sb("st", [128, 96])
        hprev = sb("hprev", [128, 4, 4])
        gains = sb("gains", [128, 3, 8])
        pp = lambda n: es.enter_context(nc.psum_tensor(n, [128, 512], F32))
        psA = [pp("psA%d" % i) for i in range(4)]
        psS = [pp("psS%d" % i) for i in range(2)]
        psO = [pp("psO%d" % i) for i in range(2)]

        CW = lambda j, c: cst[:, j * 4 + c: j * 4 + c + 1]
        CB = lambda c: cst[:, 16 + c: 17 + c]
        HBA = lambda c: cst[:, 20 + c: 21 + c]
        HBX = lambda c: cst[:, 24 + c: 25 + c]
        C8 = lambda c: cst[:, 28 + c: 29 + c]
        HC8 = lambda c: cst[:, 32 + c: 33 + c]
        LAM = cst[:, 36:40]
        EM05 = cst[:, 40:41]
        EP05 = cst[:, 41:42]
        CBH = cst[:, 48:56]
        b_cst = Buf("cst")
        b_gains = Buf("gains")

        ring_b = [Buf("ring%d" % i) for i in range(NSLOT)]
        X_b = [Buf() for _ in range(4)]
        xin_b = [Buf() for _ in range(4)]
        actT_b = [Buf() for _ in range(4)]
        upre_b = [Buf() for _ in range(4)]
        gbuf_b = [Buf() for _ in range(4)]
        qT_b = [Buf() for _ in range(4)]
        kT_b = [[Buf(), Buf()] for _ in range(4)]
        vext_b = [Buf() for _ in range(8)]
        hT_b = [Buf() for _ in range(NF)]
        ucb_b = [Buf(), Buf()]
        pT_b = [Buf() for _ in range(4)]
        ya_b = [Buf() for _ in range(4)]
        yab_b = [Buf(), Buf()]
        xsb_b = [Buf(), Buf()]
        scrF_b = [Buf() for _ in range(4)]
        psA_b = [Buf() for _ in range(4)]
        psS_b = [Buf() for _ in range(2)]
        psO_b = [Buf() for _ in range(2)]
        wsc_b = [Buf() for _ in range(NBLK)]
        b_bias = Buf()
        b_const = Buf()
        hprev_b = [Buf() for _ in range(4)]
        st_b = {}

        def stb(name):
            if name not in st_b:
                st_b[name] = Buf(name)
            return st_b[name]
        SS1, R1, SS2, R2, SS3, R3, SSA, SSL, RL, SCA, TMP, RS = (0, 4, 8, 12, 16, 20, 24, 28, 32, 36, 40, 48)

        def tmpv(q, k):
            i = q * 5 + k
            return big[:, i * 1024:(i + 1) * 1024].bitcast(F32)

        def tmpb(q, k):
            i = q * 5 + k
            return [hT_b[2 * i], hT_b[2 * i + 1]]

        def y2v(i):
            return big[:, (20 + i) * 512:(21 + i) * 512]

        def y2b(i):
            return [hT_b[20 + i]]
        hTv = lambda f: big[:, f * 512:(f + 1) * 512]

        rrA = [0]

        def nextA():
            i = rrA[0] % 4
            rrA[0] += 1
            return i
        ch_misc = P.chan()
        ch_ring = [P.chan() for _ in range(NSLOT)]
        ch_xin = [P.chan() for _ in range(4)]
        ch_X = P.chan()
        ch_y = [P.chan() for _ in range(4)]
        ch_kv = [P.chan() for _ in range(4)]
        ch_conv = [P.chan() for _ in range(4)]
        ch_lru = [P.chan() for _ in range(4)]
        ch_state = P.chan()
        ch_ck = P.chan()
        ch_cv = P.chan()
        ch_e = [P.chan() for _ in range(3)]
        store_chans = ch_y + ch_kv + ch_conv + ch_lru

        with ExitStack() as es1:
            sb1 = lambda n, s, d=F32: es1.enter_context(nc.sbuf_tensor(n, s, d))
            stage = wring[:, 0:4, :].rearrange("p (q a) c -> p q (a c)", q=2).bitcast(F32)
            obf = wring[:, 4:6, :]
            gst = ya[:, 0:2, :].rearrange("p a (j c) -> p a j c", j=4)
            identf = sb1("identf", [128, 128])
            E8 = sb1("E8", [8, 640])
            B34 = scrF[:].rearrange("p a c -> p (a c)").rearrange("p (i h c) -> p i h c", i=2, h=8)
            BNf = sb1("BNf", [64, 8, 64])
            stage_b = [Buf(), Buf()]
            obf_b = [Buf(), Buf()]
            b_gst, b_identf, b_E8, b_B34, b_BNf, b_ext, b_ext2 = (Buf() for _ in range(7))
            ch_stg = [P.chan(), P.chan()]
            ch_obf = [P.chan(), P.chan()]
            block = es1.enter_context(nc.Block())

            for gi, (src, lo) in enumerate([(norm_mix, None), (norm_ffn, None)]):
                P.dma("sp", ch_misc, gains[:, gi, :], src.rearrange("(k p) -> p k", p=128),
                      writes=[b_gains], **NCD)
            P.dma("sp", ch_misc, gains[:, 2, 0:4], norm_lru_out.rearrange("(k p) -> p k", p=128),
                  writes=[b_gains], **NCD)
            P.dma("sp", ch_misc, gains[:, 2, 4:8], norm_attn_out.rearrange("(k p) -> p k", p=128),
                  writes=[b_gains], **NCD)
            P.dma("sp", ch_misc, cst[:, 0:16].rearrange("p (j c) -> p j c", j=4),
                  conv_w.rearrange("j (c p) -> p j c", p=128), writes=[b_cst], **NCD)
            for col, src in [(16, conv_b), (20, lru_ba), (24, lru_bx), (36, lru_lambda)]:
                P.dma("sp", ch_misc, cst[:, col:col + 4], src.rearrange("(c p) -> p c", p=128),
                      writes=[b_cst], **NCD)
            P.dma("sp", ch_misc, CBH, bass.AP(tensor=rel_bias.tensor, offset=256, ap=[[0, 128], [257, 8]]),
                  writes=[b_cst], **NCD)
            P.dma("sp", ch_misc, gfin[:], bass.AP(tensor=norm_final.tensor, offset=0, ap=[[0, 128], [1, D]]),
                  writes=[b_const])
            P.op("pool", lambda e: e.memset(gst[:], 0.0), writes=[b_gst])
            for g, src in enumerate([lru_wa, lru_wx]):
                for n in range(8):
                    j, hf = n // 2, n % 2
                    P.dma("sp", ch_misc, gst[hf * 64:(hf + 1) * 64, g, j, hf * 64:(hf + 1) * 64], src[n],
                          writes=[b_gst])
            P.dma("sp", ch_misc, E8[:, 0:257], rel_bias, writes=[b_E8])
            P.seal(ch_misc, [b_gains, b_cst, b_const, b_gst, b_E8])
            P.op("pool", lambda e: e.memset(cst[:, 40:41], -0.5), writes=[b_cst])
            P.op("pool", lambda e: e.memset(cst[:, 41:42], 0.5), writes=[b_cst])
            P.op("dve", lambda e: e.tensor_scalar(out=cst[:, 20:28], in0=cst[:, 20:28], scalar1=0.5, scalar2=None,
                                                  op0=ALU.mult), reads=[], writes=[b_cst])
            P.op("act", lambda e: e.activation(out=cst[:, 44:48], in_=LAM, func=AF.Exp, scale=-1.0), writes=[b_cst])
            P.op("act", lambda e: e.activation(out=cst[:, 44:48], in_=cst[:, 44:48], func=AF.Ln, bias=1.0),
                 writes=[b_cst])
            P.op("dve", lambda e: e.tensor_scalar(out=cst[:, 28:32], in0=cst[:, 44:48], scalar1=-8.0, scalar2=None,
                                                  op0=ALU.mult), writes=[b_cst])
            P.op("dve", lambda e: e.tensor_scalar(out=cst[:, 32:36], in0=cst[:, 44:48], scalar1=-4.0, scalar2=None,
                                                  op0=ALU.mult), writes=[b_cst])
            P.op("dve", lambda e: e.tensor_copy(out=gw[:], in_=gst[:]), reads=[b_gst], writes=[b_const])
            P.op("pool", lambda e: e.memset(identf[:], 0.0), writes=[b_identf])
            P.op("pool", lambda e: e.affine_select(out=identf[:], in_=identf[:], pattern=[[-1, 128]],
                                                   compare_op=ALU.not_equal, fill=1.0, base=0,
                                                   channel_multiplier=1), writes=[b_identf])
            P.op("dve", lambda e: e.tensor_copy(out=ident[:], in_=identf[:]), reads=[b_identf], writes=[b_const])
            P.op("pool", lambda e: e.memset(ones[:], 1.0), writes=[b_const])
            for blk in range(8):
                P.op("pool", lambda e, blk=blk: e.memset(vext[:, blk, :, 64:65], 1.0), writes=[vext_b[blk]])
            P.op("pool", lambda e: e.memset(qT[:], 0.0), writes=qT_b)
            P.op("pool", lambda e: e.memset(hprev[:], 0.0), writes=hprev_b)
            P.op("pool", lambda e: e.memset(upre[:], 0.0), writes=upre_b)
            SKIPB = dbg is not None and "nobias" in dbg
            SKIPW = dbg is not None and "noweights" in dbg
            P.mute = SKIPB
            P.op("dve", lambda e: e.tensor_copy(out=E8[:, 257:640], in_=E8[:, 256:257].to_broadcast([8, 383])),
                 writes=[b_E8])
            P.dma("sp", ch_e[0], ext, E8[:], reads=[b_E8], writes=[b_ext])
            P.dma("sp", ch_e[1], ext2, bass.AP(tensor=ext.tensor, offset=0, ap=[[640, 8], [0, 128], [1, 640]]),
                  reads=[b_ext], writes=[b_ext2])
            for i, base in enumerate([256, 128]):
                P.dma("sp", ch_e[2], B34[:, i, :, :],
                      bass.AP(tensor=ext2.tensor, offset=base, ap=[[639, 128], [81920, 8], [1, 128]]),
                      reads=[b_ext2], writes=[b_B34])
            P.op("pool", lambda e: e.memset(BNf[:], NEG), writes=[b_BNf])
            for s in range(4):
                P.dma("sp", ch_e[2], BNf[16 * s:16 * s + 16, :, 16 * s:16 * s + 16],
                      bass.AP(tensor=ext2.tensor, offset=128, ap=[[639, 16], [81920, 8], [1, 16]]),
                      reads=[b_ext2], writes=[b_BNf])
            P.seal(ch_e[2], [b_B34, b_BNf])
            for i in range(2):
                P.op("dve", lambda e, i=i: e.tensor_tensor(
                    out=B34[:, i, :, :], in0=B34[:, i, :, :],
                    in1=CBH.unsqueeze(2).to_broadcast([128, 8, 128]), op=ALU.subtract),
                    reads=[b_cst], writes=[b_B34])
                P.op("dve", lambda e, i=i: e.tensor_scalar(out=biasT[:, 1 + i, :, :], in0=B34[:, i, :, :], scalar1=8.0,
                                                          scalar2=None, op0=ALU.mult), reads=[b_B34], writes=[b_bias])
            P.op("pool", lambda e: e.memset(biasT[:, 0, :, :], 0.0), writes=[b_bias])
            P.op("pool", lambda e: e.memset(biasT[0:64, 0, :, 64:128], NEG), writes=[b_bias])
            P.op("pool", lambda e: e.memset(biasT[64:128, 2, :, 0:64], NEG), writes=[b_bias])
            P.op("dve", lambda e: e.tensor_tensor(out=BNf[:], in0=BNf[:],
                                                  in1=CBH[0:64, :].unsqueeze(2).to_broadcast([64, 8, 64]),
                                                  op=ALU.subtract), reads=[b_cst], writes=[b_BNf])
            P.op("dve", lambda e: e.tensor_scalar(out=biasN[:], in0=BNf[:], scalar1=8.0, scalar2=None, op0=ALU.mult),
                 reads=[b_BNf], writes=[b_bias])

            P.mute = SKIPW
            w_in_r = w_in.rearrange("(k p) c -> p k c", p=128)
            w_out_r = w_out.rearrange("(k p) c -> p k c", p=128)
            w_g_r = w_gate.rearrange("(k p) c -> p k c", p=128)
            w_u_r = w_up.rearrange("(k p) c -> p k c", p=128)
            w_d_r = w_down.rearrange("(k p) c -> p k c", p=128)
            eng_rr = ["act", "dve", "act", "dve", "act"]
            cnt = [0]

            def conv_block(blk, loads, gi, views, plain=False, used=4096):
                q = blk % 2
                for dst, src in loads:
                    P.dma("sp", ch_stg[q], dst, src, writes=[stage_b[q]])
                if plain:
                    eng = eng_rr[cnt[0] % 5]
                    cnt[0] += 1
                    if eng == "act":
                        P.op(eng, lambda e: e.activation(out=obf[:, q, 0:used], in_=stage[:, q, 0:used], func=AF.Copy),
                             reads=[stage_b[q]], writes=[obf_b[q]])
                    else:
                        P.op(eng, lambda e: e.tensor_copy(out=obf[:, q, 0:used], in_=stage[:, q, 0:used]),
                             reads=[stage_b[q]], writes=[obf_b[q]])
                else:
                    for k, (iv, ov, gk) in enumerate(views):
                        eng = eng_rr[cnt[0] % 5]
                        cnt[0] += 1
                        gsc = gains[:, gi, gk:gk + 1]
                        if eng == "act":
                            P.op(eng, lambda e, iv=iv, ov=ov, gsc=gsc: e.activation(out=ov, in_=iv, func=AF.Copy,
                                                                                 scale=gsc),
                                 reads=[stage_b[q], b_gains], writes=[obf_b[q]])
                        else:
                            P.op(eng, lambda e, iv=iv, ov=ov, gsc=gsc: e.tensor_scalar(out=ov, in0=iv, scalar1=gsc,
                                                                                    scalar2=None, op0=ALU.mult),
                                 reads=[stage_b[q], b_gains], writes=[obf_b[q]])
                P.dma("pool", ch_obf[q], wsc[blk][:, 0:used], obf[:, q, 0:used], reads=[obf_b[q]], writes=[wsc_b[blk]])

            def v5(t, q, a, b):
                return t[:, q, :].rearrange("p (a b c) -> p a b c", a=a, b=b)
            for i in range(4):
                blk = B_IN[i]
                q = blk % 2
                sv = stage[:, q, :].rearrange("p (k a c) -> p k a c", k=8, a=4)
                ov = v5(obf, q, 4, 8)
                loads = [(stage[:, q, :].rearrange("p (k c) -> p k c", k=8), w_in_r[:, :, i * 512:(i + 1) * 512])]
                conv_block(blk, loads, 0, [(sv[:, k, :, :], ov[:, :, k, :], k) for k in range(8)])
            for blk, c0 in [(B_INV, 2048), (B_INK, 1536)]:
                q = blk % 2
                sv = stage[:, q, :].rearrange("p (k c) -> p k c", k=8)
                ov = obf[:, q, :].rearrange("p (k c) -> p k c", k=8)
                conv_block(blk, [(sv, w_in_r[:, :, c0:c0 + 512])], 0,
                           [(sv[:, k, :], ov[:, k, :], k) for k in range(8)])
            for j in range(2):
                blk = B_OUT[j]
                q = blk % 2
                sv = stage[:, q, :].rearrange("p (k c) -> p k c", k=4)
                ov = obf[:, q, :].rearrange("p (k c) -> p k c", k=4)
                conv_block(blk, [(sv, w_out_r[:, 4 * j:4 * j + 4, :])], 2,
                           [(sv[:, k, :], ov[:, k, :], 4 * j + k) for k in range(4)])
            for i in range(11):
                blk = B_GU[i]
                q = blk % 2
                sv = stage[:, q, :].rearrange("p (g k f c) -> p g k f c", g=2, k=8, f=2)
                ov = obf[:, q, :].rearrange("p (f g k c) -> p f g k c", f=2, g=2, k=8)
                sl = stage[:, q, :].rearrange("p (g k c) -> p g k c", g=2, k=8)
                loads = [(sl[:, 0, :, :], w_g_r[:, :, 2 * i * 128:(2 * i + 2) * 128]),
                         (sl[:, 1, :, :], w_u_r[:, :, 2 * i * 128:(2 * i + 2) * 128])]
                conv_block(blk, loads, 1, [(sv[:, :, k, :, :].rearrange("p g f c -> p f g c"), ov[:, :, :, k, :], k)
                                           for k in range(8)])
            for i in range(6):
                blk = B_DN[i]
                q = blk % 2
                nk = 4 if i < 5 else 2
                sv = stage[:, q, 0:nk * 1024].rearrange("p (k c) -> p k c", k=nk)
                conv_block(blk, [(sv, w_d_r[:, 4 * i:4 * i + nk, :])], None, None, plain=True, used=nk * 1024)
            P.mute = False
            for e in ("sp", "pool", "act", "dve", "pe"):
                P.wait_all(e, [ch_misc] + ch_e + ch_stg + ch_obf)
            P.flush(block)

        block = es.enter_context(nc.Block())
        load_seq = []
        next_load = [0]

        def emit_loads_upto(n):
            while next_load[0] < min(n, len(load_seq)):
                idx = next_load[0]
                blk = load_seq[idx]
                i = idx % NSLOT
                used = 2048 if blk == B_DN[5] else 4096
                P.dma("sp", ch_ring[i], wring[:, i, 0:used], wsc[blk][:, 0:used], reads=[wsc_b[blk]],
                      writes=[ring_b[i]])
                next_load[0] += 1

        def consumed(T, blk):
            emit_loads_upto(T["lpos"][blk] + NSLOT + 1)

        def pool_rsqrt(dst_col, src_col, n, scale, nrow=128, srcs=(), dsts=()):
            P.op("pool", lambda e: e.tensor_scalar(out=st[0:nrow, TMP:TMP + n], in0=st[0:nrow, src_col:src_col + n],
                                                   scalar1=scale, scalar2=EPS, op0=ALU.mult, op1=ALU.add),
                 reads=list(srcs), writes=[stb("tmp")])
            P.op("pool", lambda e: e.tensor_tensor(out=st[0:nrow, dst_col:dst_col + n], in0=st[0:nrow, TMP:TMP + n],
                                                   in1=EM05[0:nrow, :].to_broadcast([nrow, n]), op=ALU.pow),
                 reads=[stb("tmp"), b_cst], writes=list(dsts))

        def norm_transpose(T, b, src_ap, src_b, ss_col, r_col, tag):
            nr = T["nr"]
            q = T["xq"][0] % 2
            T["xq"][0] += 1
            P.op("act", lambda e: e.activation(out=xsb[0:nr, q, :], in_=src_ap, func=AF.Square,
                                               accum_out=st[0:nr, ss_col + b:ss_col + b + 1]),
                 reads=[src_b], writes=[xsb_b[q], stb(tag + "ss%d" % b)])
            pool_rsqrt(r_col + b, ss_col + b, 1, 1.0 / D, nr, [stb(tag + "ss%d" % b)], [stb(tag + "r%d" % b)])
            P.op("dve", lambda e: e.tensor_scalar(out=xsb[0:nr, q, :], in0=src_ap, scalar1=st[0:nr, r_col + b:r_col + b + 1],
                                                  scalar2=None, op0=ALU.mult),
                 reads=[src_b, stb(tag + "r%d" % b)], writes=[xsb_b[q]])
            a = nextA()
            pv = psA[a][:].bitcast(BF16).rearrange("p (k c) -> p k c", k=8)
            for k in range(8):
                P.op("pe", lambda e, k=k: e.transpose(out=pv[:, k, 0:nr], in_=xsb[0:nr, q, k * 128:(k + 1) * 128],
                                                      identity=ident[0:nr, 0:nr]),
                     reads=[xsb_b[q], b_const], writes=[psA_b[a]], signal=(k == 7))
            eng = "act" if b % 2 == 0 else "dve"
            dst = actT[:, :, b * 128:b * 128 + nr]
            if eng == "act":
                P.op("act", lambda e: e.activation(out=dst, in_=pv[:, :, 0:nr], func=AF.Copy),
                     reads=[psA_b[a]], writes=[actT_b[b]])
            else:
                P.op("dve", lambda e: e.tensor_copy(out=dst, in_=pv[:, :, 0:nr]), reads=[psA_b[a]], writes=[actT_b[b]])

        def run_tile(T):
            kind = T["kind"]
            NTOK, nb, nr = T["ntok"], T["nb"], T["nr"]
            nseg, L = T["nseg"], T["L"]
            s_idx, t_idx = T["s"], T["t"]
            last = T["last"]
            first = T["first"]
            want_kv = last
            row0 = T["row0"]
            xsrc = T["xsrc"]
            ydst = T["ydst"]
            TS = slice(0, NTOK)
            cur_slot = {blk: idx % NSLOT for blk, idx in T["lpos"].items()}
            if kind == "p":
                P.dma("sp", ch_X, X[:, :, :], xsrc[row0:row0 + 512, :].rearrange("(b p) d -> p b d", p=128),
                      writes=X_b)
            else:
                P.dma("sp", ch_X, X[0:nr, 0, :], xsrc[row0:row0 + nr, :], writes=[X_b[0]])
            if not T.get("n1_done"):
                for b in range(nb):
                    q = T["xinq"][b]
                    norm_transpose(T, b, xin[0:nr, q, :], xin_b[q], SS1, R1, "n1")
            nxt = T.get("next")
            if dbg is not None and dbg.endswith("_n1"):
                P.mute = True
            u_view = lambda c: upre[:, c, 0:nseg * (3 + L)].rearrange("p (s l) -> p s l", s=nseg)
            woff = T["woff"]
            def lv(ap):
                return ap.rearrange("p (s l) -> p s l", s=nseg)

            def lru_front(c):
                q = c % 2
                T4 = tmpv(q, 3)[:, TS]
                B4 = tmpb(q, 3)
                uv = u_view(c)
                P.op("dve", lambda e: e.tensor_scalar(
                    out=lv(T4), in0=uv[:, :, 3:3 + L], scalar1=CW(3, c), scalar2=CB(c), op0=ALU.mult, op1=ALU.add),
                    reads=[upre_b[c], b_cst], writes=B4)
                for j in (2, 1, 0):
                    P.op("dve", lambda e, j=j: e.scalar_tensor_tensor(
                        out=lv(T4), in0=uv[:, :, j:j + L], scalar=CW(j, c), in1=lv(T4), op0=ALU.mult, op1=ALU.add),
                        reads=[upre_b[c], b_cst], writes=B4)
                if last:
                    for sg in range(nseg):
                        dstc = T["convdst"][sg][:, c * 128:(c + 1) * 128].rearrange("j p -> p j")
                        P.dma("pool", ch_conv[c], dstc, uv[:, sg, L:L + 3], reads=[upre_b[c]], **NCD)
                if kind == "p" and not last:
                    P.op("pool", lambda e: e.tensor_copy(out=upre[:, c, 0:3], in_=upre[:, c, 512:515]),
                         writes=[upre_b[c]])
                elif kind == "p" and last:
                    P.op("pool", lambda e: e.memset(upre[:, c, 0:3], 0.0), writes=[upre_b[c]])
                P.op("act", lambda e: e.activation(out=ucb[:, q, TS], in_=T4, func=AF.Copy), reads=B4,
                     writes=[ucb_b[q]])
                ar, ai = nextA(), nextA()
                T["gates"][c] = (ar, ai)
                P.op("pe", lambda e: e.matmul(psA[ar][:, TS], lhsT=gw[:, 0, c, :], rhs=ucb[:, q, TS],
                                              start=True, stop=True),
                     reads=[ucb_b[q], b_const], writes=[psA_b[ar]])
                P.op("pe", lambda e: e.matmul(psA[ai][:, TS], lhsT=gw[:, 1, c, :], rhs=ucb[:, q, TS],
                                              start=True, stop=True),
                     reads=[ucb_b[q], b_const], writes=[psA_b[ai]])

            def lru_chainA(c):
                q = c % 2
                T1, T2, T3, T4, T5 = (tmpv(q, k)[:, TS] for k in range(5))
                B1, B2, B3, B4, B5 = (tmpb(q, k) for k in range(5))
                ar, ai = T["gates"][c]
                P.op("act", lambda e: e.activation(out=T5, in_=gbuf[:, c, TS], func=AF.Square),
                     reads=[gbuf_b[c]], writes=B5)
                P.op("pool", lambda e: e.tensor_scalar(out=T5, in0=T5, scalar1=C1, scalar2=C0, op0=ALU.mult,
                                                       op1=ALU.add), writes=B5)
                P.op("pool", lambda e: e.tensor_tensor(out=T5, in0=T5, in1=gbuf[:, c, TS], op=ALU.mult),
                     reads=[gbuf_b[c]], writes=B5)
                P.op("act", lambda e: e.activation(out=T1, in_=psA[ar][:, TS], func=AF.Tanh, bias=HBA(c), scale=0.5),
                     reads=[psA_b[ar], b_cst], writes=B1)
                P.op("act", lambda e: e.activation(out=T2, in_=psA[ai][:, TS], func=AF.Tanh, bias=HBX(c), scale=0.5),
                     reads=[psA_b[ai], b_cst], writes=B2)
                P.op("act", lambda e: e.activation(out=T3, in_=T1, func=AF.Exp, bias=HC8(c), scale=HC8(c)),
                     reads=B1 + [b_cst], writes=B3)
                P.op("act", lambda e: e.activation(out=T1, in_=T1, func=AF.Exp, bias=C8(c), scale=C8(c)),
                     reads=[b_cst], writes=B1)
                P.op("act", lambda e: e.activation(out=T5, in_=T5, func=AF.Tanh), writes=B5)
                P.op("dve", lambda e: e.scalar_tensor_tensor(out=T2, in0=T2, scalar=1.0, in1=T4, op0=ALU.add,
                                                             op1=ALU.mult), reads=B4, writes=B2)
                P.op("dve", lambda e: e.scalar_tensor_tensor(out=T5, in0=T5, scalar=1.0, in1=gbuf[:, c, TS],
                                                             op0=ALU.add, op1=ALU.mult), reads=[gbuf_b[c]], writes=B5)

            def lru_sqrt(c):
                q = c % 2
                T1 = tmpv(q, 0)[:, TS]
                P.op("act", lambda e: e.activation(out=T1, in_=T1, func=AF.Sqrt, bias=0.25, scale=-0.25),
                     writes=tmpb(q, 0))

            def lru_chainB(c):
                q = c % 2
                T1, T2, T3, T4, T5 = (tmpv(q, k)[:, TS] for k in range(5))
                B1, B2, B3, B4, B5 = (tmpb(q, k) for k in range(5))
                P.op("dve", lambda e: e.tensor_tensor(out=T2, in0=T2, in1=T1, op=ALU.mult), reads=B1, writes=B2)
                for sg in range(nseg):
                    init = hprev[:, c, sg:sg + 1]
                    P.op("dve", lambda e, sg=sg, init=init: e.tensor_tensor_scan(
                        out=lv(T4)[:, sg, :], data0=lv(T3)[:, sg, :], data1=lv(T2)[:, sg, :], initial=init,
                        op0=ALU.mult, op1=ALU.add), reads=B3 + B2 + [hprev_b[c]], writes=B4)
                if last:
                    for sg in range(nseg):
                        dsth = T["lrudst"][sg][c * 128:(c + 1) * 128].rearrange("(p o) -> p o", o=1)
                        P.dma("pool", ch_lru[c], dsth, lv(T4)[:, sg, L - 1:L], reads=B4, **NCD)
                if kind == "p":
                    if not last:
                        P.op("pool", lambda e: e.tensor_copy(out=hprev[:, c, 0:1], in_=T4[:, NTOK - 1:NTOK]),
                             reads=B4, writes=[hprev_b[c]])
                    else:
                        P.op("pool", lambda e: e.memset(hprev[:, c, 0:1], 0.0), writes=[hprev_b[c]])
                P.op("dve", lambda e: e.scalar_tensor_tensor(out=actT[:, c, TS], in0=T5, scalar=0.5, in1=T4,
                                                             op0=ALU.mult, op1=ALU.mult),
                     reads=B5 + B4, writes=actT_b[0:nb])
                P.op("pool", lambda e: e.tensor_tensor(out=y2v(q)[:, TS], in0=actT[:, c, TS], in1=actT[:, c, TS],
                                                       op=ALU.mult), reads=actT_b[0:nb], writes=y2b(q))

            def lru_ssl(c):
                q = c % 2
                for b in range(nb):
                    P.op("pe", lambda e, b=b: e.matmul(
                        psO[1][0:nr, 300 + 4 * b + c:301 + 4 * b + c], lhsT=y2v(q)[:, b * 128:b * 128 + nr],
                        rhs=ones[:, 0:1], start=True, stop=True, skip_group_check=True),
                        reads=y2b(q) + [b_const], writes=[psO_b[1]], signal=(b == nb - 1))

            def lru_finish():
                P.op("dve", lambda e: e.tensor_reduce(
                    out=st[0:nr, SSL:SSL + nb], in_=psO[1][0:nr, 300:300 + 4 * nb].rearrange("p (b c) -> p b c", c=4),
                    axis=mybir.AxisListType.X, op=ALU.add), reads=[psO_b[1]], writes=[stb("ssl")])
                P.op("pool", lambda e: e.tensor_scalar(out=st[0:nr, TMP + 4:TMP + 4 + nb], in0=st[0:nr, SSL:SSL + nb],
                                                       scalar1=1.0 / LW, scalar2=EPS, op0=ALU.mult, op1=ALU.add),
                     reads=[stb("ssl")], writes=[stb("tl")])
                P.op("pool", lambda e: e.tensor_tensor(out=st[0:nr, RL:RL + nb], in0=st[0:nr, TMP + 4:TMP + 4 + nb],
                                                       in1=EM05[0:nr, :].to_broadcast([nr, nb]), op=ALU.pow),
                     reads=[stb("tl"), b_cst], writes=[stb("rl")])
                P.op("pool", lambda e: e.tensor_tensor(out=st[0:nr, 56:56 + nb], in0=st[0:nr, TMP + 4:TMP + 4 + nb],
                                                       in1=EP05[0:nr, :].to_broadcast([nr, nb]), op=ALU.pow),
                     reads=[stb("tl"), b_cst], writes=[stb("sql")])

            for m in range(16):
                if m == 8:
                    for c_ in (0, 1):
                        lru_front(c_)
                    for c_ in (0, 1):
                        lru_chainA(c_)
                    for c_ in (0, 1):
                        lru_sqrt(c_)
                blk = B_IN[m // 4]
                sl = cur_slot[blk]
                wv = wring[:, sl, :].rearrange("p (a k c) -> p a k c", a=4, k=8)
                a = nextA()
                for k in range(8):
                    P.op("pe", lambda e, a=a, wv=wv, m=m, k=k: e.matmul(psA[a][:, TS], lhsT=wv[:, m % 4, k, :],
                                                                        rhs=actT[:, k, TS], start=(k == 0),
                                                                        stop=(k == 7)),
                         reads=[ring_b[sl]] + actT_b[0:nb], writes=[psA_b[a]], signal=(k == 7))
                if m % 4 == 3:
                    consumed(T, blk)
                c = m % 4
                if m < 4:
                    dst = u_view(c)[:, :, 3:3 + L]
                    src = psA[a][:, TS].rearrange("p (s l) -> p s l", s=nseg)
                    P.op("act", lambda e, dst=dst, src=src: e.activation(out=dst, in_=src, func=AF.Copy),
                         reads=[psA_b[a]], writes=[upre_b[c]])
                elif m < 8:
                    P.op("act", lambda e, a=a, c=c: e.activation(out=gbuf[:, c, TS], in_=psA[a][:, TS], func=AF.Copy),
                         reads=[psA_b[a]], writes=[gbuf_b[c]])
                elif m < 12:
                    for e2 in range(2):
                        P.op("act", lambda e, a=a, c=c, e2=e2: e.activation(
                            out=qT[e2 * 64:(e2 + 1) * 64, 2 * c + e2, TS], in_=psA[a][e2 * 64:(e2 + 1) * 64, TS],
                            func=AF.Copy), reads=[psA_b[a]], writes=[qT_b[c]])
                else:
                    P.op("act", lambda e, a=a, c=c: e.activation(out=kT[:, c, woff:woff + NTOK], in_=psA[a][:, TS],
                                                                 func=AF.Copy),
                         reads=[psA_b[a]], writes=[kT_b[c][T["whalf"]]])
            if dbg is not None and dbg.endswith("_in1"):
                P.mute = True
            for which in (["v", "k"] if want_kv else ["v"]):
                blk = B_INV if which == "v" else B_INK
                sl = cur_slot[blk]
                wv = wring[:, sl, :].rearrange("p (k c) -> p k c", k=8)
                for b in range(nb):
                    a = nextA()
                    for k in range(8):
                        P.op("pe", lambda e, a=a, wv=wv, b=b, k=k: e.matmul(
                            psA[a][0:nr, :], lhsT=actT[:, k, b * 128:b * 128 + nr], rhs=wv[:, k, :],
                            start=(k == 0), stop=(k == 7)),
                            reads=[ring_b[sl], actT_b[b]], writes=[psA_b[a]], signal=(k == 7))
                    if which == "v" and not (dbg is not None and "novext" in dbg):
                        vb = T["vblk"][b]
                        if True:
                            P.op("act", lambda e, a=a, vb=vb: e.activation(
                                out=vext[0:nr, vb, :, 0:64], in_=psA[a][0:nr, :].rearrange("p (h d) -> p h d", h=8),
                                func=AF.Copy), reads=[psA_b[a]], writes=[vext_b[vb]])
                        else:
                            P.op("dve", lambda e, a=a, vb=vb: e.tensor_copy(
                                out=vext[0:nr, vb, :, 0:64], in_=psA[a][0:nr, :].rearrange("p (h d) -> p h d", h=8)),
                                reads=[psA_b[a]], writes=[vext_b[vb]])
                    if want_kv:
                        sq = (b * 2 + (0 if which == "v" else 1)) % 4
                        P.op("act", lambda e, a=a, sq=sq: e.activation(out=scrF[0:nr, sq, :], in_=psA[a][0:nr, :],
                                                                      func=AF.Copy),
                             reads=[psA_b[a]], writes=[scrF_b[sq]])
                        dst = T["kvdst"][which][b]
                        if not (dbg is not None and "nokvst" in dbg):
                            P.dma("pool", ch_kv[sq], dst, scrF[0:nr, sq, :], reads=[scrF_b[sq]])
                consumed(T, blk)
            if dbg is not None and dbg.endswith("_in"):
                P.mute = True
            if nxt is not None:
                for b in range(nxt["nb"]):
                    q = nxt["xinq"][b]
                    r0 = nxt["row0"] + b * 128
                    P.dma("sp", ch_xin[q], xin[0:nxt["nr"], q, :], nxt["xsrc"][r0:r0 + nxt["nr"], :],
                          writes=[xin_b[q]])
            def finish_group(yq, hg, po_, nrq):
                P.op("dve", lambda e: e.reciprocal(out=st[0:nrq, RS + hg * 4:RS + hg * 4 + 4],
                                                   in_=psO[po_][0:nrq, 0:260].rearrange("p (h d) -> p h d", h=4)[:, :, 64]),
                     reads=[psO_b[po_]], writes=[stb("rs%d" % hg)])
                P.op("dve", lambda e: e.tensor_tensor(
                    out=ya[0:nrq, yq, hg * 256:(hg + 1) * 256].rearrange("p (h d) -> p h d", h=4),
                    in0=psO[po_][0:nrq, 0:260].rearrange("p (h d) -> p h d", h=4)[:, :, 0:64],
                    in1=st[0:nrq, RS + hg * 4:RS + hg * 4 + 4].unsqueeze(2).to_broadcast([nrq, 4, 64]), op=ALU.mult),
                    reads=[psO_b[po_], stb("rs%d" % hg)], writes=[ya_b[yq]])

            def finish_blocks(blocks, nrq):
                for b in blocks:
                    bq = b % 2
                    P.op("act", lambda e, b=b, bq=bq: e.activation(out=yab[0:nrq, bq, :], in_=ya[0:nrq, b, :],
                                                                   func=AF.Square,
                                                                   accum_out=st[0:nrq, SSA + b:SSA + b + 1]),
                         reads=[ya_b[b]], writes=[yab_b[bq], stb("ssa%d" % b)])
                nbk = len(blocks)
                b0 = blocks[0]
                P.op("pool", lambda e: e.tensor_scalar(out=st[0:nrq, TMP:TMP + nbk], in0=st[0:nrq, SSA + b0:SSA + b0 + nbk],
                                                       scalar1=1.0 / LW, scalar2=EPS, op0=ALU.mult, op1=ALU.add),
                     reads=[stb("ssa%d" % b) for b in blocks], writes=[stb("tmp")])
                P.op("pool", lambda e: e.tensor_tensor(out=st[0:nrq, SCA + b0:SCA + b0 + nbk], in0=st[0:nrq, TMP:TMP + nbk],
                                                       in1=EM05[0:nrq, :].to_broadcast([nrq, nbk]), op=ALU.pow),
                     reads=[stb("tmp"), b_cst], writes=[stb("sca")])
                P.op("pool", lambda e: e.tensor_tensor(out=st[0:nrq, SCA + b0:SCA + b0 + nbk],
                                                       in0=st[0:nrq, SCA + b0:SCA + b0 + nbk],
                                                       in1=st[0:nrq, 56 + b0:56 + b0 + nbk], op=ALU.mult),
                     reads=[stb("sql")], writes=[stb("sca")])
                for b in blocks:
                    bq = b % 2
                    P.op("dve", lambda e, b=b, bq=bq: e.tensor_scalar(
                        out=yab[0:nrq, bq, :], in0=ya[0:nrq, b, :], scalar1=st[0:nrq, SCA + b:SCA + b + 1], scalar2=None,
                        op0=ALU.mult), reads=[ya_b[b], stb("sca")], writes=[yab_b[bq]])
                    a = nextA()
                    pv = psA[a][:].bitcast(BF16).rearrange("p (k c) -> p k c", k=8)
                    for hp in range(4):
                        P.op("pe", lambda e, hp=hp, pv=pv, bq=bq: e.transpose(
                            out=pv[:, hp, 0:nrq], in_=yab[0:nrq, bq, hp * 128:(hp + 1) * 128],
                            identity=ident[0:nrq, 0:nrq]),
                            reads=[yab_b[bq], b_const], writes=[psA_b[a]], signal=(hp == 3))
                    P.op("act" if b % 2 == 0 else "dve", (lambda e, b=b, pv=pv: e.activation(
                        out=actT[:, 4:8, b * 128:b * 128 + nrq], in_=pv[:, 0:4, 0:nrq], func=AF.Copy)) if b % 2 == 0 else
                        (lambda e, b=b, pv=pv: e.tensor_copy(out=actT[:, 4:8, b * 128:b * 128 + nrq], in_=pv[:, 0:4, 0:nrq])),
                        reads=[psA_b[a]], writes=[actT_b[b]])

            def outproj(b):
                a0, a1 = nextA(), nextA()
                for k in range(8):
                    sl = cur_slot[B_OUT[k // 4]]
                    wv = wring[:, sl, :].rearrange("p (k c) -> p k c", k=4)
                    for half, a in enumerate((a0, a1)):
                        P.op("pe", lambda e, a=a, k=k, wv=wv, half=half: e.matmul(
                            psA[a][0:nr, :], lhsT=actT[:, k, b * 128:b * 128 + nr],
                            rhs=wv[:, k % 4, half * 512:(half + 1) * 512], start=(k == 0), stop=(k == 7)),
                            reads=[ring_b[sl], actT_b[b]], writes=[psA_b[a]], signal=(k == 7))
                for half, a in enumerate((a0, a1)):
                    P.op("dve", lambda e, a=a, half=half: e.scalar_tensor_tensor(
                        out=X[0:nr, b, half * 512:(half + 1) * 512], in0=psA[a][0:nr, :],
                        scalar=st[0:nr, RL + b:RL + b + 1], in1=X[0:nr, b, half * 512:(half + 1) * 512],
                        op0=ALU.mult, op1=ALU.add), reads=[psA_b[a], stb("rl")], writes=[X_b[b]])
                if b == nb - 1:
                    for blk in B_OUT:
                        consumed(T, blk)

            def n2(b):
                norm_transpose(T, b, X[0:nr, b, :], X_b[b], SS2, R2, "n2")

            if kind == "p":
                units = []
                for m in range(4):
                    for hg in range(2):
                        kbs = [kb for kb in range(5) if 4 * t_idx + m - 4 + kb >= 0]
                        for kb in kbs:
                            units.append((m, hg, kb, kb == kbs[0], kb == kbs[-1]))

                def qk(u, i):
                    m, hg, kb, fst, lst = u
                    sS = i % 2
                    ablk = (4 * t_idx + m - 4 + kb) % 8
                    kcol = ablk * 128
                    khalf = ablk // 4
                    biased = kb in (0, 3, 4)
                    kbi = {0: 0, 3: 1, 4: 2}.get(kb, 0)
                    sv = psS[sS][:].rearrange("p (j q) -> p j q", j=4)
                    for j in range(4):
                        h = hg * 4 + j
                        hp, po = h // 2, (h % 2) * 64
                        P.op("pe", lambda e, j=j, hp=hp, h=h: e.matmul(
                            sv[:, j, :], lhsT=kT[:, hp, kcol:kcol + 128],
                            rhs=qT[:, h, m * 128:(m + 1) * 128], start=(j == 0),
                            stop=(not biased and j == 3), skip_group_check=True),
                            reads=[kT_b[hp][khalf], qT_b[hp]], writes=[psS_b[sS]],
                            signal=(not biased and j == 3))
                    if biased:
                        for j in range(4):
                            h = hg * 4 + j
                            P.op("pe", lambda e, j=j, h=h: e.matmul(sv[:, j, :], lhsT=ident[:], rhs=biasT[:, kbi, h, :],
                                                                    start=False, stop=(j == 3), skip_group_check=True),
                                 reads=[b_bias, b_const], writes=[psS_b[sS]], signal=(j == 3))
                    sP = i % 4
                    P.op("act", lambda e: e.activation(out=pT[:, sP, :], in_=psS[sS][:], func=AF.Exp, scale=0.125),
                         reads=[psS_b[sS]], writes=[pT_b[sP]])

                def pv_(u, i):
                    m, hg, kb, fst, lst = u
                    sP = i % 4
                    ablk = (4 * t_idx + m - 4 + kb) % 8
                    ov = psO[hg][:, 0:260].rearrange("p (h d) -> p h d", h=4)
                    for j in range(4):
                        h = hg * 4 + j
                        P.op("pe", lambda e, j=j, h=h: e.matmul(
                            ov[:, j, :], lhsT=pT[:, sP, j * 128:(j + 1) * 128], rhs=vext[:, ablk, h, 0:65],
                            start=(fst and j == 0), stop=(lst and j == 3), skip_group_check=True),
                            reads=[pT_b[sP], vext_b[ablk]], writes=[psO_b[hg]], signal=(j == 3))
                    if lst:
                        finish_group(m, hg, hg, 128)
                def att(m, hooks=None):
                    hooks = dict(hooks or {})
                    um = [(i, u) for i, u in enumerate(units) if u[0] == m]
                    n = len(um)
                    for x in range(min(2, n)):
                        qk(um[x][1], um[x][0])
                    for x in range(n):
                        pv_(um[x][1], um[x][0])
                        if x + 2 < n:
                            qk(um[x + 2][1], um[x + 2][0])
                        if x in hooks:
                            hooks.pop(x)()
                    for x in sorted(hooks):
                        hooks[x]()

                lru_chainB(0)
                lru_chainB(1)
                lru_front(2)
                lru_front(3)
                att(0)
                lru_ssl(0)
                lru_chainA(2)
                att(1)
                lru_ssl(1)
                lru_chainA(3)
                lru_sqrt(2)
                lru_sqrt(3)
                lru_chainB(2)
                att(2)
                lru_ssl(2)
                lru_chainB(3)

                def h_a():
                    lru_ssl(3)
                    lru_finish()

                def h_b():
                    finish_blocks([0, 1, 2], 128)

                def h_c():
                    outproj(0)

                def h_d():
                    outproj(1)
                    n2(0)

                def h_e():
                    outproj(2)
                    n2(1)

                def h_f():
                    n2(2)
                att(3, {1: h_a, 3: h_b, 4: h_c, 5: h_d, 6: h_e, 7: h_f})
                finish_blocks([3], 128)
                outproj(3)
                n2(3)
            else:
                cstK = xin[:, 0:2, :].rearrange("p a (b c) -> p (a b) c", b=2)
                cstV = xin[:, 2:4, :].rearrange("p a (b c) -> p (a b) c", b=2)
                ui = [0]
                fstO = [True, True]

                def load_cache(sq_):
                    P.dma("sp", ch_ck, cstK, cache_k[sq_].rearrange("(cb p) f -> p cb f", p=128), writes=xin_b[0:2])
                    P.dma("sp", ch_cv, cstV, cache_v[sq_].rearrange("(cb p) f -> p cb f", p=128), writes=xin_b[2:4])
                    for half in range(2):
                        qx = T["xq"][0] % 2
                        T["xq"][0] += 1
                        xv = xsb[:, qx, :].rearrange("p (cb f) -> p cb f", cb=2)
                        P.op("dve", lambda e, xv=xv, half=half: e.tensor_copy(out=xv, in_=cstK[:, 2 * half:2 * half + 2, :]),
                             reads=xin_b[0:2], writes=[xsb_b[qx]])
                        a = nextA()
                        pv = psA[a][:].bitcast(BF16).rearrange("p (k c) -> p k c", k=8)
                        for cbl in range(2):
                            for hp in range(4):
                                P.op("pe", lambda e, cbl=cbl, hp=hp, xv=xv, pv=pv: e.transpose(
                                    out=pv[:, cbl * 4 + hp, :], in_=xv[:, cbl, hp * 128:(hp + 1) * 128], identity=ident[:]),
                                    reads=[xsb_b[qx], b_const], writes=[psA_b[a]], signal=(cbl == 1 and hp == 3))
                        for cbl in range(2):
                            cb = 2 * half + cbl
                            P.op("act", lambda e, cb=cb, cbl=cbl, pv=pv: e.activation(
                                out=kT[:, :, cb * 128:(cb + 1) * 128], in_=pv[:, cbl * 4:cbl * 4 + 4, :], func=AF.Copy),
                                reads=[psA_b[a]], writes=[kT_b[hp][0] for hp in range(4)])
                    for cb in range(4):
                        P.op("act", lambda e, cb=cb: e.activation(
                            out=vext[:, cb, :, 0:64], in_=cstV[:, cb, :].rearrange("p (h d) -> p h d", h=8),
                            func=AF.Copy), reads=xin_b[2:4], writes=[vext_b[cb]])

                def cache_unit(sq_, hg, cb):
                    i = ui[0]
                    ui[0] += 1
                    sS, sP = i % 2, i % 4
                    sv = psS[sS][:].rearrange("p (j q) -> p j q", j=4)
                    biased = cb == 3
                    for j in range(4):
                        h = hg * 4 + j
                        hp, po = h // 2, (h % 2) * 64
                        P.op("pe", lambda e, j=j, hp=hp, h=h: e.matmul(
                            sv[:, j, 0:16], lhsT=kT[:, hp, cb * 128:(cb + 1) * 128],
                            rhs=qT[:, h, sq_ * 16:sq_ * 16 + 16], start=(j == 0),
                            stop=(not biased and j == 3), skip_group_check=True),
                            reads=[kT_b[hp][0], qT_b[hp]], writes=[psS_b[sS]], signal=(not biased and j == 3))
                    if biased:
                        for j in range(4):
                            h = hg * 4 + j
                            P.op("pe", lambda e, j=j, h=h: e.matmul(
                                sv[:, j, 0:16], lhsT=ident[:], rhs=biasT[:, 1, h, 0:16], start=False,
                                stop=(j == 3), skip_group_check=True),
                                reads=[b_bias, b_const], writes=[psS_b[sS]], signal=(j == 3))
                    pview = pT[:, sP, 0:256].rearrange("p (j q) -> p j q", j=4)
                    P.op("pool", lambda e: e.memset(pT[:, sP, 0:256], 0.0), writes=[pT_b[sP]])
                    P.op("act", lambda e: e.activation(out=pview[:, :, sq_ * 16:sq_ * 16 + 16], in_=sv[:, :, 0:16],
                                                       func=AF.Exp, scale=0.125),
                         reads=[psS_b[sS]], writes=[pT_b[sP]])
                    ov = psO[hg][:, 0:260].rearrange("p (h d) -> p h d", h=4)
                    f0 = fstO[hg]
                    fstO[hg] = False
                    for j in range(4):
                        h = hg * 4 + j
                        P.op("pe", lambda e, j=j, h=h: e.matmul(
                            ov[0:64, j, :], lhsT=pview[:, j, :], rhs=vext[:, cb, h, 0:65],
                            start=(f0 and j == 0), stop=False, skip_group_check=True),
                            reads=[pT_b[sP], vext_b[cb]], writes=[psO_b[hg]], signal=(j == 3))

                def new_unit(hg):
                    i = ui[0]
                    ui[0] += 1
                    sS, sP = i % 2, i % 4
                    sv = psS[sS][:].rearrange("p (j q) -> p j q", j=4)
                    for j in range(4):
                        h = hg * 4 + j
                        hp, po = h // 2, (h % 2) * 64
                        P.op("pe", lambda e, j=j, hp=hp, h=h: e.matmul(
                            sv[0:64, j, 0:64], lhsT=kT[:, hp, 512:576], rhs=qT[:, h, 0:64],
                            start=(j == 0), stop=False, skip_group_check=True),
                            reads=[kT_b[hp][1], qT_b[hp]], writes=[psS_b[sS]], signal=False)
                    for j in range(4):
                        h = hg * 4 + j
                        P.op("pe", lambda e, j=j, h=h: e.matmul(
                            sv[0:64, j, 0:64], lhsT=ident[0:64, 0:64], rhs=biasN[:, h, :], start=False, stop=(j == 3),
                            skip_group_check=True), reads=[b_bias, b_const], writes=[psS_b[sS]], signal=(j == 3))
                    pview = pT[:, sP, 0:256].rearrange("p (j q) -> p j q", j=4)
                    P.op("act", lambda e: e.activation(out=pview[0:64, :, :], in_=sv[0:64, :, 0:64], func=AF.Exp,
                                                       scale=0.125), reads=[psS_b[sS]], writes=[pT_b[sP]])
                    ov = psO[hg][:, 0:260].rearrange("p (h d) -> p h d", h=4)
                    for j in range(4):
                        h = hg * 4 + j
                        P.op("pe", lambda e, j=j, h=h: e.matmul(
                            ov[0:64, j, :], lhsT=pview[0:64, j, :], rhs=vext[0:64, 4, h, 0:65], start=False,
                            stop=(j == 3), skip_group_check=True),
                            reads=[pT_b[sP], vext_b[4]], writes=[psO_b[hg]], signal=(j == 3))
                    finish_group(0, hg, hg, 64)

                for pr in range(2):
                    cs = (2 * pr, 2 * pr + 1)
                    if pr == 1:
                        for c in cs:
                            lru_front(c)
                        for c in cs:
                            lru_chainA(c)
                        for c in cs:
                            lru_sqrt(c)
                    for c in cs:
                        lru_chainB(c)
                        lru_ssl(c)
                lru_finish()
                for sq_ in range(NS):
                    load_cache(sq_)
                    for hg in range(2):
                        for cb in range(4):
                            cache_unit(sq_, hg, cb)
                for hg in range(2):
                    new_unit(hg)
                finish_blocks([0], 64)
                outproj(0)
                n2(0)

            if dbg is not None and dbg.endswith("_att"):
                P.mute = True
            if dbg is not None and dbg.endswith("_out"):
                P.mute = True
            for f in range(NF):
                i = f // 2
                sl = cur_slot[B_GU[i]]
                wv = wring[:, sl, :].rearrange("p (a k c) -> p a k c", a=4, k=8)
                ag, au = nextA(), nextA()
                for gu, a in enumerate((ag, au)):
                    for k in range(8):
                        P.op("pe", lambda e, a=a, k=k, wv=wv, gu=gu, f=f: e.matmul(
                            psA[a][:, TS], lhsT=wv[:, (f % 2) * 2 + gu, k, :], rhs=actT[:, k, TS], start=(k == 0),
                            stop=(k == 7)), reads=[ring_b[sl]] + actT_b[0:nb], writes=[psA_b[a]], signal=(k == 7))
                tq = f % 2
                P.op("act", lambda e, ag=ag, tq=tq: e.activation(out=scrF[:, tq, TS], in_=psA[ag][:, TS], func=AF.Tanh,
                                                                 scale=0.5), reads=[psA_b[ag]], writes=[scrF_b[tq]])
                P.op("dve", lambda e, ag=ag, tq=tq: e.scalar_tensor_tensor(
                    out=scrF[:, 2 + tq, TS], in0=scrF[:, tq, TS], scalar=1.0, in1=psA[ag][:, TS], op0=ALU.add,
                    op1=ALU.mult), reads=[scrF_b[tq], psA_b[ag]], writes=[scrF_b[2 + tq]])
                als = [hT_b[f]]
                P.op("dve", lambda e, au=au, tq=tq, f=f: e.scalar_tensor_tensor(
                    out=hTv(f)[:, TS], in0=scrF[:, 2 + tq, TS], scalar=0.5, in1=psA[au][:, TS], op0=ALU.mult,
                    op1=ALU.mult), reads=[scrF_b[2 + tq], psA_b[au]], writes=als)
                if f % 2 == 1:
                    consumed(T, B_GU[i])
            if dbg is not None and dbg.endswith("_ffn"):
                P.mute = True
            for b in range(nb):
                a0, a1 = nextA(), nextA()
                for kf in range(NF):
                    sl = cur_slot[B_DN[kf // 4]]
                    wv = wring[:, sl, :].rearrange("p (k c) -> p k c", k=4)
                    for half, a in enumerate((a0, a1)):
                        P.op("pe", lambda e, a=a, kf=kf, wv=wv, b=b, half=half: e.matmul(
                            psA[a][0:nr, :], lhsT=hTv(kf)[:, b * 128:b * 128 + nr],
                            rhs=wv[:, kf % 4, half * 512:(half + 1) * 512], start=(kf == 0), stop=(kf == NF - 1)),
                            reads=[ring_b[sl], hT_b[kf]], writes=[psA_b[a]], signal=(kf == NF - 1))
                    if b == nb - 1 and (kf % 4 == 3 or kf == NF - 1):
                        consumed(T, B_DN[kf // 4])
                if nxt is not None and b < nxt["nb"] and OVERLAP_N1:
                    qn = nxt["xinq"][b]
                    norm_transpose(nxt, b, xin[0:nxt["nr"], qn, :], xin_b[qn], SS1, R1, "n1")
                    nxt["n1_done"] = True
                for half, a in enumerate((a0, a1)):
                    P.op("dve", lambda e, a=a, b=b, half=half: e.tensor_tensor(
                        out=X[0:nr, b, half * 512:(half + 1) * 512], in0=psA[a][0:nr, :],
                        in1=X[0:nr, b, half * 512:(half + 1) * 512], op=ALU.add), reads=[psA_b[a]], writes=[X_b[b]])
                q = T["xq"][0] % 2
                T["xq"][0] += 1
                P.op("act", lambda e, b=b, q=q: e.activation(out=xsb[0:nr, q, :], in_=X[0:nr, b, :], func=AF.Square,
                                                             accum_out=st[0:nr, SS3 + b:SS3 + b + 1]),
                     reads=[X_b[b]], writes=[xsb_b[q], stb("n3ss%d" % b)])
                pool_rsqrt(R3 + b, SS3 + b, 1, 1.0 / D, nr, [stb("n3ss%d" % b)], [stb("n3r%d" % b)])
                P.op("dve", lambda e, b=b: e.scalar_tensor_tensor(
                    out=X[0:nr, b, :], in0=X[0:nr, b, :], scalar=st[0:nr, R3 + b:R3 + b + 1], in1=gfin[0:nr, :],
                    op0=ALU.mult, op1=ALU.mult), reads=[stb("n3r%d" % b), b_const], writes=[X_b[b]])
                r0 = row0 + b * 128
                P.dma("pool", ch_y[b], ydst[r0:r0 + nr, :], X[0:nr, b, :], reads=[X_b[b]])

        tiles = []
        xq = [0]
        for s in range(NSEQ):
            for t in range(NT):
                half = t % 2
                tiles.append(dict(
                    kind="p", s=s, t=t, ntok=512, nb=4, nr=128, nseg=1, L=512, first=(t == 0), last=(t == NT - 1),
                    row0=s * SEQ + t * 512, xsrc=xp, ydst=y_prompt, woff=half * 512, whalf=half,
                    vblk=[(4 * t + b) % 8 for b in range(4)], xinq=[0, 1, 2, 3], xq=xq, y2={}, yaq=0, gates={},
                    kvdst={"k": [prompt_k[s][b * 128:(b + 1) * 128, :] for b in range(4)],
                           "v": [prompt_v[s][b * 128:(b + 1) * 128, :] for b in range(4)]},
                    convdst=[prompt_conv[s]], lrudst=[prompt_lru[s]]))
        nr_s = NS * DSEQ
        tiles.append(dict(
            kind="s", s=0, t=0, ntok=nr_s, nb=1, nr=nr_s, nseg=NS, L=DSEQ, first=True, last=True, row0=0, xsrc=xs,
            ydst=y_sample, woff=512, whalf=1, vblk=[4], xinq=[0], xq=xq, y2={}, yaq=0, gates={},
            kvdst={"k": [sample_k[:, :]], "v": [sample_v[:, :]]},
            convdst=[sample_conv[sg] for sg in range(NS)], lrudst=[sample_lru[sg] for sg in range(NS)]))
        for i, T in enumerate(tiles):
            T["next"] = tiles[i + 1] if i + 1 < len(tiles) else None
            seq = B_IN + [B_INV] + ([B_INK] if T["last"] else []) + B_OUT + B_GU + B_DN
            T["lpos"] = {blk: len(load_seq) + j for j, blk in enumerate(seq)}
            load_seq.extend(seq)
        if not (dbg is not None and dbg.startswith("setup")):
            emit_loads_upto(NSLOT)
        T0 = tiles[0]
        for b in range(T0["nb"] if not (dbg is not None and dbg.startswith("setup")) else 0):
            q = T0["xinq"][b]
            r0 = T0["row0"] + b * 128
            P.dma("sp", ch_xin[q], xin[0:T0["nr"], q, :], T0["xsrc"][r0:r0 + T0["nr"], :], writes=[xin_b[q]])
        if dbg is not None and dbg.startswith("setup"):
            tiles = []
        if dbg is not None and dbg.startswith("p1"):
            tiles = tiles[:1]
        for T in tiles:
            if T["kind"] == "s":
                for sg in range(NS):
                    for c in range(4):
                        P.dma("sp", ch_state, upre[:, c, sg * 19:sg * 19 + 3],
                              state_conv[sg][:, c * 128:(c + 1) * 128].rearrange("j p -> p j"),
                              writes=[upre_b[c]], allow_slow_non_contiguous=True)
                for c in range(4):
                    P.dma("sp", ch_state, hprev[:, c, 0:NS], state_lru[:, c * 128:(c + 1) * 128].rearrange("s p -> p s"),
                          writes=[hprev_b[c]], allow_slow_non_contiguous=True)
                P.seal(ch_state, upre_b + hprev_b)
            run_tile(T)
        P.mute = False
        for e in ("pool",):
            P.wait_all(e, store_chans + ch_ring + ch_xin + [ch_X, ch_state, ch_ck, ch_cv])
        P.flush(block)
    return nc


_CACHE = {}


def kernel(**inputs):
    NC = 8
    B, SEQ, _ = inputs["x_prompt"].shape
    DB, DSEQ, _ = inputs["x_sample"].shape
    NSEQ = B // NC
    NS = DB // NC
    key = (NSEQ, SEQ, NS, DSEQ)
    if key not in _CACHE:
        _CACHE[key] = build(NSEQ, SEQ, NS, DSEQ)
    nc = _CACHE[key]
    f = lambda a: np.ascontiguousarray(a, dtype=np.float32)
    shared = {}
    for n in ["norm_mix", "w_in", "conv_w", "conv_b", "lru_wa", "lru_ba", "lru_wx", "lru_bx", "lru_lambda",
              "rel_bias", "norm_lru_out", "norm_attn_out", "w_out", "norm_ffn", "w_gate", "w_up", "w_down"]:
        shared[n] = f(inputs[n][0])
    shared["norm_final"] = f(inputs["norm_final"])
    in_maps = []
    for c in range(NC):
        m = dict(shared)
        m["x_prompt"] = f(inputs["x_prompt"][c * NSEQ:(c + 1) * NSEQ]).reshape(NSEQ * SEQ, D)
        m["x_sample"] = f(inputs["x_sample"][c * NS:(c + 1) * NS]).reshape(NS * DSEQ, D)
        m["state_conv"] = f(inputs["state_conv"][0, c * NS:(c + 1) * NS])
        m["state_lru"] = f(inputs["state_lru"][0, c * NS:(c + 1) * NS])
        m["cache_k"] = f(inputs["cache_k"][0, c * NS:(c + 1) * NS]).reshape(NS, 512, 512)
        m["cache_v"] = f(inputs["cache_v"][0, c * NS:(c + 1) * NS]).reshape(NS, 512, 512)
        in_maps.append(m)
    res = run_bass_kernel_spmd(nc, in_maps, core_ids=list(range(NC)))
    R = res.results
    cat = lambda n: np.concatenate([np.asarray(r[n], dtype=np.float32) for r in R], axis=0)
    y_prompt = cat("y_prompt").reshape(B, SEQ, D)
    y_sample = cat("y_sample").reshape(DB, DSEQ, D)
    keep = min(512, SEQ)
    outs = (
        y_prompt, y_sample,
        cat("prompt_conv").reshape(1, B, 3, LW), cat("prompt_lru").reshape(1, B, LW),
        cat("prompt_k").reshape(1, B, keep, 8, 64), cat("prompt_v").reshape(1, B, keep, 8, 64),
        cat("sample_conv").reshape(1, DB, 3, LW), cat("sample_lru").reshape(1, DB, LW),
        cat("sample_k").reshape(1, DB, DSEQ, 8, 64), cat("sample_v").reshape(1, DB, DSEQ, 8, 64),
    )
    return outs
```

```python
import numpy as np
from contextlib import ExitStack
import concourse.bass as bass
import concourse.mybir as mybir
from concourse.bass_utils import run_bass_kernel_spmd

F32 = mybir.dt.float32
BF16 = mybir.dt.bfloat16
AF = mybir.ActivationFunctionType
ALU = mybir.AluOpType

D = 1024
KC = 8
LW = 512
DFF = 2816
NF = 22
INC = 2560
EPS = 1e-6
NEG = -30000.0
NCD = dict(allow_slow_non_contiguous=True)
C0 = 0.7978845608028654
C1 = 0.7978845608028654 * 0.044715
B_IN = [0, 1, 2, 3]
B_INV = 4
B_INK = 5
B_OUT = [6, 7]
B_GU = list(range(8, 19))
B_DN = list(range(19, 25))
NBLK = 25
NSLOT = 6
OVERLAP_N1 = True


class Buf:
    def __init__(self, name=""):
        self.name = name
        self.w = {}
        self.r = {}


class Chan:
    def __init__(self, sem):
        self.sem = sem
        self.count = 0


class Prog:
    ENG = ("pe", "act", "dve", "pool", "sp")

    def __init__(self, nc, es):
        self.nc = nc
        self.streams = {e: [] for e in self.ENG}
        self.sem = {e: es.enter_context(nc.semaphore("sem_" + e)) for e in self.ENG if e != "sp"}
        self.cnt = {e: 0 for e in self.ENG}
        self.waited = {e: {} for e in self.ENG}
        self.semid = {}
        self.es = es
        self.nchan = 0

    def chan(self):
        self.nchan += 1
        return Chan(self.es.enter_context(self.nc.semaphore("ch%d" % self.nchan)))

    def _key(self, sem):
        k = id(sem)
        self.semid[k] = sem
        return k

    def _deps(self, eng, reads, writes, skip=None):
        deps = {}
        for b in reads:
            for k, v in b.w.items():
                deps[k] = max(deps.get(k, 0), v)
        for b in writes:
            for k, v in b.w.items():
                deps[k] = max(deps.get(k, 0), v)
            for k, v in b.r.items():
                deps[k] = max(deps.get(k, 0), v)
        pek = self._key(self.sem["pe"])
        for k, v in deps.items():
            if eng == "pe" and k == pek:
                continue
            if skip is not None and k == skip[0] and v < skip[1]:
                continue
            if self.waited[eng].get(k, 0) >= v:
                continue
            self.waited[eng][k] = v
            sem = self.semid[k]
            self.streams[eng].append(lambda e, s=sem, vv=v: e.wait_ge(s, vv))

    def _mark(self, k, v, reads, writes):
        for b in reads:
            b.r[k] = max(b.r.get(k, 0), v)
        for b in writes:
            b.w = {k: v}
            b.r = {}

    mute = False

    def op(self, eng, fn, reads=(), writes=(), signal=True):
        if self.mute:
            return
        reads = [b for b in reads if b is not None]
        writes = [b for b in writes if b is not None]
        self._deps(eng, reads, writes)
        sem = self.sem[eng]
        k = self._key(sem)
        if signal:
            self.cnt[eng] += 1
            v = self.cnt[eng]
            self.streams[eng].append(lambda e, f=fn, s=sem: f(e).then_inc(s, 1))
        else:
            v = self.cnt[eng] + 1
            self.streams[eng].append(lambda e, f=fn: f(e))
        self._mark(k, v, reads, writes)

    def dma(self, q, ch, out, in_, reads=(), writes=(), **kw):
        if self.mute:
            return
        reads = [b for b in reads if b is not None]
        writes = [b for b in writes if b is not None]
        self._deps(q, reads, writes, skip=(self._key(ch.sem), ch.count))
        ch.count += 16
        sem = ch.sem
        self.streams[q].append(
            lambda e, o=out, i=in_, s=sem, kw=kw: e.dma_start(out=o, in_=i, **kw).then_inc(s, 16))
        self._mark(self._key(sem), ch.count, reads, writes)

    def seal(self, ch, bufs):
        k = self._key(ch.sem)
        for b in bufs:
            if k in b.w:
                b.w[k] = ch.count

    def wait_all(self, eng, chans):
        for ch in chans:
            if ch.count:
                k = self._key(ch.sem)
                if self.waited[eng].get(k, 0) < ch.count:
                    self.waited[eng][k] = ch.count
                    self.streams[eng].append(lambda e, s=ch.sem, v=ch.count: e.wait_ge(s, v))

    def flush(self, block):
        m = {"sp": block.sync, "act": block.scalar, "dve": block.vector, "pool": block.gpsimd,
             "pe": block.tensor}
        for e in self.ENG:
            lst = self.streams[e]
            if not lst:
                continue
            self.streams[e] = []

            def body(eng, lst=lst):
                for f in lst:
                    f(eng)
            m[e](body)


def build(NSEQ, SEQ, NS, DSEQ=16, dbg=None):
    NT = SEQ // 512
    nc = bass.Bass("TRN2", target_bir_lowering=False)
    dt = lambda n, s, k="ExternalInput", d=F32: nc.dram_tensor(n, s, d, kind=k).ap()
    xp = dt("x_prompt", [NSEQ * SEQ, D])
    xs = dt("x_sample", [NS * DSEQ, D])
    state_conv = dt("state_conv", [NS, 3, LW])
    state_lru = dt("state_lru", [NS, LW])
    cache_k = dt("cache_k", [NS, 512, 512])
    cache_v = dt("cache_v", [NS, 512, 512])
    norm_mix = dt("norm_mix", [D])
    w_in = dt("w_in", [D, INC])
    conv_w = dt("conv_w", [4, LW])
    conv_b = dt("conv_b", [LW])
    lru_wa = dt("lru_wa", [8, 64, 64])
    lru_ba = dt("lru_ba", [LW])
    lru_wx = dt("lru_wx", [8, 64, 64])
    lru_bx = dt("lru_bx", [LW])
    lru_lambda = dt("lru_lambda", [LW])
    rel_bias = dt("rel_bias", [8, 257])
    norm_lru_out = dt("norm_lru_out", [LW])
    norm_attn_out = dt("norm_attn_out", [LW])
    w_out = dt("w_out", [D, D])
    norm_ffn = dt("norm_ffn", [D])
    w_gate = dt("w_gate", [D, DFF])
    w_up = dt("w_up", [D, DFF])
    w_down = dt("w_down", [DFF, D])
    norm_final = dt("norm_final", [D])
    O = "ExternalOutput"
    y_prompt = dt("y_prompt", [NSEQ * SEQ, D], O)
    y_sample = dt("y_sample", [NS * DSEQ, D], O)
    prompt_conv = dt("prompt_conv", [NSEQ, 3, LW], O)
    prompt_lru = dt("prompt_lru", [NSEQ, LW], O)
    prompt_k = dt("prompt_k", [NSEQ, 512, 512], O)
    prompt_v = dt("prompt_v", [NSEQ, 512, 512], O)
    sample_conv = dt("sample_conv", [NS, 3, LW], O)
    sample_lru = dt("sample_lru", [NS, LW], O)
    sample_k = dt("sample_k", [NS * DSEQ, 512], O)
    sample_v = dt("sample_v", [NS * DSEQ, 512], O)
    wsc = dt("wsc", [NBLK, 128, 4096], "Internal", BF16)
    ext = dt("ext", [8, 640], "Internal")
    ext2 = dt("ext2", [8, 128, 640], "Internal")

    with ExitStack() as es:
        P = Prog(nc, es)
        sb = lambda n, s, d=F32: es.enter_context(nc.sbuf_tensor(n, s, d))
        wring = sb("wring", [128, NSLOT, 4096], BF16)
        X = sb("X", [128, 4, D])
        xin = sb("xin", [128, 4, D])
        actT = sb("actT", [128, 8, 512], BF16)
        upre = sb("upre", [128, 4, 515])
        gbuf = sb("gbuf", [128, 4, 512])
        qT = sb("qT", [128, 8, 512], BF16)
        kT = sb("kT", [128, 4, 1024], BF16)
        vext = sb("vext", [128, 8, 8, 66], BF16)
        big = sb("big", [128, NF * 512], BF16)
        ucb = sb("ucb", [128, 2, 512], BF16)
        pT = sb("pT", [128, 4, 512], BF16)
        ya = sb("ya", [128, 4, 512])
        yab = sb("yab", [128, 2, 512], BF16)
        xsb = sb("xsb", [128, 2, D], BF16)
        biasT = sb("biasT", [128, 3, 8, 128], BF16)
        biasN = sb("biasN", [64, 8, 64], BF16)
        scrF = sb("scrF", [128, 4, 512])
        gfin = sb("gfin", [128, D])
        gw = sb("gw", [128, 2, 4, 128], BF16)
        ident = sb("ident", [128, 128], BF16)
        ones = sb("ones", [128, 2], BF16)
        cst = sb("cst", [128, 64])
        st = sb("st", [128, 96])
        hprev = sb("hprev", [128, 4, 4])
        gains = sb("gains", [128, 3, 8])
        pp = lambda n: es.enter_context(nc.psum_tensor(n, [128, 512], F32))
        psA = [pp("psA%d" % i) for i in range(4)]
        psS = [pp("psS%d" % i) for i in range(2)]
        psO = [pp("psO%d" % i) for i in range(2)]

        CW = lambda j, c: cst[:, j * 4 + c: j * 4 + c + 1]
        CB = lambda c: cst[:, 16 + c: 17 + c]
        HBA = lambda c: cst[:, 20 + c: 21 + c]
        HBX = lambda c: cst[:, 24 + c: 25 + c]
        C8 = lambda c: cst[:, 28 + c: 29 + c]
        HC8 = lambda c: cst[:, 32 + c: 33 + c]
        LAM = cst[:, 36:40]
        EM05 = cst[:, 40:41]
        EP05 = cst[:, 41:42]
        CBH = cst[:, 48:56]
        b_cst = Buf("cst")
        b_gains = Buf("gains")

        ring_b = [Buf("ring%d" % i) for i in range(NSLOT)]
        X_b = [Buf() for _ in range(4)]
        xin_b = [Buf() for _ in range(4)]
        actT_b = [Buf() for _ in range(4)]
        upre_b = [Buf() for _ in range(4)]
        gbuf_b = [Buf() for _ in range(4)]
        qT_b = [Buf() for _ in range(4)]
        kT_b = [[Buf(), Buf()] for _ in range(4)]
        vext_b = [Buf() for _ in range(8)]
        hT_b = [Buf() for _ in range(NF)]
        ucb_b = [Buf(), Buf()]
        pT_b = [Buf() for _ in range(4)]
        ya_b = [Buf() for _ in range(4)]
        yab_b = [Buf(), Buf()]
        xsb_b = [Buf(), Buf()]
        scrF_b = [Buf() for _ in range(4)]
        psA_b = [Buf() for _ in range(4)]
        psS_b = [Buf() for _ in range(2)]
        psO_b = [Buf() for _ in range(2)]
        wsc_b = [Buf() for _ in range(NBLK)]
        b_bias = Buf()
        b_const = Buf()
        hprev_b = [Buf() for _ in range(4)]
        st_b = {}

        def stb(name):
            if name not in st_b:
                st_b[name] = Buf(name)
            return st_b[name]
        SS1, R1, SS2, R2, SS3, R3, SSA, SSL, RL, SCA, TMP, RS = (0, 4, 8, 12, 16, 20, 24, 28, 32, 36, 40, 48)

        def tmpv(q, k):
            i = q * 5 + k
            return big[:, i * 1024:(i + 1) * 1024].bitcast(F32)

        def tmpb(q, k):
            i = q * 5 + k
            return [hT_b[2 * i], hT_b[2 * i + 1]]

        def y2v(i):
            return big[:, (20 + i) * 512:(21 + i) * 512]

        def y2b(i):
            return [hT_b[20 + i]]
        hTv = lambda f: big[:, f * 512:(f + 1) * 512]

        rrA = [0]

        def nextA():
            i = rrA[0] % 4
            rrA[0] += 1
            return i
        ch_misc = P.chan()
        ch_ring = [P.chan() for _ in range(NSLOT)]
        ch_xin = [P.chan() for _ in range(4)]
        ch_X = P.chan()
        ch_y = [P.chan() for _ in range(4)]
        ch_kv = [P.chan() for _ in range(4)]
        ch_conv = [P.chan() for _ in range(4)]
        ch_lru = [P.chan() for _ in range(4)]
        ch_state = P.chan()
        ch_ck = P.chan()
        ch_cv = P.chan()
        ch_e = [P.chan() for _ in range(3)]
        store_chans = ch_y + ch_kv + ch_conv + ch_lru

        with ExitStack() as es1:
            sb1 = lambda n, s, d=F32: es1.enter_context(nc.sbuf_tensor(n, s, d))
            stage = wring[:, 0:4, :].rearrange("p (q a) c -> p q (a c)", q=2).bitcast(F32)
            obf = wring[:, 4:6, :]
            gst = ya[:, 0:2, :].rearrange("p a (j c) -> p a j c", j=4)
            identf = sb1("identf", [128, 128])
            E8 = sb1("E8", [8, 640])
            B34 = scrF[:].rearrange("p a c -> p (a c)").rearrange("p (i h c) -> p i h c", i=2, h=8)
            BNf = sb1("BNf", [64, 8, 64])
            stage_b = [Buf(), Buf()]
            obf_b = [Buf(), Buf()]
            b_gst, b_identf, b_E8, b_B34, b_BNf, b_ext, b_ext2 = (Buf() for _ in range(7))
            ch_stg = [P.chan(), P.chan()]
            ch_obf = [P.chan(), P.chan()]
            block = es1.enter_context(nc.Block())

            cstage = sb1("cstage", [64, 128])
            gstage = sb1("gstage", [24, 128])
            b_cstage, b_gstage = Buf(), Buf()
            P.op("pool", lambda e: e.memset(cstage[:], 0.0), writes=[b_cstage])
            P.dma("sp", ch_misc, gstage[0:8, :], norm_mix.rearrange("(k p) -> k p", p=128), writes=[b_gstage])
            P.dma("sp", ch_misc, gstage[8:16, :], norm_ffn.rearrange("(k p) -> k p", p=128), writes=[b_gstage])
            P.dma("sp", ch_misc, gstage[16:20, :], norm_lru_out.rearrange("(k p) -> k p", p=128), writes=[b_gstage])
            P.dma("sp", ch_misc, gstage[20:24, :], norm_attn_out.rearrange("(k p) -> k p", p=128), writes=[b_gstage])
            P.dma("sp", ch_misc, cstage[0:16, :], conv_w.rearrange("j (c p) -> (j c) p", p=128), writes=[b_cstage])
            for row, src in [(16, conv_b), (20, lru_ba), (24, lru_bx), (36, lru_lambda)]:
                P.dma("sp", ch_misc, cstage[row:row + 4, :], src.rearrange("(c p) -> c p", p=128), writes=[b_cstage])
            P.dma("sp", ch_misc, CBH, bass.AP(tensor=rel_bias.tensor, offset=256, ap=[[0, 128], [257, 8]]),
                  writes=[b_cst], **NCD)
            P.dma("sp", ch_misc, gfin[:], bass.AP(tensor=norm_final.tensor, offset=0, ap=[[0, 128], [1, D]]),
                  writes=[b_const])
            P.op("pool", lambda e: e.memset(gst[:], 0.0), writes=[b_gst])
            for g, src in enumerate([lru_wa, lru_wx]):
                for n in range(8):
                    j, hf = n // 2, n % 2
                    P.dma("sp", ch_misc, gst[hf * 64:(hf + 1) * 64, g, j, hf * 64:(hf + 1) * 64], src[n],
                          writes=[b_gst])
            P.dma("sp", ch_misc, E8[:, 0:257], rel_bias, writes=[b_E8])
            P.seal(ch_misc, [b_gains, b_cst, b_const, b_gst, b_E8, b_cstage, b_gstage])
            P.op("pool", lambda e: e.memset(identf[:], 0.0), writes=[b_identf])
            P.op("pool", lambda e: e.affine_select(out=identf[:], in_=identf[:], pattern=[[-1, 128]],
                                                   compare_op=ALU.not_equal, fill=1.0, base=0,
                                                   channel_multiplier=1), writes=[b_identf])
            P.op("dve", lambda e: e.tensor_copy(out=ident[:], in_=identf[:]), reads=[b_identf], writes=[b_const])
            P.op("pe", lambda e: e.transpose(out=psA[0][:, 0:64], in_=cstage[:, :], identity=identf[0:64, 0:64]),
                 reads=[b_cstage, b_identf], writes=[psA_b[0]])
            P.op("pe", lambda e: e.transpose(out=psA[1][:, 0:24], in_=gstage[:, :], identity=identf[0:24, 0:24]),
                 reads=[b_gstage, b_identf], writes=[psA_b[1]])
            P.op("act", lambda e: e.activation(out=cst[:, 0:40], in_=psA[0][:, 0:40], func=AF.Copy),
                 reads=[psA_b[0]], writes=[b_cst])
            P.op("act", lambda e: e.activation(out=gains[:].rearrange("p a k -> p (a k)"), in_=psA[1][:, 0:24],
                                               func=AF.Copy), reads=[psA_b[1]], writes=[b_gains])
            P.op("pool", lambda e: e.memset(cst[:, 40:41], -0.5), writes=[b_cst])
            P.op("pool", lambda e: e.memset(cst[:, 41:42], 0.5), writes=[b_cst])
            P.op("dve", lambda e: e.tensor_scalar(out=cst[:, 20:28], in0=cst[:, 20:28], scalar1=0.5, scalar2=None,
                                                  op0=ALU.mult), reads=[], writes=[b_cst])
            P.op("act", lambda e: e.activation(out=cst[:, 44:48], in_=LAM, func=AF.Exp, scale=-1.0), writes=[b_cst])
            P.op("act", lambda e: e.activation(out=cst[:, 44:48], in_=cst[:, 44:48], func=AF.Ln, bias=1.0),
                 writes=[b_cst])
            P.op("dve", lambda e: e.tensor_scalar(out=cst[:, 28:32], in0=cst[:, 44:48], scalar1=-8.0, scalar2=None,
                                                  op0=ALU.mult), writes=[b_cst])
            P.op("dve", lambda e: e.tensor_scalar(out=cst[:, 32:36], in0=cst[:, 44:48], scalar1=-4.0, scalar2=None,
                                                  op0=ALU.mult), writes=[b_cst])
            P.op("dve", lambda e: e.tensor_copy(out=gw[:], in_=gst[:]), reads=[b_gst], writes=[b_const])
            P.op("pool", lambda e: e.memset(ones[:], 1.0), writes=[b_const])
            for blk in range(8):
                P.op("pool", lambda e, blk=blk: e.memset(vext[:, blk, :, 64:65], 1.0), writes=[vext_b[blk]])
            P.op("pool", lambda e: e.memset(qT[:], 0.0), writes=qT_b)
            P.op("pool", lambda e: e.memset(hprev[:], 0.0), writes=hprev_b)
            P.op("pool", lambda e: e.memset(upre[:], 0.0), writes=upre_b)
            SKIPB = dbg is not None and "nobias" in dbg
            SKIPW = dbg is not None and "noweights" in dbg
            P.mute = SKIPB
            P.op("dve", lambda e: e.tensor_copy(out=E8[:, 257:640], in_=E8[:, 256:257].to_broadcast([8, 383])),
                 writes=[b_E8])
            P.dma("sp", ch_e[0], ext, E8[:], reads=[b_E8], writes=[b_ext])
            P.dma("sp", ch_e[1], ext2, bass.AP(tensor=ext.tensor, offset=0, ap=[[640, 8], [0, 128], [1, 640]]),
                  reads=[b_ext], writes=[b_ext2])
            for i, base in enumerate([256, 128]):
                P.dma("sp", ch_e[2], B34[:, i, :, :],
                      bass.AP(tensor=ext2.tensor, offset=base, ap=[[639, 128], [81920, 8], [1, 128]]),
                      reads=[b_ext2], writes=[b_B34])
            P.op("pool", lambda e: e.memset(BNf[:], NEG), writes=[b_BNf])
            for s in range(4):
                P.dma("sp", ch_e[2], BNf[16 * s:16 * s + 16, :, 16 * s:16 * s + 16],
                      bass.AP(tensor=ext2.tensor, offset=128, ap=[[639, 16], [81920, 8], [1, 16]]),
                      reads=[b_ext2], writes=[b_BNf])
            P.seal(ch_e[2], [b_B34, b_BNf])
            for i in range(2):
                P.op("dve", lambda e, i=i: e.tensor_tensor(
                    out=B34[:, i, :, :], in0=B34[:, i, :, :],
                    in1=CBH.unsqueeze(2).to_broadcast([128, 8, 128]), op=ALU.subtract),
                    reads=[b_cst], writes=[b_B34])
                P.op("dve", lambda e, i=i: e.tensor_scalar(out=biasT[:, 1 + i, :, :], in0=B34[:, i, :, :], scalar1=8.0,
                                                          scalar2=None, op0=ALU.mult), reads=[b_B34], writes=[b_bias])
            P.op("pool", lambda e: e.memset(biasT[:, 0, :, :], 0.0), writes=[b_bias])
            P.op("pool", lambda e: e.memset(biasT[0:64, 0, :, 64:128], NEG), writes=[b_bias])
            P.op("pool", lambda e: e.memset(biasT[64:128, 2, :, 0:64], NEG), writes=[b_bias])
            P.op("dve", lambda e: e.tensor_tensor(out=BNf[:], in0=BNf[:],
                                                  in1=CBH[0:64, :].unsqueeze(2).to_broadcast([64, 8, 64]),
                                                  op=ALU.subtract), reads=[b_cst], writes=[b_BNf])
            P.op("dve", lambda e: e.tensor_scalar(out=biasN[:], in0=BNf[:], scalar1=8.0, scalar2=None, op0=ALU.mult),
                 reads=[b_BNf], writes=[b_bias])

            P.mute = SKIPW
            w_in_r = w_in.rearrange("(k p) c -> p k c", p=128)
            w_out_r = w_out.rearrange("(k p) c -> p k c", p=128)
            w_g_r = w_gate.rearrange("(k p) c -> p k c", p=128)
            w_u_r = w_up.rearrange("(k p) c -> p k c", p=128)
            w_d_r = w_down.rearrange("(k p) c -> p k c", p=128)
            eng_rr = ["act", "dve", "act", "dve", "act"]
            cnt = [0]

            def conv_block(blk, loads, gi, views, plain=False, used=4096):
                q = blk % 2
                for dst, src in loads:
                    P.dma("sp", ch_stg[q], dst, src, writes=[stage_b[q]])
                if plain:
                    eng = eng_rr[cnt[0] % 5]
                    cnt[0] += 1
                    if eng == "act":
                        P.op(eng, lambda e: e.activation(out=obf[:, q, 0:used], in_=stage[:, q, 0:used], func=AF.Copy),
                             reads=[stage_b[q]], writes=[obf_b[q]])
                    else:
                        P.op(eng, lambda e: e.tensor_copy(out=obf[:, q, 0:used], in_=stage[:, q, 0:used]),
                             reads=[stage_b[q]], writes=[obf_b[q]])
                else:
                    for k, (iv, ov, gk) in enumerate(views):
                        eng = eng_rr[cnt[0] % 5]
                        cnt[0] += 1
                        gsc = gains[:, gi, gk:gk + 1]
                        if eng == "act":
                            P.op(eng, lambda e, iv=iv, ov=ov, gsc=gsc: e.activation(out=ov, in_=iv, func=AF.Copy,
                                                                                 scale=gsc),
                                 reads=[stage_b[q], b_gains], writes=[obf_b[q]])
                        else:
                            P.op(eng, lambda e, iv=iv, ov=ov, gsc=gsc: e.tensor_scalar(out=ov, in0=iv, scalar1=gsc,
                                                                                    scalar2=None, op0=ALU.mult),
                                 reads=[stage_b[q], b_gains], writes=[obf_b[q]])
                P.dma("pool", ch_obf[q], wsc[blk][:, 0:used], obf[:, q, 0:used], reads=[obf_b[q]], writes=[wsc_b[blk]])

            def v5(t, q, a, b):
                return t[:, q, :].rearrange("p (a b c) -> p a b c", a=a, b=b)
            for i in range(4):
                blk = B_IN[i]
                q = blk % 2
                sv = stage[:, q, :].rearrange("p (k a c) -> p k a c", k=8, a=4)
                ov = v5(obf, q, 4, 8)
                loads = [(stage[:, q, :].rearrange("p (k c) -> p k c", k=8), w_in_r[:, :, i * 512:(i + 1) * 512])]
                conv_block(blk, loads, 0, [(sv[:, k, :, :], ov[:, :, k, :], k) for k in range(8)])
            for blk, c0 in [(B_INV, 2048), (B_INK, 1536)]:
                q = blk % 2
                sv = stage[:, q, :].rearrange("p (k c) -> p k c", k=8)
                ov = obf[:, q, :].rearrange("p (k c) -> p k c", k=8)
                conv_block(blk, [(sv, w_in_r[:, :, c0:c0 + 512])], 0,
                           [(sv[:, k, :], ov[:, k, :], k) for k in range(8)])
            for j in range(2):
                blk = B_OUT[j]
                q = blk % 2
                sv = stage[:, q, :].rearrange("p (k c) -> p k c", k=4)
                ov = obf[:, q, :].rearrange("p (k c) -> p k c", k=4)
                conv_block(blk, [(sv, w_out_r[:, 4 * j:4 * j + 4, :])], 2,
                           [(sv[:, k, :], ov[:, k, :], 4 * j + k) for k in range(4)])
            for i in range(11):
                blk = B_GU[i]
                q = blk % 2
                sv = stage[:, q, :].rearrange("p (g k f c) -> p g k f c", g=2, k=8, f=2)
                ov = obf[:, q, :].rearrange("p (f g k c) -> p f g k c", f=2, g=2, k=8)
                sl = stage[:, q, :].rearrange("p (g k c) -> p g k c", g=2, k=8)
                loads = [(sl[:, 0, :, :], w_g_r[:, :, 2 * i * 128:(2 * i + 2) * 128]),
                         (sl[:, 1, :, :], w_u_r[:, :, 2 * i * 128:(2 * i + 2) * 128])]
                conv_block(blk, loads, 1, [(sv[:, :, k, :, :].rearrange("p g f c -> p f g c"), ov[:, :, :, k, :], k)
                                           for k in range(8)])
            for i in range(6):
                blk = B_DN[i]
                q = blk % 2
                nk = 4 if i < 5 else 2
                sv = stage[:, q, 0:nk * 1024].rearrange("p (k c) -> p k c", k=nk)
                conv_block(blk, [(sv, w_d_r[:, 4 * i:4 * i + nk, :])], None, None, plain=True, used=nk * 1024)
            P.mute = False
            for e in ("sp", "pool", "act", "dve", "pe"):
                P.wait_all(e, [ch_misc] + ch_e + ch_stg + ch_obf)
            P.flush(block)

        block = es.enter_context(nc.Block())
        load_seq = []
        next_load = [0]

        def emit_loads_upto(n):
            while next_load[0] < min(n, len(load_seq)):
                idx = next_load[0]
                blk = load_seq[idx]
                i = idx % NSLOT
                used = 2048 if blk == B_DN[5] else 4096
                P.dma("sp", ch_ring[i], wring[:, i, 0:used], wsc[blk][:, 0:used], reads=[wsc_b[blk]],
                      writes=[ring_b[i]])
                next_load[0] += 1

        def consumed(T, blk):
            emit_loads_upto(T["lpos"][blk] + NSLOT + 1)

        def pool_rsqrt(dst_col, src_col, n, scale, nrow=128, srcs=(), dsts=()):
            P.op("pool", lambda e: e.tensor_scalar(out=st[0:nrow, TMP:TMP + n], in0=st[0:nrow, src_col:src_col + n],
                                                   scalar1=scale, scalar2=EPS, op0=ALU.mult, op1=ALU.add),
                 reads=list(srcs), writes=[stb("tmp")])
            P.op("pool", lambda e: e.tensor_tensor(out=st[0:nrow, dst_col:dst_col + n], in0=st[0:nrow, TMP:TMP + n],
                                                   in1=EM05[0:nrow, :].to_broadcast([nrow, n]), op=ALU.pow),
                 reads=[stb("tmp"), b_cst], writes=list(dsts))

        def norm_transpose(T, b, src_ap, src_b, ss_col, r_col, tag):
            nr = T["nr"]
            q = T["xq"][0] % 2
            T["xq"][0] += 1
            P.op("act", lambda e: e.activation(out=xsb[0:nr, q, :], in_=src_ap, func=AF.Square,
                                               accum_out=st[0:nr, ss_col + b:ss_col + b + 1]),
                 reads=[src_b], writes=[xsb_b[q], stb(tag + "ss%d" % b)])
            pool_rsqrt(r_col + b, ss_col + b, 1, 1.0 / D, nr, [stb(tag + "ss%d" % b)], [stb(tag + "r%d" % b)])
            P.op("dve", lambda e: e.tensor_scalar(out=xsb[0:nr, q, :], in0=src_ap, scalar1=st[0:nr, r_col + b:r_col + b + 1],
                                                  scalar2=None, op0=ALU.mult),
                 reads=[src_b, stb(tag + "r%d" % b)], writes=[xsb_b[q]])
            a = nextA()
            pv = psA[a][:].bitcast(BF16).rearrange("p (k c) -> p k c", k=8)
            for k in range(8):
                P.op("pe", lambda e, k=k: e.transpose(out=pv[:, k, 0:nr], in_=xsb[0:nr, q, k * 128:(k + 1) * 128],
                                                      identity=ident[0:nr, 0:nr]),
                     reads=[xsb_b[q], b_const], writes=[psA_b[a]], signal=(k == 7))
            eng = "act" if b % 2 == 0 else "dve"
            dst = actT[:, :, b * 128:b * 128 + nr]
            if eng == "act":
                P.op("act", lambda e: e.activation(out=dst, in_=pv[:, :, 0:nr], func=AF.Copy),
                     reads=[psA_b[a]], writes=[actT_b[b]])
            else:
                P.op("dve", lambda e: e.tensor_copy(out=dst, in_=pv[:, :, 0:nr]), reads=[psA_b[a]], writes=[actT_b[b]])

        def run_tile(T):
            kind = T["kind"]
            NTOK, nb, nr = T["ntok"], T["nb"], T["nr"]
            nseg, L = T["nseg"], T["L"]
            s_idx, t_idx = T["s"], T["t"]
            last = T["last"]
            first = T["first"]
            want_kv = last
            row0 = T["row0"]
            xsrc = T["xsrc"]
            ydst = T["ydst"]
            TS = slice(0, NTOK)
            cur_slot = {blk: idx % NSLOT for blk, idx in T["lpos"].items()}
            if kind == "p":
                P.dma("sp", ch_X, X[:, :, :], xsrc[row0:row0 + 512, :].rearrange("(b p) d -> p b d", p=128),
                      writes=X_b)
            else:
                P.dma("sp", ch_X, X[0:nr, 0, :], xsrc[row0:row0 + nr, :], writes=[X_b[0]])
            if not T.get("n1_done"):
                for b in range(nb):
                    q = T["xinq"][b]
                    norm_transpose(T, b, xin[0:nr, q, :], xin_b[q], SS1, R1, "n1")
            nxt = T.get("next")
            if dbg is not None and dbg.endswith("_n1"):
                P.mute = True
            u_view = lambda c: upre[:, c, 0:nseg * (3 + L)].rearrange("p (s l) -> p s l", s=nseg)
            woff = T["woff"]
            def lv(ap):
                return ap.rearrange("p (s l) -> p s l", s=nseg)

            def lru_front(c):
                q = c % 2
                T4 = tmpv(q, 3)[:, TS]
                B4 = tmpb(q, 3)
                uv = u_view(c)
                P.op("dve", lambda e: e.tensor_scalar(
                    out=lv(T4), in0=uv[:, :, 3:3 + L], scalar1=CW(3, c), scalar2=CB(c), op0=ALU.mult, op1=ALU.add),
                    reads=[upre_b[c], b_cst], writes=B4)
                for j in (2, 1, 0):
                    P.op("dve", lambda e, j=j: e.scalar_tensor_tensor(
                        out=lv(T4), in0=uv[:, :, j:j + L], scalar=CW(j, c), in1=lv(T4), op0=ALU.mult, op1=ALU.add),
                        reads=[upre_b[c], b_cst], writes=B4)
                if last:
                    for sg in range(nseg):
                        dstc = T["convdst"][sg][:, c * 128:(c + 1) * 128].rearrange("j p -> p j")
                        P.dma("pool", ch_conv[c], dstc, uv[:, sg, L:L + 3], reads=[upre_b[c]], **NCD)
                if kind == "p" and not last:
                    P.op("pool", lambda e: e.tensor_copy(out=upre[:, c, 0:3], in_=upre[:, c, 512:515]),
                         writes=[upre_b[c]])
                elif kind == "p" and last:
                    P.op("pool", lambda e: e.memset(upre[:, c, 0:3], 0.0), writes=[upre_b[c]])
                P.op("act", lambda e: e.activation(out=ucb[:, q, TS], in_=T4, func=AF.Copy), reads=B4,
                     writes=[ucb_b[q]])
                ar, ai = nextA(), nextA()
                T["gates"][c] = (ar, ai)
                P.op("pe", lambda e: e.matmul(psA[ar][:, TS], lhsT=gw[:, 0, c, :], rhs=ucb[:, q, TS],
                                              start=True, stop=True),
                     reads=[ucb_b[q], b_const], writes=[psA_b[ar]])
                P.op("pe", lambda e: e.matmul(psA[ai][:, TS], lhsT=gw[:, 1, c, :], rhs=ucb[:, q, TS],
                                              start=True, stop=True),
                     reads=[ucb_b[q], b_const], writes=[psA_b[ai]])

            def lru_chainA(c):
                q = c % 2
                T1, T2, T3, T4, T5 = (tmpv(q, k)[:, TS] for k in range(5))
                B1, B2, B3, B4, B5 = (tmpb(q, k) for k in range(5))
                ar, ai = T["gates"][c]
                P.op("act", lambda e: e.activation(out=T5, in_=gbuf[:, c, TS], func=AF.Square),
                     reads=[gbuf_b[c]], writes=B5)
                P.op("pool", lambda e: e.tensor_scalar(out=T5, in0=T5, scalar1=C1, scalar2=C0, op0=ALU.mult,
                                                       op1=ALU.add), writes=B5)
                P.op("pool", lambda e: e.tensor_tensor(out=T5, in0=T5, in1=gbuf[:, c, TS], op=ALU.mult),
                     reads=[gbuf_b[c]], writes=B5)
                P.op("act", lambda e: e.activation(out=T1, in_=psA[ar][:, TS], func=AF.Tanh, bias=HBA(c), scale=0.5),
                     reads=[psA_b[ar], b_cst], writes=B1)
                P.op("act", lambda e: e.activation(out=T2, in_=psA[ai][:, TS], func=AF.Tanh, bias=HBX(c), scale=0.5),
                     reads=[psA_b[ai], b_cst], writes=B2)
                P.op("act", lambda e: e.activation(out=T3, in_=T1, func=AF.Exp, bias=HC8(c), scale=HC8(c)),
                     reads=B1 + [b_cst], writes=B3)
                P.op("act", lambda e: e.activation(out=T1, in_=T1, func=AF.Exp, bias=C8(c), scale=C8(c)),
                     reads=[b_cst], writes=B1)
                P.op("act", lambda e: e.activation(out=T5, in_=T5, func=AF.Tanh), writes=B5)
                P.op("dve", lambda e: e.scalar_tensor_tensor(out=T2, in0=T2, scalar=1.0, in1=T4, op0=ALU.add,
                                                             op1=ALU.mult), reads=B4, writes=B2)
                P.op("dve", lambda e: e.scalar_tensor_tensor(out=T5, in0=T5, scalar=1.0, in1=gbuf[:, c, TS],
                                                             op0=ALU.add, op1=ALU.mult), reads=[gbuf_b[c]], writes=B5)

            def lru_sqrt(c):
                q = c % 2
                T1 = tmpv(q, 0)[:, TS]
                P.op("act", lambda e: e.activation(out=T1, in_=T1, func=AF.Sqrt, bias=0.25, scale=-0.25),
                     writes=tmpb(q, 0))

            def lru_chainB(c):
                q = c % 2
                T1, T2, T3, T4, T5 = (tmpv(q, k)[:, TS] for k in range(5))
                B1, B2, B3, B4, B5 = (tmpb(q, k) for k in range(5))
                P.op("dve", lambda e: e.tensor_tensor(out=T2, in0=T2, in1=T1, op=ALU.mult), reads=B1, writes=B2)
                for sg in range(nseg):
                    init = hprev[:, c, sg:sg + 1]
                    P.op("dve", lambda e, sg=sg, init=init: e.tensor_tensor_scan(
                        out=lv(T4)[:, sg, :], data0=lv(T3)[:, sg, :], data1=lv(T2)[:, sg, :], initial=init,
                        op0=ALU.mult, op1=ALU.add), reads=B3 + B2 + [hprev_b[c]], writes=B4)
                if last:
                    for sg in range(nseg):
                        dsth = T["lrudst"][sg][c * 128:(c + 1) * 128].rearrange("(p o) -> p o", o=1)
                        P.dma("pool", ch_lru[c], dsth, lv(T4)[:, sg, L - 1:L], reads=B4, **NCD)
                if kind == "p":
                    if not last:
                        P.op("pool", lambda e: e.tensor_copy(out=hprev[:, c, 0:1], in_=T4[:, NTOK - 1:NTOK]),
                             reads=B4, writes=[hprev_b[c]])
                    else:
                        P.op("pool", lambda e: e.memset(hprev[:, c, 0:1], 0.0), writes=[hprev_b[c]])
                P.op("dve", lambda e: e.scalar_tensor_tensor(out=actT[:, c, TS], in0=T5, scalar=0.5, in1=T4,
                                                             op0=ALU.mult, op1=ALU.mult),
                     reads=B5 + B4, writes=actT_b[0:nb])
                P.op("pool", lambda e: e.tensor_tensor(out=y2v(q)[:, TS], in0=actT[:, c, TS], in1=actT[:, c, TS],
                                                       op=ALU.mult), reads=actT_b[0:nb], writes=y2b(q))

            def lru_ssl(c):
                q = c % 2
                for b in range(nb):
                    P.op("pe", lambda e, b=b: e.matmul(
                        psO[1][0:nr, 300 + 4 * b + c:301 + 4 * b + c], lhsT=y2v(q)[:, b * 128:b * 128 + nr],
                        rhs=ones[:, 0:1], start=True, stop=True, skip_group_check=True),
                        reads=y2b(q) + [b_const], writes=[psO_b[1]], signal=(b == nb - 1))

            def lru_finish():
                P.op("dve", lambda e: e.tensor_reduce(
                    out=st[0:nr, SSL:SSL + nb], in_=psO[1][0:nr, 300:300 + 4 * nb].rearrange("p (b c) -> p b c", c=4),
                    axis=mybir.AxisListType.X, op=ALU.add), reads=[psO_b[1]], writes=[stb("ssl")])
                P.op("pool", lambda e: e.tensor_scalar(out=st[0:nr, TMP + 4:TMP + 4 + nb], in0=st[0:nr, SSL:SSL + nb],
                                                       scalar1=1.0 / LW, scalar2=EPS, op0=ALU.mult, op1=ALU.add),
                     reads=[stb("ssl")], writes=[stb("tl")])
                P.op("pool", lambda e: e.tensor_tensor(out=st[0:nr, RL:RL + nb], in0=st[0:nr, TMP + 4:TMP + 4 + nb],
                                                       in1=EM05[0:nr, :].to_broadcast([nr, nb]), op=ALU.pow),
                     reads=[stb("tl"), b_cst], writes=[stb("rl")])
                P.op("pool", lambda e: e.tensor_tensor(out=st[0:nr, 56:56 + nb], in0=st[0:nr, TMP + 4:TMP + 4 + nb],
                                                       in1=EP05[0:nr, :].to_broadcast([nr, nb]), op=ALU.pow),
                     reads=[stb("tl"), b_cst], writes=[stb("sql")])

            for m in range(16):
                if m == 8:
                    for c_ in (0, 1):
                        lru_front(c_)
                    for c_ in (0, 1):
                        lru_chainA(c_)
                    for c_ in (0, 1):
                        lru_sqrt(c_)
                blk = B_IN[m // 4]
                sl = cur_slot[blk]
                wv = wring[:, sl, :].rearrange("p (a k c) -> p a k c", a=4, k=8)
                a = nextA()
                for k in range(8):
                    P.op("pe", lambda e, a=a, wv=wv, m=m, k=k: e.matmul(psA[a][:, TS], lhsT=wv[:, m % 4, k, :],
                                                                        rhs=actT[:, k, TS], start=(k == 0),
                                                                        stop=(k == 7)),
                         reads=[ring_b[sl]] + actT_b[0:nb], writes=[psA_b[a]], signal=(k == 7))
                if m % 4 == 3:
                    consumed(T, blk)
                c = m % 4
                if m < 4:
                    dst = u_view(c)[:, :, 3:3 + L]
                    src = psA[a][:, TS].rearrange("p (s l) -> p s l", s=nseg)
                    P.op("act", lambda e, dst=dst, src=src: e.activation(out=dst, in_=src, func=AF.Copy),
                         reads=[psA_b[a]], writes=[upre_b[c]])
                elif m < 8:
                    P.op("act", lambda e, a=a, c=c: e.activation(out=gbuf[:, c, TS], in_=psA[a][:, TS], func=AF.Copy),
                         reads=[psA_b[a]], writes=[gbuf_b[c]])
                elif m < 12:
                    for e2 in range(2):
                        P.op("act", lambda e, a=a, c=c, e2=e2: e.activation(
                            out=qT[e2 * 64:(e2 + 1) * 64, 2 * c + e2, TS], in_=psA[a][e2 * 64:(e2 + 1) * 64, TS],
                            func=AF.Copy), reads=[psA_b[a]], writes=[qT_b[c]])
                else:
                    P.op("act", lambda e, a=a, c=c: e.activation(out=kT[:, c, woff:woff + NTOK], in_=psA[a][:, TS],
                                                                 func=AF.Copy),
                         reads=[psA_b[a]], writes=[kT_b[c][T["whalf"]]])
            if dbg is not None and dbg.endswith("_in1"):
                P.mute = True
            for which in (["v", "k"] if want_kv else ["v"]):
                blk = B_INV if which == "v" else B_INK
                sl = cur_slot[blk]
                wv = wring[:, sl, :].rearrange("p (k c) -> p k c", k=8)
                for b in range(nb):
                    a = nextA()
                    for k in range(8):
                        P.op("pe", lambda e, a=a, wv=wv, b=b, k=k: e.matmul(
                            psA[a][0:nr, :], lhsT=actT[:, k, b * 128:b * 128 + nr], rhs=wv[:, k, :],
                            start=(k == 0), stop=(k == 7)),
                            reads=[ring_b[sl], actT_b[b]], writes=[psA_b[a]], signal=(k == 7))
                    if which == "v" and not (dbg is not None and "novext" in dbg):
                        vb = T["vblk"][b]
                        if True:
                            P.op("act", lambda e, a=a, vb=vb: e.activation(
                                out=vext[0:nr, vb, :, 0:64], in_=psA[a][0:nr, :].rearrange("p (h d) -> p h d", h=8),
                                func=AF.Copy), reads=[psA_b[a]], writes=[vext_b[vb]])
                        else:
                            P.op("dve", lambda e, a=a, vb=vb: e.tensor_copy(
                                out=vext[0:nr, vb, :, 0:64], in_=psA[a][0:nr, :].rearrange("p (h d) -> p h d", h=8)),
                                reads=[psA_b[a]], writes=[vext_b[vb]])
                    if want_kv:
                        sq = (b * 2 + (0 if which == "v" else 1)) % 4
                        P.op("act", lambda e, a=a, sq=sq: e.activation(out=scrF[0:nr, sq, :], in_=psA[a][0:nr, :],
                                                                      func=AF.Copy),
                             reads=[psA_b[a]], writes=[scrF_b[sq]])
                        dst = T["kvdst"][which][b]
                        if not (dbg is not None and "nokvst" in dbg):
                            P.dma("pool", ch_kv[sq], dst, scrF[0:nr, sq, :], reads=[scrF_b[sq]])
                consumed(T, blk)
            if dbg is not None and dbg.endswith("_in"):
                P.mute = True
            if nxt is not None:
                for b in range(nxt["nb"]):
                    q = nxt["xinq"][b]
                    r0 = nxt["row0"] + b * 128
                    P.dma("sp", ch_xin[q], xin[0:nxt["nr"], q, :], nxt["xsrc"][r0:r0 + nxt["nr"], :],
                          writes=[xin_b[q]])
            def finish_group(yq, hg, po_, nrq):
                P.op("dve", lambda e: e.reciprocal(out=st[0:nrq, RS + hg * 4:RS + hg * 4 + 4],
                                                   in_=psO[po_][0:nrq, 0:260].rearrange("p (h d) -> p h d", h=4)[:, :, 64]),
                     reads=[psO_b[po_]], writes=[stb("rs%d" % hg)])
                P.op("dve", lambda e: e.tensor_tensor(
                    out=ya[0:nrq, yq, hg * 256:(hg + 1) * 256].rearrange("p (h d) -> p h d", h=4),
                    in0=psO[po_][0:nrq, 0:260].rearrange("p (h d) -> p h d", h=4)[:, :, 0:64],
                    in1=st[0:nrq, RS + hg * 4:RS + hg * 4 + 4].unsqueeze(2).to_broadcast([nrq, 4, 64]), op=ALU.mult),
                    reads=[psO_b[po_], stb("rs%d" % hg)], writes=[ya_b[yq]])

            def finish_stats(blocks, nrq):
                for b in blocks:
                    bq = b % 2
                    P.op("act", lambda e, b=b, bq=bq: e.activation(out=yab[0:nrq, bq, :], in_=ya[0:nrq, b, :],
                                                                   func=AF.Square,
                                                                   accum_out=st[0:nrq, SSA + b:SSA + b + 1]),
                         reads=[ya_b[b]], writes=[yab_b[bq], stb("ssa%d" % b)])
                nbk = len(blocks)
                b0 = blocks[0]
                P.op("pool", lambda e: e.tensor_scalar(out=st[0:nrq, TMP:TMP + nbk], in0=st[0:nrq, SSA + b0:SSA + b0 + nbk],
                                                       scalar1=1.0 / LW, scalar2=EPS, op0=ALU.mult, op1=ALU.add),
                     reads=[stb("ssa%d" % b) for b in blocks], writes=[stb("tmp")])
                P.op("pool", lambda e: e.tensor_tensor(out=st[0:nrq, SCA + b0:SCA + b0 + nbk], in0=st[0:nrq, TMP:TMP + nbk],
                                                       in1=EM05[0:nrq, :].to_broadcast([nrq, nbk]), op=ALU.pow),
                     reads=[stb("tmp"), b_cst], writes=[stb("sca%d" % b) for b in blocks])
                P.op("pool", lambda e: e.tensor_tensor(out=st[0:nrq, SCA + b0:SCA + b0 + nbk],
                                                       in0=st[0:nrq, SCA + b0:SCA + b0 + nbk],
                                                       in1=st[0:nrq, 56 + b0:56 + b0 + nbk], op=ALU.mult),
                     reads=[stb("sql")], writes=[stb("sca%d" % b) for b in blocks])

            def finish_apply(blocks, nrq):
                for b in blocks:
                    bq = b % 2
                    P.op("dve", lambda e, b=b, bq=bq: e.tensor_scalar(
                        out=yab[0:nrq, bq, :], in0=ya[0:nrq, b, :], scalar1=st[0:nrq, SCA + b:SCA + b + 1], scalar2=None,
                        op0=ALU.mult), reads=[ya_b[b], stb("sca%d" % b)], writes=[yab_b[bq]])
                    a = nextA()
                    pv = psA[a][:].bitcast(BF16).rearrange("p (k c) -> p k c", k=8)
                    for hp in range(4):
                        P.op("pe", lambda e, hp=hp, pv=pv, bq=bq: e.transpose(
                            out=pv[:, hp, 0:nrq], in_=yab[0:nrq, bq, hp * 128:(hp + 1) * 128],
                            identity=ident[0:nrq, 0:nrq]),
                            reads=[yab_b[bq], b_const], writes=[psA_b[a]], signal=(hp == 3))
                    P.op("act" if b % 2 == 0 else "dve", (lambda e, b=b, pv=pv: e.activation(
                        out=actT[:, 4:8, b * 128:b * 128 + nrq], in_=pv[:, 0:4, 0:nrq], func=AF.Copy)) if b % 2 == 0 else
                        (lambda e, b=b, pv=pv: e.tensor_copy(out=actT[:, 4:8, b * 128:b * 128 + nrq], in_=pv[:, 0:4, 0:nrq])),
                        reads=[psA_b[a]], writes=[actT_b[b]])

            if kind == "p":
                units = []
                for m in range(4):
                    for hg in range(2):
                        kbs = [kb for kb in range(5) if 4 * t_idx + m - 4 + kb >= 0]
                        for kb in kbs:
                            units.append((m, hg, kb, kb == kbs[0], kb == kbs[-1]))

                def qk(u, i):
                    m, hg, kb, fst, lst = u
                    sS = i % 2
                    ablk = (4 * t_idx + m - 4 + kb) % 8
                    kcol = ablk * 128
                    khalf = ablk // 4
                    biased = kb in (0, 3, 4)
                    kbi = {0: 0, 3: 1, 4: 2}.get(kb, 0)
                    sv = psS[sS][:].rearrange("p (j q) -> p j q", j=4)
                    for j in range(4):
                        h = hg * 4 + j
                        hp, po = h // 2, (h % 2) * 64
                        P.op("pe", lambda e, j=j, hp=hp, h=h: e.matmul(
                            sv[:, j, :], lhsT=kT[:, hp, kcol:kcol + 128],
                            rhs=qT[:, h, m * 128:(m + 1) * 128], start=(j == 0),
                            stop=(not biased and j == 3), skip_group_check=True),
                            reads=[kT_b[hp][khalf], qT_b[hp]], writes=[psS_b[sS]],
                            signal=(not biased and j == 3))
                    if biased:
                        for j in range(4):
                            h = hg * 4 + j
                            P.op("pe", lambda e, j=j, h=h: e.matmul(sv[:, j, :], lhsT=ident[:], rhs=biasT[:, kbi, h, :],
                                                                    start=False, stop=(j == 3), skip_group_check=True),
                                 reads=[b_bias, b_const], writes=[psS_b[sS]], signal=(j == 3))
                    sP = i % 4
                    P.op("act", lambda e: e.activation(out=pT[:, sP, :], in_=psS[sS][:], func=AF.Exp, scale=0.125),
                         reads=[psS_b[sS]], writes=[pT_b[sP]])

                def pv_(u, i):
                    m, hg, kb, fst, lst = u
                    sP = i % 4
                    ablk = (4 * t_idx + m - 4 + kb) % 8
                    ov = psO[hg][:, 0:260].rearrange("p (h d) -> p h d", h=4)
                    for j in range(4):
                        h = hg * 4 + j
                        P.op("pe", lambda e, j=j, h=h: e.matmul(
                            ov[:, j, :], lhsT=pT[:, sP, j * 128:(j + 1) * 128], rhs=vext[:, ablk, h, 0:65],
                            start=(fst and j == 0), stop=(lst and j == 3), skip_group_check=True),
                            reads=[pT_b[sP], vext_b[ablk]], writes=[psO_b[hg]], signal=(j == 3))
                    if lst:
                        finish_group(m, hg, hg, 128)
                for pr in range(2):
                    ms = (2 * pr, 2 * pr + 1)
                    if pr == 1:
                        for m in ms:
                            lru_front(m)
                        for m in ms:
                            lru_chainA(m)
                        for m in ms:
                            lru_sqrt(m)
                    for m in ms:
                        lru_chainB(m)
                        um = [(i, u) for i, u in enumerate(units) if u[0] == m]
                        n = len(um)
                        hooks = {}
                        if m == 3:
                            def h_a():
                                lru_ssl(3)
                                lru_finish()

                            def h_b():
                                finish_stats([0, 1, 2], 128)
                            hooks = {min(3, n - 1): h_a, min(5, n - 1) + (1 if n - 1 <= 3 else 0): h_b}
                        for x in range(min(2, n)):
                            qk(um[x][1], um[x][0])
                        for x in range(n):
                            pv_(um[x][1], um[x][0])
                            if x + 2 < n:
                                qk(um[x + 2][1], um[x + 2][0])
                            if x in hooks:
                                hooks.pop(x)()
                        for x in sorted(hooks):
                            hooks[x]()
                        if m < 3:
                            lru_ssl(m)
                finish_stats([3], 128)
                finish_apply([0, 1, 2, 3], 128)
            else:
                cstK = xin[:, 0:2, :].rearrange("p a (b c) -> p (a b) c", b=2)
                cstV = xin[:, 2:4, :].rearrange("p a (b c) -> p (a b) c", b=2)
                ui = [0]
                fstO = [True, True]

                def load_cache(sq_):
                    P.dma("sp", ch_ck, cstK, cache_k[sq_].rearrange("(cb p) f -> p cb f", p=128), writes=xin_b[0:2])
                    P.dma("sp", ch_cv, cstV, cache_v[sq_].rearrange("(cb p) f -> p cb f", p=128), writes=xin_b[2:4])
                    for half in range(2):
                        qx = T["xq"][0] % 2
                        T["xq"][0] += 1
                        xv = xsb[:, qx, :].rearrange("p (cb f) -> p cb f", cb=2)
                        P.op("dve", lambda e, xv=xv, half=half: e.tensor_copy(out=xv, in_=cstK[:, 2 * half:2 * half + 2, :]),
                             reads=xin_b[0:2], writes=[xsb_b[qx]])
                        a = nextA()
                        pv = psA[a][:].bitcast(BF16).rearrange("p (k c) -> p k c", k=8)
                        for cbl in range(2):
                            for hp in range(4):
                                P.op("pe", lambda e, cbl=cbl, hp=hp, xv=xv, pv=pv: e.transpose(
                                    out=pv[:, cbl * 4 + hp, :], in_=xv[:, cbl, hp * 128:(hp + 1) * 128], identity=ident[:]),
                                    reads=[xsb_b[qx], b_const], writes=[psA_b[a]], signal=(cbl == 1 and hp == 3))
                        for cbl in range(2):
                            cb = 2 * half + cbl
                            P.op("act", lambda e, cb=cb, cbl=cbl, pv=pv: e.activation(
                                out=kT[:, :, cb * 128:(cb + 1) * 128], in_=pv[:, cbl * 4:cbl * 4 + 4, :], func=AF.Copy),
                                reads=[psA_b[a]], writes=[kT_b[hp][0] for hp in range(4)])
                    for cb in range(4):
                        P.op("act", lambda e, cb=cb: e.activation(
                            out=vext[:, cb, :, 0:64], in_=cstV[:, cb, :].rearrange("p (h d) -> p h d", h=8),
                            func=AF.Copy), reads=xin_b[2:4], writes=[vext_b[cb]])

                def cache_unit(sq_, hg, cb):
                    i = ui[0]
                    ui[0] += 1
                    sS, sP = i % 2, i % 4
                    sv = psS[sS][:].rearrange("p (j q) -> p j q", j=4)
                    biased = cb == 3
                    for j in range(4):
                        h = hg * 4 + j
                        hp, po = h // 2, (h % 2) * 64
                        P.op("pe", lambda e, j=j, hp=hp, h=h: e.matmul(
                            sv[:, j, 0:16], lhsT=kT[:, hp, cb * 128:(cb + 1) * 128],
                            rhs=qT[:, h, sq_ * 16:sq_ * 16 + 16], start=(j == 0),
                            stop=(not biased and j == 3), skip_group_check=True),
                            reads=[kT_b[hp][0], qT_b[hp]], writes=[psS_b[sS]], signal=(not biased and j == 3))
                    if biased:
                        for j in range(4):
                            h = hg * 4 + j
                            P.op("pe", lambda e, j=j, h=h: e.matmul(
                                sv[:, j, 0:16], lhsT=ident[:], rhs=biasT[:, 1, h, 0:16], start=False,
                                stop=(j == 3), skip_group_check=True),
                                reads=[b_bias, b_const], writes=[psS_b[sS]], signal=(j == 3))
                    pview = pT[:, sP, 0:256].rearrange("p (j q) -> p j q", j=4)
                    P.op("pool", lambda e: e.memset(pT[:, sP, 0:256], 0.0), writes=[pT_b[sP]])
                    P.op("act", lambda e: e.activation(out=pview[:, :, sq_ * 16:sq_ * 16 + 16], in_=sv[:, :, 0:16],
                                                       func=AF.Exp, scale=0.125),
                         reads=[psS_b[sS]], writes=[pT_b[sP]])
                    ov = psO[hg][:, 0:260].rearrange("p (h d) -> p h d", h=4)
                    f0 = fstO[hg]
                    fstO[hg] = False
                    for j in range(4):
                        h = hg * 4 + j
                        P.op("pe", lambda e, j=j, h=h: e.matmul(
                            ov[0:64, j, :], lhsT=pview[:, j, :], rhs=vext[:, cb, h, 0:65],
                            start=(f0 and j == 0), stop=False, skip_group_check=True),
                            reads=[pT_b[sP], vext_b[cb]], writes=[psO_b[hg]], signal=(j == 3))

                def new_unit(hg):
                    i = ui[0]
                    ui[0] += 1
                    sS, sP = i % 2, i % 4
                    sv = psS[sS][:].rearrange("p (j q) -> p j q", j=4)
                    for j in range(4):
                        h = hg * 4 + j
                        hp, po = h // 2, (h % 2) * 64
                        P.op("pe", lambda e, j=j, hp=hp, h=h: e.matmul(
                            sv[0:64, j, 0:64], lhsT=kT[:, hp, 512:576], rhs=qT[:, h, 0:64],
                            start=(j == 0), stop=False, skip_group_check=True),
                            reads=[kT_b[hp][1], qT_b[hp]], writes=[psS_b[sS]], signal=False)
                    for j in range(4):
                        h = hg * 4 + j
                        P.op("pe", lambda e, j=j, h=h: e.matmul(
                            sv[0:64, j, 0:64], lhsT=ident[0:64, 0:64], rhs=biasN[:, h, :], start=False, stop=(j == 3),
                            skip_group_check=True), reads=[b_bias, b_const], writes=[psS_b[sS]], signal=(j == 3))
                    pview = pT[:, sP, 0:256].rearrange("p (j q) -> p j q", j=4)
                    P.op("act", lambda e: e.activation(out=pview[0:64, :, :], in_=sv[0:64, :, 0:64], func=AF.Exp,
                                                       scale=0.125), reads=[psS_b[sS]], writes=[pT_b[sP]])
                    ov = psO[hg][:, 0:260].rearrange("p (h d) -> p h d", h=4)
                    for j in range(4):
                        h = hg * 4 + j
                        P.op("pe", lambda e, j=j, h=h: e.matmul(
                            ov[0:64, j, :], lhsT=pview[0:64, j, :], rhs=vext[0:64, 4, h, 0:65], start=False,
                            stop=(j == 3), skip_group_check=True),
                            reads=[pT_b[sP], vext_b[4]], writes=[psO_b[hg]], signal=(j == 3))
                    finish_group(0, hg, hg, 64)

                for pr in range(2):
                    cs = (2 * pr, 2 * pr + 1)
                    if pr == 1:
                        for c in cs:
                            lru_front(c)
                        for c in cs:
                            lru_chainA(c)
                        for c in cs:
                            lru_sqrt(c)
                    for c in cs:
                        lru_chainB(c)
                        lru_ssl(c)
                lru_finish()
                for sq_ in range(NS):
                    load_cache(sq_)
                    for hg in range(2):
                        for cb in range(4):
                            cache_unit(sq_, hg, cb)
                for hg in range(2):
                    new_unit(hg)
                finish_stats([0], 64)
                finish_apply([0], 64)

            if dbg is not None and dbg.endswith("_att"):
                P.mute = True
            for b in range(nb):
                a0, a1 = nextA(), nextA()
                for k in range(8):
                    sl = cur_slot[B_OUT[k // 4]]
                    wv = wring[:, sl, :].rearrange("p (k c) -> p k c", k=4)
                    for half, a in enumerate((a0, a1)):
                        P.op("pe", lambda e, a=a, k=k, wv=wv, b=b, half=half: e.matmul(
                            psA[a][0:nr, :], lhsT=actT[:, k, b * 128:b * 128 + nr],
                            rhs=wv[:, k % 4, half * 512:(half + 1) * 512], start=(k == 0), stop=(k == 7)),
                            reads=[ring_b[sl], actT_b[b]], writes=[psA_b[a]], signal=(k == 7))
                for half, a in enumerate((a0, a1)):
                    P.op("dve", lambda e, a=a, b=b, half=half: e.scalar_tensor_tensor(
                        out=X[0:nr, b, half * 512:(half + 1) * 512], in0=psA[a][0:nr, :],
                        scalar=st[0:nr, RL + b:RL + b + 1], in1=X[0:nr, b, half * 512:(half + 1) * 512],
                        op0=ALU.mult, op1=ALU.add), reads=[psA_b[a], stb("rl")], writes=[X_b[b]])
            for blk in B_OUT:
                consumed(T, blk)
            if dbg is not None and dbg.endswith("_out"):
                P.mute = True
            for b in range(nb):
                norm_transpose(T, b, X[0:nr, b, :], X_b[b], SS2, R2, "n2")
            for f in range(NF):
                i = f // 2
                sl = cur_slot[B_GU[i]]
                wv = wring[:, sl, :].rearrange("p (a k c) -> p a k c", a=4, k=8)
                ag, au = nextA(), nextA()
                for gu, a in enumerate((ag, au)):
                    for k in range(8):
                        P.op("pe", lambda e, a=a, k=k, wv=wv, gu=gu, f=f: e.matmul(
                            psA[a][:, TS], lhsT=wv[:, (f % 2) * 2 + gu, k, :], rhs=actT[:, k, TS], start=(k == 0),
                            stop=(k == 7)), reads=[ring_b[sl]] + actT_b[0:nb], writes=[psA_b[a]], signal=(k == 7))
                tq = f % 2
                P.op("act", lambda e, ag=ag, tq=tq: e.activation(out=scrF[:, tq, TS], in_=psA[ag][:, TS], func=AF.Tanh,
                                                                 scale=0.5), reads=[psA_b[ag]], writes=[scrF_b[tq]])
                P.op("dve", lambda e, ag=ag, tq=tq: e.scalar_tensor_tensor(
                    out=scrF[:, 2 + tq, TS], in0=scrF[:, tq, TS], scalar=1.0, in1=psA[ag][:, TS], op0=ALU.add,
                    op1=ALU.mult), reads=[scrF_b[tq], psA_b[ag]], writes=[scrF_b[2 + tq]])
                als = [hT_b[f]]
                P.op("dve", lambda e, au=au, tq=tq, f=f: e.scalar_tensor_tensor(
                    out=hTv(f)[:, TS], in0=scrF[:, 2 + tq, TS], scalar=0.5, in1=psA[au][:, TS], op0=ALU.mult,
                    op1=ALU.mult), reads=[scrF_b[2 + tq], psA_b[au]], writes=als)
                if f % 2 == 1:
                    consumed(T, B_GU[i])
            if dbg is not None and dbg.endswith("_ffn"):
                P.mute = True
            for b in range(nb):
                a0, a1 = nextA(), nextA()
                for kf in range(NF):
                    sl = cur_slot[B_DN[kf // 4]]
                    wv = wring[:, sl, :].rearrange("p (k c) -> p k c", k=4)
                    for half, a in enumerate((a0, a1)):
                        P.op("pe", lambda e, a=a, kf=kf, wv=wv, b=b, half=half: e.matmul(
                            psA[a][0:nr, :], lhsT=hTv(kf)[:, b * 128:b * 128 + nr],
                            rhs=wv[:, kf % 4, half * 512:(half + 1) * 512], start=(kf == 0), stop=(kf == NF - 1)),
                            reads=[ring_b[sl], hT_b[kf]], writes=[psA_b[a]], signal=(kf == NF - 1))
                    if b == nb - 1 and (kf % 4 == 3 or kf == NF - 1):
                        consumed(T, B_DN[kf // 4])
                if nxt is not None and b < nxt["nb"] and OVERLAP_N1:
                    qn = nxt["xinq"][b]
                    norm_transpose(nxt, b, xin[0:nxt["nr"], qn, :], xin_b[qn], SS1, R1, "n1")
                    nxt["n1_done"] = True
                for half, a in enumerate((a0, a1)):
                    P.op("dve", lambda e, a=a, b=b, half=half: e.tensor_tensor(
                        out=X[0:nr, b, half * 512:(half + 1) * 512], in0=psA[a][0:nr, :],
                        in1=X[0:nr, b, half * 512:(half + 1) * 512], op=ALU.add), reads=[psA_b[a]], writes=[X_b[b]])
                q = T["xq"][0] % 2
                T["xq"][0] += 1
                P.op("act", lambda e, b=b, q=q: e.activation(out=xsb[0:nr, q, :], in_=X[0:nr, b, :], func=AF.Square,
                                                             accum_out=st[0:nr, SS3 + b:SS3 + b + 1]),
                     reads=[X_b[b]], writes=[xsb_b[q], stb("n3ss%d" % b)])
                pool_rsqrt(R3 + b, SS3 + b, 1, 1.0 / D, nr, [stb("n3ss%d" % b)], [stb("n3r%d" % b)])
                P.op("dve", lambda e, b=b: e.scalar_tensor_tensor(
                    out=X[0:nr, b, :], in0=X[0:nr, b, :], scalar=st[0:nr, R3 + b:R3 + b + 1], in1=gfin[0:nr, :],
                    op0=ALU.mult, op1=ALU.mult), reads=[stb("n3r%d" % b), b_const], writes=[X_b[b]])
                r0 = row0 + b * 128
                P.dma("pool", ch_y[b], ydst[r0:r0 + nr, :], X[0:nr, b, :], reads=[X_b[b]])

        tiles = []
        xq = [0]
        for s in range(NSEQ):
            for t in range(NT):
                half = t % 2
                tiles.append(dict(
                    kind="p", s=s, t=t, ntok=512, nb=4, nr=128, nseg=1, L=512, first=(t == 0), last=(t == NT - 1),
                    row0=s * SEQ + t * 512, xsrc=xp, ydst=y_prompt, woff=half * 512, whalf=half,
                    vblk=[(4 * t + b) % 8 for b in range(4)], xinq=[0, 1, 2, 3], xq=xq, y2={}, yaq=0, gates={},
                    kvdst={"k": [prompt_k[s][b * 128:(b + 1) * 128, :] for b in range(4)],
                           "v": [prompt_v[s][b * 128:(b + 1) * 128, :] for b in range(4)]},
                    convdst=[prompt_conv[s]], lrudst=[prompt_lru[s]]))
        nr_s = NS * DSEQ
        tiles.append(dict(
            kind="s", s=0, t=0, ntok=nr_s, nb=1, nr=nr_s, nseg=NS, L=DSEQ, first=True, last=True, row0=0, xsrc=xs,
            ydst=y_sample, woff=512, whalf=1, vblk=[4], xinq=[0], xq=xq, y2={}, yaq=0, gates={},
            kvdst={"k": [sample_k[:, :]], "v": [sample_v[:, :]]},
            convdst=[sample_conv[sg] for sg in range(NS)], lrudst=[sample_lru[sg] for sg in range(NS)]))
        for i, T in enumerate(tiles):
            T["next"] = tiles[i + 1] if i + 1 < len(tiles) else None
            seq = B_IN + [B_INV] + ([B_INK] if T["last"] else []) + B_OUT + B_GU + B_DN
            T["lpos"] = {blk: len(load_seq) + j for j, blk in enumerate(seq)}
            load_seq.extend(seq)
        if not (dbg is not None and dbg.startswith("setup")):
            emit_loads_upto(NSLOT)
        T0 = tiles[0]
        for b in range(T0["nb"] if not (dbg is not None and dbg.startswith("setup")) else 0):
            q = T0["xinq"][b]
            r0 = T0["row0"] + b * 128
            P.dma("sp", ch_xin[q], xin[0:T0["nr"], q, :], T0["xsrc"][r0:r0 + T0["nr"], :], writes=[xin_b[q]])
        if dbg is not None and dbg.startswith("setup"):
            tiles = []
        if dbg is not None and dbg.startswith("p1"):
            tiles = tiles[:1]
        for T in tiles:
            if T["kind"] == "s":
                for sg in range(NS):
                    for c in range(4):
                        P.dma("sp", ch_state, upre[:, c, sg * 19:sg * 19 + 3],
                              state_conv[sg][:, c * 128:(c + 1) * 128].rearrange("j p -> p j"),
                              writes=[upre_b[c]], allow_slow_non_contiguous=True)
                for c in range(4):
                    P.dma("sp", ch_state, hprev[:, c, 0:NS], state_lru[:, c * 128:(c + 1) * 128].rearrange("s p -> p s"),
                          writes=[hprev_b[c]], allow_slow_non_contiguous=True)
                P.seal(ch_state, upre_b + hprev_b)
            run_tile(T)
        P.mute = False
        for e in ("pool",):
            P.wait_all(e, store_chans + ch_ring + ch_xin + [ch_X, ch_state, ch_ck, ch_cv])
        P.flush(block)
    return nc


_CACHE = {}


def kernel(**inputs):
    NC = 8
    B, SEQ, _ = inputs["x_prompt"].shape
    DB, DSEQ, _ = inputs["x_sample"].shape
    NSEQ = B // NC
    NS = DB // NC
    key = (NSEQ, SEQ, NS, DSEQ)
    if key not in _CACHE:
        _CACHE[key] = build(NSEQ, SEQ, NS, DSEQ)
    nc = _CACHE[key]
    f = lambda a: np.ascontiguousarray(a, dtype=np.float32)
    shared = {}
    for n in ["norm_mix", "w_in", "conv_w", "conv_b", "lru_wa", "lru_ba", "lru_wx", "lru_bx", "lru_lambda",
              "rel_bias", "norm_lru_out", "norm_attn_out", "w_out", "norm_ffn", "w_gate", "w_up", "w_down"]:
        shared[n] = f(inputs[n][0])
    shared["norm_final"] = f(inputs["norm_final"])
    in_maps = []
    for c in range(NC):
        m = dict(shared)
        m["x_prompt"] = f(inputs["x_prompt"][c * NSEQ:(c + 1) * NSEQ]).reshape(NSEQ * SEQ, D)
        m["x_sample"] = f(inputs["x_sample"][c * NS:(c + 1) * NS]).reshape(NS * DSEQ, D)
        m["state_conv"] = f(inputs["state_conv"][0, c * NS:(c + 1) * NS])
        m["state_lru"] = f(inputs["state_lru"][0, c * NS:(c + 1) * NS])
        m["cache_k"] = f(inputs["cache_k"][0, c * NS:(c + 1) * NS]).reshape(NS, 512, 512)
        m["cache_v"] = f(inputs["cache_v"][0, c * NS:(c + 1) * NS]).reshape(NS, 512, 512)
        in_maps.append(m)
    res = run_bass_kernel_spmd(nc, in_maps, core_ids=list(range(NC)))
    R = res.results
    cat = lambda n: np.concatenate([np.asarray(r[n], dtype=np.float32) for r in R], axis=0)
    y_prompt = cat("y_prompt").reshape(B, SEQ, D)
    y_sample = cat("y_sample").reshape(DB, DSEQ, D)
    keep = min(512, SEQ)
    outs = (
        y_prompt, y_sample,
        cat("prompt_conv").reshape(1, B, 3, LW), cat("prompt_lru").reshape(1, B, LW),
        cat("prompt_k").reshape(1, B, keep, 8, 64), cat("prompt_v").reshape(1, B, keep, 8, 64),
        cat("sample_conv").reshape(1, DB, 3, LW), cat("sample_lru").reshape(1, DB, LW),
        cat("sample_k").reshape(1, DB, DSEQ, 8, 64), cat("sample_v").reshape(1, DB, DSEQ, 8, 64),
    )
    return outs
```

```python
import numpy as np
from contextlib import ExitStack
import concourse.bass as bass
import concourse.mybir as mybir
from concourse.bass_utils import run_bass_kernel_spmd

F32 = mybir.dt.float32
BF16 = mybir.dt.bfloat16
AF = mybir.ActivationFunctionType
ALU = mybir.AluOpType

D = 1024
KC = 8
LW = 512
DFF = 2816
NF = 22
INC = 2560
EPS = 1e-6
NEG = -30000.0
NCD = dict(allow_slow_non_contiguous=True)
C0 = 0.7978845608028654
C1 = 0.7978845608028654 * 0.044715
B_IN = [0, 1, 2, 3]
B_INV = 4
B_INK = 5
B_OUT = [6, 7]
B_GU = list(range(8, 19))
B_DN = list(range(19, 25))
NBLK = 25
NSLOT = 6
OVERLAP_N1 = True


class Buf:
    def __init__(self, name=""):
        self.name = name
        self.w = {}
        self.r = {}


class Chan:
    def __init__(self, sem):
        self.sem = sem
        self.count = 0


class Prog:
    ENG = ("pe", "act", "dve", "pool", "sp")

    def __init__(self, nc, es):
        self.nc = nc
        self.streams = {e: [] for e in self.ENG}
        self.sem = {e: es.enter_context(nc.semaphore("sem_" + e)) for e in self.ENG if e != "sp"}
        self.cnt = {e: 0 for e in self.ENG}
        self.waited = {e: {} for e in self.ENG}
        self.semid = {}
        self.es = es
        self.nchan = 0

    def chan(self):
        self.nchan += 1
        return Chan(self.es.enter_context(self.nc.semaphore("ch%d" % self.nchan)))

    def _key(self, sem):
        k = id(sem)
        self.semid[k] = sem
        return k

    def _deps(self, eng, reads, writes, skip=None):
        deps = {}
        for b in reads:
            for k, v in b.w.items():
                deps[k] = max(deps.get(k, 0), v)
        for b in writes:
            for k, v in b.w.items():
                deps[k] = max(deps.get(k, 0), v)
            for k, v in b.r.items():
                deps[k] = max(deps.get(k, 0), v)
        pek = self._key(self.sem["pe"])
        for k, v in deps.items():
            if eng == "pe" and k == pek:
                continue
            if skip is not None and k == skip[0] and v < skip[1]:
                continue
            if self.waited[eng].get(k, 0) >= v:
                continue
            self.waited[eng][k] = v
            sem = self.semid[k]
            self.streams[eng].append(lambda e, s=sem, vv=v: e.wait_ge(s, vv))

    def _mark(self, k, v, reads, writes):
        for b in reads:
            b.r[k] = max(b.r.get(k, 0), v)
        for b in writes:
            b.w = {k: v}
            b.r = {}

    mute = False

    def op(self, eng, fn, reads=(), writes=(), signal=True):
        if self.mute:
            return
        reads = [b for b in reads if b is not None]
        writes = [b for b in writes if b is not None]
        self._deps(eng, reads, writes)
        sem = self.sem[eng]
        k = self._key(sem)
        if signal:
            self.cnt[eng] += 1
            v = self.cnt[eng]
            self.streams[eng].append(lambda e, f=fn, s=sem: f(e).then_inc(s, 1))
        else:
            v = self.cnt[eng] + 1
            self.streams[eng].append(lambda e, f=fn: f(e))
        self._mark(k, v, reads, writes)

    def dma(self, q, ch, out, in_, reads=(), writes=(), **kw):
        if self.mute:
            return
        reads = [b for b in reads if b is not None]
        writes = [b for b in writes if b is not None]
        self._deps(q, reads, writes, skip=(self._key(ch.sem), ch.count))
        ch.count += 16
        sem = ch.sem
        self.streams[q].append(
            lambda e, o=out, i=in_, s=sem, kw=kw: e.dma_start(out=o, in_=i, **kw).then_inc(s, 16))
        self._mark(self._key(sem), ch.count, reads, writes)

    def seal(self, ch, bufs):
        k = self._key(ch.sem)
        for b in bufs:
            if k in b.w:
                b.w[k] = ch.count

    def wait_all(self, eng, chans):
        for ch in chans:
            if ch.count:
                k = self._key(ch.sem)
                if self.waited[eng].get(k, 0) < ch.count:
                    self.waited[eng][k] = ch.count
                    self.streams[eng].append(lambda e, s=ch.sem, v=ch.count: e.wait_ge(s, v))

    def flush(self, block):
        m = {"sp": block.sync, "act": block.scalar, "dve": block.vector, "pool": block.gpsimd,
             "pe": block.tensor}
        for e in self.ENG:
            lst = self.streams[e]
            if not lst:
                continue
            self.streams[e] = []

            def body(eng, lst=lst):
                for f in lst:
                    f(eng)
            m[e](body)


def build(NSEQ, SEQ, NS, DSEQ=16, dbg=None):
    NT = SEQ // 512
    nc = bass.Bass("TRN2", target_bir_lowering=False)
    dt = lambda n, s, k="ExternalInput", d=F32: nc.dram_tensor(n, s, d, kind=k).ap()
    xp = dt("x_prompt", [NSEQ * SEQ, D])
    xs = dt("x_sample", [NS * DSEQ, D])
    state_conv = dt("state_conv", [NS, 3, LW])
    state_lru = dt("state_lru", [NS, LW])
    cache_k = dt("cache_k", [NS, 512, 512])
    cache_v = dt("cache_v", [NS, 512, 512])
    norm_mix = dt("norm_mix", [D])
    w_in = dt("w_in", [D, INC])
    conv_w = dt("conv_w", [4, LW])
    conv_b = dt("conv_b", [LW])
    lru_wa = dt("lru_wa", [8, 64, 64])
    lru_ba = dt("lru_ba", [LW])
    lru_wx = dt("lru_wx", [8, 64, 64])
    lru_bx = dt("lru_bx", [LW])
    lru_lambda = dt("lru_lambda", [LW])
    rel_bias = dt("rel_bias", [8, 257])
    norm_lru_out = dt("norm_lru_out", [LW])
    norm_attn_out = dt("norm_attn_out", [LW])
    w_out = dt("w_out", [D, D])
    norm_ffn = dt("norm_ffn", [D])
    w_gate = dt("w_gate", [D, DFF])
    w_up = dt("w_up", [D, DFF])
    w_down = dt("w_down", [DFF, D])
    norm_final = dt("norm_final", [D])
    O = "ExternalOutput"
    y_prompt = dt("y_prompt", [NSEQ * SEQ, D], O)
    y_sample = dt("y_sample", [NS * DSEQ, D], O)
    prompt_conv = dt("prompt_conv", [NSEQ, 3, LW], O)
    prompt_lru = dt("prompt_lru", [NSEQ, LW], O)
    prompt_k = dt("prompt_k", [NSEQ, 512, 512], O)
    prompt_v = dt("prompt_v", [NSEQ, 512, 512], O)
    sample_conv = dt("sample_conv", [NS, 3, LW], O)
    sample_lru = dt("sample_lru", [NS, LW], O)
    sample_k = dt("sample_k", [NS * DSEQ, 512], O)
    sample_v = dt("sample_v", [NS * DSEQ, 512], O)
    wsc = dt("wsc", [NBLK, 128, 4096], "Internal", BF16)
    ext = dt("ext", [8, 640], "Internal")
    ext2 = dt("ext2", [8, 128, 640], "Internal")

    with ExitStack() as es:
        P = Prog(nc, es)
        sb = lambda n, s, d=F32: es.enter_context(nc.sbuf_tensor(n, s, d))
        wring = sb("wring", [128, NSLOT, 4096], BF16)
        X = sb("X", [128, 4, D])
        xin = sb("xin", [128, 4, D])
        actT = sb("actT", [128, 8, 512], BF16)
        upre = sb("upre", [128, 4, 515])
        gbuf = sb("gbuf", [128, 4, 512])
        qT = sb("qT", [128, 8, 512], BF16)
        kT = sb("kT", [128, 4, 1024], BF16)
        vext = sb("vext", [128, 8, 8, 66], BF16)
        big = sb("big", [128, NF * 512], BF16)
        ucb = sb("ucb", [128, 2, 512], BF16)
        pT = sb("pT", [128, 4, 512], BF16)
        ya = sb("ya", [128, 4, 512])
        yab = sb("yab", [128, 2, 512], BF16)
        xsb = sb("xsb", [128, 2, D], BF16)
        junk = sb("junk", [128, D], BF16)
        biasT = sb("biasT", [128, 3, 8, 128], BF16)
        biasN = sb("biasN", [64, 8, 64], BF16)
        scrF = sb("scrF", [128, 4, 512])
        gfin = sb("gfin", [128, D])
        gw = sb("gw", [128, 2, 4, 128], BF16)
        ident = sb("ident", [128, 128], BF16)
        ones = sb("ones", [128, 2], BF16)
        cst = sb("cst", [128, 64])
        st = sb("st", [128, 96])
        hprev = sb("hprev", [128, 4, 4])
        gains = sb("gains", [128, 3, 8])
        pp = lambda n: es.enter_context(nc.psum_tensor(n, [128, 512], F32))
        psA = [pp("psA%d" % i) for i in range(4)]
        psS = [pp("psS%d" % i) for i in range(2)]
        psO = [pp("psO%d" % i) for i in range(2)]

        CW = lambda j, c: cst[:, j * 4 + c: j * 4 + c + 1]
        CB = lambda c: cst[:, 16 + c: 17 + c]
        HBA = lambda c: cst[:, 20 + c: 21 + c]
        HBX = lambda c: cst[:, 24 + c: 25 + c]
        C8 = lambda c: cst[:, 28 + c: 29 + c]
        HC8 = lambda c: cst[:, 32 + c: 33 + c]
        LAM = cst[:, 36:40]
        EM05 = cst[:, 40:41]
        EP05 = cst[:, 41:42]
        CBH = cst[:, 48:56]
        b_cst = Buf("cst")
        b_gains = Buf("gains")

        ring_b = [Buf("ring%d" % i) for i in range(NSLOT)]
        X_b = [Buf() for _ in range(4)]
        xin_b = [Buf() for _ in range(4)]
        actT_b = [Buf() for _ in range(4)]
        upre_b = [Buf() for _ in range(4)]
        gbuf_b = [Buf() for _ in range(4)]
        qT_b = [Buf() for _ in range(4)]
        kT_b = [[Buf(), Buf()] for _ in range(4)]
        vext_b = [Buf() for _ in range(8)]
        hT_b = [Buf() for _ in range(NF)]
        ucb_b = [Buf(), Buf()]
        pT_b = [Buf() for _ in range(4)]
        ya_b = [Buf() for _ in range(4)]
        yab_b = [Buf(), Buf()]
        xsb_b = [Buf(), Buf()]
        b_junk = Buf()
        scrF_b = [Buf() for _ in range(4)]
        psA_b = [Buf() for _ in range(4)]
        psS_b = [Buf() for _ in range(2)]
        psO_b = [Buf() for _ in range(2)]
        wsc_b = [Buf() for _ in range(NBLK)]
        b_bias = Buf()
        b_const = Buf()
        hprev_b = [Buf() for _ in range(4)]
        st_b = {}

        def stb(name):
            if name not in st_b:
                st_b[name] = Buf(name)
            return st_b[name]
        SS1, R1, SS2, R2, SS3, R3, SSA, SSL, RL, SCA, TMP, RS = (0, 4, 8, 12, 16, 20, 24, 28, 32, 36, 40, 48)

        def tmpv(q, k):
            i = q * 5 + k
            return big[:, i * 1024:(i + 1) * 1024].bitcast(F32)

        def tmpb(q, k):
            i = q * 5 + k
            return [hT_b[2 * i], hT_b[2 * i + 1]]

        def y2v(i):
            return big[:, (20 + i) * 512:(21 + i) * 512]

        def y2b(i):
            return [hT_b[20 + i]]
        hTv = lambda f: big[:, f * 512:(f + 1) * 512]

        rrA = [0]

        def nextA():
            i = rrA[0] % 4
            rrA[0] += 1
            return i
        ch_misc = P.chan()
        ch_ring = [P.chan() for _ in range(NSLOT)]
        ch_xin = [P.chan() for _ in range(4)]
        ch_X = P.chan()
        ch_y = [P.chan() for _ in range(4)]
        ch_kv = [P.chan() for _ in range(4)]
        ch_conv = [P.chan() for _ in range(4)]
        ch_lru = [P.chan() for _ in range(4)]
        ch_state = P.chan()
        ch_ck = P.chan()
        ch_cv = P.chan()
        ch_e = [P.chan() for _ in range(3)]
        store_chans = ch_y + ch_kv + ch_conv + ch_lru

        with ExitStack() as es1:
            sb1 = lambda n, s, d=F32: es1.enter_context(nc.sbuf_tensor(n, s, d))
            stage = wring[:, 0:4, :].rearrange("p (q a) c -> p q (a c)", q=2).bitcast(F32)
            obf = wring[:, 4:6, :]
            gst = ya[:, 0:2, :].rearrange("p a (j c) -> p a j c", j=4)
            identf = sb1("identf", [128, 128])
            E8 = sb1("E8", [8, 640])
            B34 = scrF[:].rearrange("p a c -> p (a c)").rearrange("p (i h c) -> p i h c", i=2, h=8)
            BNf = sb1("BNf", [64, 8, 64])
            stage_b = [Buf(), Buf()]
            obf_b = [Buf(), Buf()]
            b_gst, b_identf, b_E8, b_B34, b_BNf, b_ext, b_ext2 = (Buf() for _ in range(7))
            ch_stg = [P.chan(), P.chan()]
            ch_obf = [P.chan(), P.chan()]
            block = es1.enter_context(nc.Block())

            cstage = sb1("cstage", [64, 128])
            gstage = sb1("gstage", [24, 128])
            b_cstage, b_gstage = Buf(), Buf()
            P.op("pool", lambda e: e.memset(cstage[:], 0.0), writes=[b_cstage])
            P.dma("sp", ch_misc, gstage[0:8, :], norm_mix.rearrange("(k p) -> k p", p=128), writes=[b_gstage])
            P.dma("sp", ch_misc, gstage[8:16, :], norm_ffn.rearrange("(k p) -> k p", p=128), writes=[b_gstage])
            P.dma("sp", ch_misc, gstage[16:20, :], norm_lru_out.rearrange("(k p) -> k p", p=128), writes=[b_gstage])
            P.dma("sp", ch_misc, gstage[20:24, :], norm_attn_out.rearrange("(k p) -> k p", p=128), writes=[b_gstage])
            P.dma("sp", ch_misc, cstage[0:16, :], conv_w.rearrange("j (c p) -> (j c) p", p=128), writes=[b_cstage])
            for row, src in [(16, conv_b), (20, lru_ba), (24, lru_bx), (36, lru_lambda)]:
                P.dma("sp", ch_misc, cstage[row:row + 4, :], src.rearrange("(c p) -> c p", p=128), writes=[b_cstage])
            P.dma("sp", ch_misc, CBH, bass.AP(tensor=rel_bias.tensor, offset=256, ap=[[0, 128], [257, 8]]),
                  writes=[b_cst], **NCD)
            P.dma("sp", ch_misc, gfin[:], bass.AP(tensor=norm_final.tensor, offset=0, ap=[[0, 128], [1, D]]),
                  writes=[b_const])
            P.op("pool", lambda e: e.memset(gst[:], 0.0), writes=[b_gst])
            for g, src in enumerate([lru_wa, lru_wx]):
                for n in range(8):
                    j, hf = n // 2, n % 2
                    P.dma("sp", ch_misc, gst[hf * 64:(hf + 1) * 64, g, j, hf * 64:(hf + 1) * 64], src[n],
                          writes=[b_gst])
            P.dma("sp", ch_misc, E8[:, 0:257], rel_bias, writes=[b_E8])
            P.seal(ch_misc, [b_gains, b_cst, b_const, b_gst, b_E8, b_cstage, b_gstage])
            P.op("pool", lambda e: e.memset(identf[:], 0.0), writes=[b_identf])
            P.op("pool", lambda e: e.affine_select(out=identf[:], in_=identf[:], pattern=[[-1, 128]],
                                                   compare_op=ALU.not_equal, fill=1.0, base=0,
                                                   channel_multiplier=1), writes=[b_identf])
            P.op("dve", lambda e: e.tensor_copy(out=ident[:], in_=identf[:]), reads=[b_identf], writes=[b_const])
            P.op("pe", lambda e: e.transpose(out=psA[0][:, 0:64], in_=cstage[:, :], identity=identf[0:64, 0:64]),
                 reads=[b_cstage, b_identf], writes=[psA_b[0]])
            P.op("pe", lambda e: e.transpose(out=psA[1][:, 0:24], in_=gstage[:, :], identity=identf[0:24, 0:24]),
                 reads=[b_gstage, b_identf], writes=[psA_b[1]])
            P.op("act", lambda e: e.activation(out=cst[:, 0:40], in_=psA[0][:, 0:40], func=AF.Copy),
                 reads=[psA_b[0]], writes=[b_cst])
            P.op("act", lambda e: e.activation(out=gains[:].rearrange("p a k -> p (a k)"), in_=psA[1][:, 0:24],
                                               func=AF.Copy), reads=[psA_b[1]], writes=[b_gains])
            P.op("pool", lambda e: e.memset(cst[:, 40:41], -0.5), writes=[b_cst])
            P.op("pool", lambda e: e.memset(cst[:, 41:42], 0.5), writes=[b_cst])
            P.op("dve", lambda e: e.tensor_scalar(out=cst[:, 20:28], in0=cst[:, 20:28], scalar1=0.5, scalar2=None,
                                                  op0=ALU.mult), reads=[], writes=[b_cst])
            P.op("act", lambda e: e.activation(out=cst[:, 44:48], in_=LAM, func=AF.Exp, scale=-1.0), writes=[b_cst])
            P.op("act", lambda e: e.activation(out=cst[:, 44:48], in_=cst[:, 44:48], func=AF.Ln, bias=1.0),
                 writes=[b_cst])
            P.op("dve", lambda e: e.tensor_scalar(out=cst[:, 28:32], in0=cst[:, 44:48], scalar1=-8.0, scalar2=None,
                                                  op0=ALU.mult), writes=[b_cst])
            P.op("dve", lambda e: e.tensor_scalar(out=cst[:, 32:36], in0=cst[:, 44:48], scalar1=-4.0, scalar2=None,
                                                  op0=ALU.mult), writes=[b_cst])
            P.op("dve", lambda e: e.tensor_copy(out=gw[:], in_=gst[:]), reads=[b_gst], writes=[b_const])
            P.op("pool", lambda e: e.memset(ones[:], 1.0), writes=[b_const])
            for blk in range(8):
                P.op("pool", lambda e, blk=blk: e.memset(vext[:, blk, :, 64:65], 1.0), writes=[vext_b[blk]])
            P.op("pool", lambda e: e.memset(qT[:], 0.0), writes=qT_b)
            P.op("pool", lambda e: e.memset(hprev[:], 0.0), writes=hprev_b)
            P.op("pool", lambda e: e.memset(upre[:], 0.0), writes=upre_b)
            SKIPB = dbg is not None and "nobias" in dbg
            SKIPW = dbg is not None and "noweights" in dbg
            P.mute = SKIPB
            P.op("dve", lambda e: e.tensor_copy(out=E8[:, 257:640], in_=E8[:, 256:257].to_broadcast([8, 383])),
                 writes=[b_E8])
            P.dma("sp", ch_e[0], ext, E8[:], reads=[b_E8], writes=[b_ext])
            P.dma("sp", ch_e[1], ext2, bass.AP(tensor=ext.tensor, offset=0, ap=[[640, 8], [0, 128], [1, 640]]),
                  reads=[b_ext], writes=[b_ext2])
            for i, base in enumerate([256, 128]):
                P.dma("sp", ch_e[2], B34[:, i, :, :],
                      bass.AP(tensor=ext2.tensor, offset=base, ap=[[639, 128], [81920, 8], [1, 128]]),
                      reads=[b_ext2], writes=[b_B34])
            P.op("pool", lambda e: e.memset(BNf[:], NEG), writes=[b_BNf])
            for s in range(4):
                P.dma("sp", ch_e[2], BNf[16 * s:16 * s + 16, :, 16 * s:16 * s + 16],
                      bass.AP(tensor=ext2.tensor, offset=128, ap=[[639, 16], [81920, 8], [1, 16]]),
                      reads=[b_ext2], writes=[b_BNf])
            P.seal(ch_e[2], [b_B34, b_BNf])
            for i in range(2):
                P.op("dve", lambda e, i=i: e.tensor_tensor(
                    out=B34[:, i, :, :], in0=B34[:, i, :, :],
                    in1=CBH.unsqueeze(2).to_broadcast([128, 8, 128]), op=ALU.subtract),
                    reads=[b_cst], writes=[b_B34])
                P.op("dve", lambda e, i=i: e.tensor_scalar(out=biasT[:, 1 + i, :, :], in0=B34[:, i, :, :], scalar1=8.0,
                                                          scalar2=None, op0=ALU.mult), reads=[b_B34], writes=[b_bias])
            P.op("pool", lambda e: e.memset(biasT[:, 0, :, :], 0.0), writes=[b_bias])
            P.op("pool", lambda e: e.memset(biasT[0:64, 0, :, 64:128], NEG), writes=[b_bias])
            P.op("pool", lambda e: e.memset(biasT[64:128, 2, :, 0:64], NEG), writes=[b_bias])
            P.op("dve", lambda e: e.tensor_tensor(out=BNf[:], in0=BNf[:],
                                                  in1=CBH[0:64, :].unsqueeze(2).to_broadcast([64, 8, 64]),
                                                  op=ALU.subtract), reads=[b_cst], writes=[b_BNf])
            P.op("dve", lambda e: e.tensor_scalar(out=biasN[:], in0=BNf[:], scalar1=8.0, scalar2=None, op0=ALU.mult),
                 reads=[b_BNf], writes=[b_bias])

            P.mute = SKIPW
            w_in_r = w_in.rearrange("(k p) c -> p k c", p=128)
            w_out_r = w_out.rearrange("(k p) c -> p k c", p=128)
            w_g_r = w_gate.rearrange("(k p) c -> p k c", p=128)
            w_u_r = w_up.rearrange("(k p) c -> p k c", p=128)
            w_d_r = w_down.rearrange("(k p) c -> p k c", p=128)
            eng_rr = ["act", "dve", "act", "dve", "act"]
            cnt = [0]

            def conv_block(blk, loads, gi, views, plain=False, used=4096):
                q = blk % 2
                for dst, src in loads:
                    P.dma("sp", ch_stg[q], dst, src, writes=[stage_b[q]])
                if plain:
                    eng = eng_rr[cnt[0] % 5]
                    cnt[0] += 1
                    if eng == "act":
                        P.op(eng, lambda e: e.activation(out=obf[:, q, 0:used], in_=stage[:, q, 0:used], func=AF.Copy),
                             reads=[stage_b[q]], writes=[obf_b[q]])
                    else:
                        P.op(eng, lambda e: e.tensor_copy(out=obf[:, q, 0:used], in_=stage[:, q, 0:used]),
                             reads=[stage_b[q]], writes=[obf_b[q]])
                else:
                    for k, (iv, ov, gk) in enumerate(views):
                        eng = eng_rr[cnt[0] % 5]
                        cnt[0] += 1
                        gsc = gains[:, gi, gk:gk + 1]
                        if eng == "act":
                            P.op(eng, lambda e, iv=iv, ov=ov, gsc=gsc: e.activation(out=ov, in_=iv, func=AF.Copy,
                                                                                 scale=gsc),
                                 reads=[stage_b[q], b_gains], writes=[obf_b[q]])
                        else:
                            P.op(eng, lambda e, iv=iv, ov=ov, gsc=gsc: e.tensor_scalar(out=ov, in0=iv, scalar1=gsc,
                                                                                    scalar2=None, op0=ALU.mult),
                                 reads=[stage_b[q], b_gains], writes=[obf_b[q]])
                P.dma("pool", ch_obf[q], wsc[blk][:, 0:used], obf[:, q, 0:used], reads=[obf_b[q]], writes=[wsc_b[blk]])

            def v5(t, q, a, b):
                return t[:, q, :].rearrange("p (a b c) -> p a b c", a=a, b=b)
            for i in range(4):
                blk = B_IN[i]
                q = blk % 2
                sv = stage[:, q, :].rearrange("p (k a c) -> p k a c", k=8, a=4)
                ov = v5(obf, q, 4, 8)
                loads = [(stage[:, q, :].rearrange("p (k c) -> p k c", k=8), w_in_r[:, :, i * 512:(i + 1) * 512])]
                conv_block(blk, loads, 0, [(sv[:, k, :, :], ov[:, :, k, :], k) for k in range(8)])
            for blk, c0 in [(B_INV, 2048), (B_INK, 1536)]:
                q = blk % 2
                sv = stage[:, q, :].rearrange("p (k c) -> p k c", k=8)
                ov = obf[:, q, :].rearrange("p (k c) -> p k c", k=8)
                conv_block(blk, [(sv, w_in_r[:, :, c0:c0 + 512])], 0,
                           [(sv[:, k, :], ov[:, k, :], k) for k in range(8)])
            for j in range(2):
                blk = B_OUT[j]
                q = blk % 2
                sv = stage[:, q, :].rearrange("p (k c) -> p k c", k=4)
                ov = obf[:, q, :].rearrange("p (k c) -> p k c", k=4)
                conv_block(blk, [(sv, w_out_r[:, 4 * j:4 * j + 4, :])], 2,
                           [(sv[:, k, :], ov[:, k, :], 4 * j + k) for k in range(4)])
            for i in range(11):
                blk = B_GU[i]
                q = blk % 2
                sv = stage[:, q, :].rearrange("p (g k f c) -> p g k f c", g=2, k=8, f=2)
                ov = obf[:, q, :].rearrange("p (f g k c) -> p f g k c", f=2, g=2, k=8)
                sl = stage[:, q, :].rearrange("p (g k c) -> p g k c", g=2, k=8)
                loads = [(sl[:, 0, :, :], w_g_r[:, :, 2 * i * 128:(2 * i + 2) * 128]),
                         (sl[:, 1, :, :], w_u_r[:, :, 2 * i * 128:(2 * i + 2) * 128])]
                conv_block(blk, loads, 1, [(sv[:, :, k, :, :].rearrange("p g f c -> p f g c"), ov[:, :, :, k, :], k)
                                           for k in range(8)])
            for i in range(6):
                blk = B_DN[i]
                q = blk % 2
                nk = 4 if i < 5 else 2
                sv = stage[:, q, 0:nk * 1024].rearrange("p (k c) -> p k c", k=nk)
                conv_block(blk, [(sv, w_d_r[:, 4 * i:4 * i + nk, :])], None, None, plain=True, used=nk * 1024)
            P.mute = False
            for e in ("sp", "pool", "act", "dve", "pe"):
                P.wait_all(e, [ch_misc] + ch_e + ch_stg + ch_obf)
            P.flush(block)

        block = es.enter_context(nc.Block())
        load_seq = []
        next_load = [0]

        def emit_loads_upto(n):
            while next_load[0] < min(n, len(load_seq)):
                idx = next_load[0]
                blk = load_seq[idx]
                i = idx % NSLOT
                used = 2048 if blk == B_DN[5] else 4096
                P.dma("sp", ch_ring[i], wring[:, i, 0:used], wsc[blk][:, 0:used], reads=[wsc_b[blk]],
                      writes=[ring_b[i]])
                next_load[0] += 1

        def consumed(T, blk):
            emit_loads_upto(T["lpos"][blk] + NSLOT + 1)

        def pool_rsqrt(dst_col, src_col, n, scale, nrow=128, srcs=(), dsts=()):
            P.op("pool", lambda e: e.tensor_scalar(out=st[0:nrow, TMP:TMP + n], in0=st[0:nrow, src_col:src_col + n],
                                                   scalar1=scale, scalar2=EPS, op0=ALU.mult, op1=ALU.add),
                 reads=list(srcs), writes=[stb("tmp")])
            P.op("pool", lambda e: e.tensor_tensor(out=st[0:nrow, dst_col:dst_col + n], in0=st[0:nrow, TMP:TMP + n],
                                                   in1=EM05[0:nrow, :].to_broadcast([nrow, n]), op=ALU.pow),
                 reads=[stb("tmp"), b_cst], writes=list(dsts))

        def norm_transpose(T, b, src_ap, src_b, ss_col, r_col, tag, phase="both"):
            nr = T["nr"]
            if phase in ("both", "stats"):
                P.op("act", lambda e: e.activation(out=junk[0:nr, :], in_=src_ap, func=AF.Square,
                                                   accum_out=st[0:nr, ss_col + b:ss_col + b + 1]),
                     reads=[src_b], writes=[b_junk, stb(tag + "ss%d" % b)])
                pool_rsqrt(r_col + b, ss_col + b, 1, 1.0 / D, nr, [stb(tag + "ss%d" % b)], [stb(tag + "r%d" % b)])
            if phase == "stats":
                return
            q = T["xq"][0] % 2
            T["xq"][0] += 1
            P.op("dve", lambda e: e.tensor_scalar(out=xsb[0:nr, q, :], in0=src_ap, scalar1=st[0:nr, r_col + b:r_col + b + 1],
                                                  scalar2=None, op0=ALU.mult),
                 reads=[src_b, stb(tag + "r%d" % b)], writes=[xsb_b[q]])
            a = nextA()
            pv = psA[a][:].bitcast(BF16).rearrange("p (k c) -> p k c", k=8)
            for k in range(8):
                P.op("pe", lambda e, k=k: e.transpose(out=pv[:, k, 0:nr], in_=xsb[0:nr, q, k * 128:(k + 1) * 128],
                                                      identity=ident[0:nr, 0:nr]),
                     reads=[xsb_b[q], b_const], writes=[psA_b[a]], signal=(k == 7))
            eng = "act" if b % 2 == 0 else "dve"
            dst = actT[:, :, b * 128:b * 128 + nr]
            if eng == "act":
                P.op("act", lambda e: e.activation(out=dst, in_=pv[:, :, 0:nr], func=AF.Copy),
                     reads=[psA_b[a]], writes=[actT_b[b]])
            else:
                P.op("dve", lambda e: e.tensor_copy(out=dst, in_=pv[:, :, 0:nr]), reads=[psA_b[a]], writes=[actT_b[b]])

        def run_tile(T):
            kind = T["kind"]
            NTOK, nb, nr = T["ntok"], T["nb"], T["nr"]
            nseg, L = T["nseg"], T["L"]
            s_idx, t_idx = T["s"], T["t"]
            last = T["last"]
            first = T["first"]
            want_kv = last
            row0 = T["row0"]
            xsrc = T["xsrc"]
            ydst = T["ydst"]
            TS = slice(0, NTOK)
            cur_slot = {blk: idx % NSLOT for blk, idx in T["lpos"].items()}
            if kind == "p":
                P.dma("sp", ch_X, X[:, :, :], xsrc[row0:row0 + 512, :].rearrange("(b p) d -> p b d", p=128),
                      writes=X_b)
            else:
                P.dma("sp", ch_X, X[0:nr, 0, :], xsrc[row0:row0 + nr, :], writes=[X_b[0]])
            if not T.get("n1_done"):
                for b in range(nb):
                    q = T["xinq"][b]
                    norm_transpose(T, b, xin[0:nr, q, :], xin_b[q], SS1, R1, "n1")
            nxt = T.get("next")
            if dbg is not None and dbg.endswith("_n1"):
                P.mute = True
            u_view = lambda c: upre[:, c, 0:nseg * (3 + L)].rearrange("p (s l) -> p s l", s=nseg)
            woff = T["woff"]
            def lv(ap):
                return ap.rearrange("p (s l) -> p s l", s=nseg)

            def lru_front(c):
                q = c % 2
                T4 = tmpv(q, 3)[:, TS]
                B4 = tmpb(q, 3)
                uv = u_view(c)
                P.op("dve", lambda e: e.tensor_scalar(
                    out=lv(T4), in0=uv[:, :, 3:3 + L], scalar1=CW(3, c), scalar2=CB(c), op0=ALU.mult, op1=ALU.add),
                    reads=[upre_b[c], b_cst], writes=B4)
                for j in (2, 1, 0):
                    P.op("dve", lambda e, j=j: e.scalar_tensor_tensor(
                        out=lv(T4), in0=uv[:, :, j:j + L], scalar=CW(j, c), in1=lv(T4), op0=ALU.mult, op1=ALU.add),
                        reads=[upre_b[c], b_cst], writes=B4)
                if last:
                    for sg in range(nseg):
                        dstc = T["convdst"][sg][:, c * 128:(c + 1) * 128].rearrange("j p -> p j")
                        P.dma("pool", ch_conv[c], dstc, uv[:, sg, L:L + 3], reads=[upre_b[c]], **NCD)
                if kind == "p" and not last:
                    P.op("pool", lambda e: e.tensor_copy(out=upre[:, c, 0:3], in_=upre[:, c, 512:515]),
                         writes=[upre_b[c]])
                elif kind == "p" and last:
                    P.op("pool", lambda e: e.memset(upre[:, c, 0:3], 0.0), writes=[upre_b[c]])
                P.op("act", lambda e: e.activation(out=ucb[:, q, TS], in_=T4, func=AF.Copy), reads=B4,
                     writes=[ucb_b[q]])
                ar, ai = nextA(), nextA()
                T["gates"][c] = (ar, ai)
                P.op("pe", lambda e: e.matmul(psA[ar][:, TS], lhsT=gw[:, 0, c, :], rhs=ucb[:, q, TS],
                                              start=True, stop=True),
                     reads=[ucb_b[q], b_const], writes=[psA_b[ar]])
                P.op("pe", lambda e: e.matmul(psA[ai][:, TS], lhsT=gw[:, 1, c, :], rhs=ucb[:, q, TS],
                                              start=True, stop=True),
                     reads=[ucb_b[q], b_const], writes=[psA_b[ai]])

            def lru_chainA(c):
                q = c % 2
                T1, T2, T3, T4, T5 = (tmpv(q, k)[:, TS] for k in range(5))
                B1, B2, B3, B4, B5 = (tmpb(q, k) for k in range(5))
                ar, ai = T["gates"][c]
                P.op("act", lambda e: e.activation(out=T5, in_=gbuf[:, c, TS], func=AF.Square),
                     reads=[gbuf_b[c]], writes=B5)
                P.op("pool", lambda e: e.tensor_scalar(out=T5, in0=T5, scalar1=C1, scalar2=C0, op0=ALU.mult,
                                                       op1=ALU.add), writes=B5)
                P.op("pool", lambda e: e.tensor_tensor(out=T5, in0=T5, in1=gbuf[:, c, TS], op=ALU.mult),
                     reads=[gbuf_b[c]], writes=B5)
                P.op("act", lambda e: e.activation(out=T1, in_=psA[ar][:, TS], func=AF.Tanh, bias=HBA(c), scale=0.5),
                     reads=[psA_b[ar], b_cst], writes=B1)
                P.op("act", lambda e: e.activation(out=T2, in_=psA[ai][:, TS], func=AF.Tanh, bias=HBX(c), scale=0.5),
                     reads=[psA_b[ai], b_cst], writes=B2)
                P.op("act", lambda e: e.activation(out=T3, in_=T1, func=AF.Exp, bias=HC8(c), scale=HC8(c)),
                     reads=B1 + [b_cst], writes=B3)
                P.op("act", lambda e: e.activation(out=T1, in_=T1, func=AF.Exp, bias=C8(c), scale=C8(c)),
                     reads=[b_cst], writes=B1)
                P.op("act", lambda e: e.activation(out=T5, in_=T5, func=AF.Tanh), writes=B5)
                P.op("dve", lambda e: e.scalar_tensor_tensor(out=T2, in0=T2, scalar=1.0, in1=T4, op0=ALU.add,
                                                             op1=ALU.mult), reads=B4, writes=B2)
                P.op("dve", lambda e: e.scalar_tensor_tensor(out=T5, in0=T5, scalar=1.0, in1=gbuf[:, c, TS],
                                                             op0=ALU.add, op1=ALU.mult), reads=[gbuf_b[c]], writes=B5)

            def lru_sqrt(c):
                q = c % 2
                T1 = tmpv(q, 0)[:, TS]
                P.op("act", lambda e: e.activation(out=T1, in_=T1, func=AF.Sqrt, bias=0.25, scale=-0.25),
                     writes=tmpb(q, 0))

            def lru_chainB(c):
                q = c % 2
                T1, T2, T3, T4, T5 = (tmpv(q, k)[:, TS] for k in range(5))
                B1, B2, B3, B4, B5 = (tmpb(q, k) for k in range(5))
                P.op("dve", lambda e: e.tensor_tensor(out=T2, in0=T2, in1=T1, op=ALU.mult), reads=B1, writes=B2)
                for sg in range(nseg):
                    init = hprev[:, c, sg:sg + 1]
                    P.op("dve", lambda e, sg=sg, init=init: e.tensor_tensor_scan(
                        out=lv(T4)[:, sg, :], data0=lv(T3)[:, sg, :], data1=lv(T2)[:, sg, :], initial=init,
                        op0=ALU.mult, op1=ALU.add), reads=B3 + B2 + [hprev_b[c]], writes=B4)
                if last:
                    for sg in range(nseg):
                        dsth = T["lrudst"][sg][c * 128:(c + 1) * 128].rearrange("(p o) -> p o", o=1)
                        P.dma("pool", ch_lru[c], dsth, lv(T4)[:, sg, L - 1:L], reads=B4, **NCD)
                if kind == "p":
                    if not last:
                        P.op("pool", lambda e: e.tensor_copy(out=hprev[:, c, 0:1], in_=T4[:, NTOK - 1:NTOK]),
                             reads=B4, writes=[hprev_b[c]])
                    else:
                        P.op("pool", lambda e: e.memset(hprev[:, c, 0:1], 0.0), writes=[hprev_b[c]])
                P.op("dve", lambda e: e.scalar_tensor_tensor(out=actT[:, c, TS], in0=T5, scalar=0.5, in1=T4,
                                                             op0=ALU.mult, op1=ALU.mult),
                     reads=B5 + B4, writes=actT_b[0:nb])
                P.op("pool", lambda e: e.tensor_tensor(out=y2v(q)[:, TS], in0=actT[:, c, TS], in1=actT[:, c, TS],
                                                       op=ALU.mult), reads=actT_b[0:nb], writes=y2b(q))

            def lru_ssl(c):
                q = c % 2
                for b in range(nb):
                    P.op("pe", lambda e, b=b: e.matmul(
                        psO[1][0:nr, 300 + 4 * b + c:301 + 4 * b + c], lhsT=y2v(q)[:, b * 128:b * 128 + nr],
                        rhs=ones[:, 0:1], start=True, stop=True, skip_group_check=True),
                        reads=y2b(q) + [b_const], writes=[psO_b[1]], signal=(b == nb - 1))

            def lru_finish():
                P.op("dve", lambda e: e.tensor_reduce(
                    out=st[0:nr, SSL:SSL + nb], in_=psO[1][0:nr, 300:300 + 4 * nb].rearrange("p (b c) -> p b c", c=4),
                    axis=mybir.AxisListType.X, op=ALU.add), reads=[psO_b[1]], writes=[stb("ssl")])
                P.op("pool", lambda e: e.tensor_scalar(out=st[0:nr, TMP + 4:TMP + 4 + nb], in0=st[0:nr, SSL:SSL + nb],
                                                       scalar1=1.0 / LW, scalar2=EPS, op0=ALU.mult, op1=ALU.add),
                     reads=[stb("ssl")], writes=[stb("tl")])
                P.op("pool", lambda e: e.tensor_tensor(out=st[0:nr, RL:RL + nb], in0=st[0:nr, TMP + 4:TMP + 4 + nb],
                                                       in1=EM05[0:nr, :].to_broadcast([nr, nb]), op=ALU.pow),
                     reads=[stb("tl"), b_cst], writes=[stb("rl")])
                P.op("pool", lambda e: e.tensor_tensor(out=st[0:nr, 56:56 + nb], in0=st[0:nr, TMP + 4:TMP + 4 + nb],
                                                       in1=EP05[0:nr, :].to_broadcast([nr, nb]), op=ALU.pow),
                     reads=[stb("tl"), b_cst], writes=[stb("sql")])

            for m in range(16):
                if m == 8:
                    for c_ in (0, 1):
                        lru_front(c_)
                    for c_ in (0, 1):
                        lru_chainA(c_)
                    for c_ in (0, 1):
                        lru_sqrt(c_)
                blk = B_IN[m // 4]
                sl = cur_slot[blk]
                wv = wring[:, sl, :].rearrange("p (a k c) -> p a k c", a=4, k=8)
                a = nextA()
                for k in range(8):
                    P.op("pe", lambda e, a=a, wv=wv, m=m, k=k: e.matmul(psA[a][:, TS], lhsT=wv[:, m % 4, k, :],
                                                                        rhs=actT[:, k, TS], start=(k == 0),
                                                                        stop=(k == 7)),
                         reads=[ring_b[sl]] + actT_b[0:nb], writes=[psA_b[a]], signal=(k == 7))
                if m % 4 == 3:
                    consumed(T, blk)
                c = m % 4
                if m < 4:
                    dst = u_view(c)[:, :, 3:3 + L]
                    src = psA[a][:, TS].rearrange("p (s l) -> p s l", s=nseg)
                    P.op("act", lambda e, dst=dst, src=src: e.activation(out=dst, in_=src, func=AF.Copy),
                         reads=[psA_b[a]], writes=[upre_b[c]])
                elif m < 8:
                    P.op("act", lambda e, a=a, c=c: e.activation(out=gbuf[:, c, TS], in_=psA[a][:, TS], func=AF.Copy),
                         reads=[psA_b[a]], writes=[gbuf_b[c]])
                elif m < 12:
                    for e2 in range(2):
                        P.op("act", lambda e, a=a, c=c, e2=e2: e.activation(
                            out=qT[e2 * 64:(e2 + 1) * 64, 2 * c + e2, TS], in_=psA[a][e2 * 64:(e2 + 1) * 64, TS],
                            func=AF.Copy), reads=[psA_b[a]], writes=[qT_b[c]])
                else:
                    P.op("act", lambda e, a=a, c=c: e.activation(out=kT[:, c, woff:woff + NTOK], in_=psA[a][:, TS],
                                                                 func=AF.Copy),
                         reads=[psA_b[a]], writes=[kT_b[c][T["whalf"]]])
            if dbg is not None and dbg.endswith("_in1"):
                P.mute = True
            for which in (["v", "k"] if want_kv else ["v"]):
                blk = B_INV if which == "v" else B_INK
                sl = cur_slot[blk]
                wv = wring[:, sl, :].rearrange("p (k c) -> p k c", k=8)
                for b in range(nb):
                    a = nextA()
                    for k in range(8):
                        P.op("pe", lambda e, a=a, wv=wv, b=b, k=k: e.matmul(
                            psA[a][0:nr, :], lhsT=actT[:, k, b * 128:b * 128 + nr], rhs=wv[:, k, :],
                            start=(k == 0), stop=(k == 7)),
                            reads=[ring_b[sl], actT_b[b]], writes=[psA_b[a]], signal=(k == 7))
                    if which == "v" and not (dbg is not None and "novext" in dbg):
                        vb = T["vblk"][b]
                        if True:
                            P.op("act", lambda e, a=a, vb=vb: e.activation(
                                out=vext[0:nr, vb, :, 0:64], in_=psA[a][0:nr, :].rearrange("p (h d) -> p h d", h=8),
                                func=AF.Copy), reads=[psA_b[a]], writes=[vext_b[vb]])
                        else:
                            P.op("dve", lambda e, a=a, vb=vb: e.tensor_copy(
                                out=vext[0:nr, vb, :, 0:64], in_=psA[a][0:nr, :].rearrange("p (h d) -> p h d", h=8)),
                                reads=[psA_b[a]], writes=[vext_b[vb]])
                    if want_kv:
                        sq = (b * 2 + (0 if which == "v" else 1)) % 4
                        P.op("act", lambda e, a=a, sq=sq: e.activation(out=scrF[0:nr, sq, :], in_=psA[a][0:nr, :],
                                                                      func=AF.Copy),
                             reads=[psA_b[a]], writes=[scrF_b[sq]])
                        dst = T["kvdst"][which][b]
                        if not (dbg is not None and "nokvst" in dbg):
                            P.dma("pool", ch_kv[sq], dst, scrF[0:nr, sq, :], reads=[scrF_b[sq]])
                consumed(T, blk)
            if dbg is not None and dbg.endswith("_in"):
                P.mute = True
            if nxt is not None:
                for b in range(nxt["nb"]):
                    q = nxt["xinq"][b]
                    r0 = nxt["row0"] + b * 128
                    P.dma("sp", ch_xin[q], xin[0:nxt["nr"], q, :], nxt["xsrc"][r0:r0 + nxt["nr"], :],
                          writes=[xin_b[q]])
            def finish_group(yq, hg, po_, nrq):
                P.op("dve", lambda e: e.reciprocal(out=st[0:nrq, RS + hg * 4:RS + hg * 4 + 4],
                                                   in_=psO[po_][0:nrq, 0:260].rearrange("p (h d) -> p h d", h=4)[:, :, 64]),
                     reads=[psO_b[po_]], writes=[stb("rs%d" % hg)])
                P.op("dve", lambda e: e.tensor_tensor(
                    out=ya[0:nrq, yq, hg * 256:(hg + 1) * 256].rearrange("p (h d) -> p h d", h=4),
                    in0=psO[po_][0:nrq, 0:260].rearrange("p (h d) -> p h d", h=4)[:, :, 0:64],
                    in1=st[0:nrq, RS + hg * 4:RS + hg * 4 + 4].unsqueeze(2).to_broadcast([nrq, 4, 64]), op=ALU.mult),
                    reads=[psO_b[po_], stb("rs%d" % hg)], writes=[ya_b[yq]])

            def finish_stats(blocks, nrq):
                for b in blocks:
                    bq = b % 2
                    P.op("act", lambda e, b=b, bq=bq: e.activation(out=yab[0:nrq, bq, :], in_=ya[0:nrq, b, :],
                                                                   func=AF.Square,
                                                                   accum_out=st[0:nrq, SSA + b:SSA + b + 1]),
                         reads=[ya_b[b]], writes=[yab_b[bq], stb("ssa%d" % b)])
                nbk = len(blocks)
                b0 = blocks[0]
                P.op("pool", lambda e: e.tensor_scalar(out=st[0:nrq, TMP:TMP + nbk], in0=st[0:nrq, SSA + b0:SSA + b0 + nbk],
                                                       scalar1=1.0 / LW, scalar2=EPS, op0=ALU.mult, op1=ALU.add),
                     reads=[stb("ssa%d" % b) for b in blocks], writes=[stb("tmp")])
                P.op("pool", lambda e: e.tensor_tensor(out=st[0:nrq, SCA + b0:SCA + b0 + nbk], in0=st[0:nrq, TMP:TMP + nbk],
                                                       in1=EM05[0:nrq, :].to_broadcast([nrq, nbk]), op=ALU.pow),
                     reads=[stb("tmp"), b_cst], writes=[stb("sca%d" % b) for b in blocks])
                P.op("pool", lambda e: e.tensor_tensor(out=st[0:nrq, SCA + b0:SCA + b0 + nbk],
                                                       in0=st[0:nrq, SCA + b0:SCA + b0 + nbk],
                                                       in1=st[0:nrq, 56 + b0:56 + b0 + nbk], op=ALU.mult),
                     reads=[stb("sql")], writes=[stb("sca%d" % b) for b in blocks])

            def finish_apply(blocks, nrq):
                for b in blocks:
                    bq = b % 2
                    P.op("dve", lambda e, b=b, bq=bq: e.tensor_scalar(
                        out=yab[0:nrq, bq, :], in0=ya[0:nrq, b, :], scalar1=st[0:nrq, SCA + b:SCA + b + 1], scalar2=None,
                        op0=ALU.mult), reads=[ya_b[b], stb("sca%d" % b)], writes=[yab_b[bq]])
                    a = nextA()
                    pv = psA[a][:].bitcast(BF16).rearrange("p (k c) -> p k c", k=8)
                    for hp in range(4):
                        P.op("pe", lambda e, hp=hp, pv=pv, bq=bq: e.transpose(
                            out=pv[:, hp, 0:nrq], in_=yab[0:nrq, bq, hp * 128:(hp + 1) * 128],
                            identity=ident[0:nrq, 0:nrq]),
                            reads=[yab_b[bq], b_const], writes=[psA_b[a]], signal=(hp == 3))
                    P.op("act" if b % 2 == 0 else "dve", (lambda e, b=b, pv=pv: e.activation(
                        out=actT[:, 4:8, b * 128:b * 128 + nrq], in_=pv[:, 0:4, 0:nrq], func=AF.Copy)) if b % 2 == 0 else
                        (lambda e, b=b, pv=pv: e.tensor_copy(out=actT[:, 4:8, b * 128:b * 128 + nrq], in_=pv[:, 0:4, 0:nrq])),
                        reads=[psA_b[a]], writes=[actT_b[b]])

            if kind == "p":
                units = []
                for m in range(4):
                    for hg in range(2):
                        kbs = [kb for kb in range(5) if 4 * t_idx + m - 4 + kb >= 0]
                        for kb in kbs:
                            units.append((m, hg, kb, kb == kbs[0], kb == kbs[-1]))

                def qk(u, i):
                    m, hg, kb, fst, lst = u
                    sS = i % 2
                    ablk = (4 * t_idx + m - 4 + kb) % 8
                    kcol = ablk * 128
                    khalf = ablk // 4
                    biased = kb in (0, 3, 4)
                    kbi = {0: 0, 3: 1, 4: 2}.get(kb, 0)
                    sv = psS[sS][:].rearrange("p (j q) -> p j q", j=4)
                    for j in range(4):
                        h = hg * 4 + j
                        hp, po = h // 2, (h % 2) * 64
                        P.op("pe", lambda e, j=j, hp=hp, h=h: e.matmul(
                            sv[:, j, :], lhsT=kT[:, hp, kcol:kcol + 128],
                            rhs=qT[:, h, m * 128:(m + 1) * 128], start=(j == 0),
                            stop=(not biased and j == 3), skip_group_check=True),
                            reads=[kT_b[hp][khalf], qT_b[hp]], writes=[psS_b[sS]],
                            signal=(not biased and j == 3))
                    if biased:
                        for j in range(4):
                            h = hg * 4 + j
                            P.op("pe", lambda e, j=j, h=h: e.matmul(sv[:, j, :], lhsT=ident[:], rhs=biasT[:, kbi, h, :],
                                                                    start=False, stop=(j == 3), skip_group_check=True),
                                 reads=[b_bias, b_const], writes=[psS_b[sS]], signal=(j == 3))
                    sP = i % 4
                    P.op("act", lambda e: e.activation(out=pT[:, sP, :], in_=psS[sS][:], func=AF.Exp, scale=0.125),
                         reads=[psS_b[sS]], writes=[pT_b[sP]])

                def pv_(u, i):
                    m, hg, kb, fst, lst = u
                    sP = i % 4
                    ablk = (4 * t_idx + m - 4 + kb) % 8
                    ov = psO[hg][:, 0:260].rearrange("p (h d) -> p h d", h=4)
                    for j in range(4):
                        h = hg * 4 + j
                        P.op("pe", lambda e, j=j, h=h: e.matmul(
                            ov[:, j, :], lhsT=pT[:, sP, j * 128:(j + 1) * 128], rhs=vext[:, ablk, h, 0:65],
                            start=(fst and j == 0), stop=(lst and j == 3), skip_group_check=True),
                            reads=[pT_b[sP], vext_b[ablk]], writes=[psO_b[hg]], signal=(j == 3))
                    if lst:
                        finish_group(m, hg, hg, 128)
                for pr in range(2):
                    ms = (2 * pr, 2 * pr + 1)
                    if pr == 1:
                        for m in ms:
                            lru_front(m)
                        for m in ms:
                            lru_chainA(m)
                        for m in ms:
                            lru_sqrt(m)
                    for m in ms:
                        lru_chainB(m)
                        um = [(i, u) for i, u in enumerate(units) if u[0] == m]
                        n = len(um)
                        hooks = {}
                        if m == 3:
                            def h_a():
                                lru_ssl(3)
                                lru_finish()

                            def h_b():
                                finish_stats([0, 1, 2], 128)
                            hooks = {min(3, n - 1): h_a, min(5, n - 1) + (1 if n - 1 <= 3 else 0): h_b}
                        for x in range(min(2, n)):
                            qk(um[x][1], um[x][0])
                        for x in range(n):
                            pv_(um[x][1], um[x][0])
                            if x + 2 < n:
                                qk(um[x + 2][1], um[x + 2][0])
                            if x in hooks:
                                hooks.pop(x)()
                        for x in sorted(hooks):
                            hooks[x]()
                        if m < 3:
                            lru_ssl(m)
                finish_stats([3], 128)
                finish_apply([0, 1, 2, 3], 128)
            else:
                cstK = xin[:, 0:2, :].rearrange("p a (b c) -> p (a b) c", b=2)
                cstV = xin[:, 2:4, :].rearrange("p a (b c) -> p (a b) c", b=2)
                ui = [0]
                fstO = [True, True]

                def load_cache(sq_):
                    P.dma("sp", ch_ck, cstK, cache_k[sq_].rearrange("(cb p) f -> p cb f", p=128), writes=xin_b[0:2])
                    P.dma("sp", ch_cv, cstV, cache_v[sq_].rearrange("(cb p) f -> p cb f", p=128), writes=xin_b[2:4])
                    for half in range(2):
                        qx = T["xq"][0] % 2
                        T["xq"][0] += 1
                        xv = xsb[:, qx, :].rearrange("p (cb f) -> p cb f", cb=2)
                        P.op("dve", lambda e, xv=xv, half=half: e.tensor_copy(out=xv, in_=cstK[:, 2 * half:2 * half + 2, :]),
                             reads=xin_b[0:2], writes=[xsb_b[qx]])
                        a = nextA()
                        pv = psA[a][:].bitcast(BF16).rearrange("p (k c) -> p k c", k=8)
                        for cbl in range(2):
                            for hp in range(4):
                                P.op("pe", lambda e, cbl=cbl, hp=hp, xv=xv, pv=pv: e.transpose(
                                    out=pv[:, cbl * 4 + hp, :], in_=xv[:, cbl, hp * 128:(hp + 1) * 128], identity=ident[:]),
                                    reads=[xsb_b[qx], b_const], writes=[psA_b[a]], signal=(cbl == 1 and hp == 3))
                        for cbl in range(2):
                            cb = 2 * half + cbl
                            P.op("act", lambda e, cb=cb, cbl=cbl, pv=pv: e.activation(
                                out=kT[:, :, cb * 128:(cb + 1) * 128], in_=pv[:, cbl * 4:cbl * 4 + 4, :], func=AF.Copy),
                                reads=[psA_b[a]], writes=[kT_b[hp][0] for hp in range(4)])
                    for cb in range(4):
                        P.op("act", lambda e, cb=cb: e.activation(
                            out=vext[:, cb, :, 0:64], in_=cstV[:, cb, :].rearrange("p (h d) -> p h d", h=8),
                            func=AF.Copy), reads=xin_b[2:4], writes=[vext_b[cb]])

                def cache_unit(sq_, hg, cb):
                    i = ui[0]
                    ui[0] += 1
                    sS, sP = i % 2, i % 4
                    sv = psS[sS][:].rearrange("p (j q) -> p j q", j=4)
                    biased = cb == 3
                    for j in range(4):
                        h = hg * 4 + j
                        hp, po = h // 2, (h % 2) * 64
                        P.op("pe", lambda e, j=j, hp=hp, h=h: e.matmul(
                            sv[:, j, 0:16], lhsT=kT[:, hp, cb * 128:(cb + 1) * 128],
                            rhs=qT[:, h, sq_ * 16:sq_ * 16 + 16], start=(j == 0),
                            stop=(not biased and j == 3), skip_group_check=True),
                            reads=[kT_b[hp][0], qT_b[hp]], writes=[psS_b[sS]], signal=(not biased and j == 3))
                    if biased:
                        for j in range(4):
                            h = hg * 4 + j
                            P.op("pe", lambda e, j=j, h=h: e.matmul(
                                sv[:, j, 0:16], lhsT=ident[:], rhs=biasT[:, 1, h, 0:16], start=False,
                                stop=(j == 3), skip_group_check=True),
                                reads=[b_bias, b_const], writes=[psS_b[sS]], signal=(j == 3))
                    pview = pT[:, sP, 0:256].rearrange("p (j q) -> p j q", j=4)
                    P.op("pool", lambda e: e.memset(pT[:, sP, 0:256], 0.0), writes=[pT_b[sP]])
                    P.op("act", lambda e: e.activation(out=pview[:, :, sq_ * 16:sq_ * 16 + 16], in_=sv[:, :, 0:16],
                                                       func=AF.Exp, scale=0.125),
                         reads=[psS_b[sS]], writes=[pT_b[sP]])
                    ov = psO[hg][:, 0:260].rearrange("p (h d) -> p h d", h=4)
                    f0 = fstO[hg]
                    fstO[hg] = False
                    for j in range(4):
                        h = hg * 4 + j
                        P.op("pe", lambda e, j=j, h=h: e.matmul(
                            ov[0:64, j, :], lhsT=pview[:, j, :], rhs=vext[:, cb, h, 0:65],
                            start=(f0 and j == 0), stop=False, skip_group_check=True),
                            reads=[pT_b[sP], vext_b[cb]], writes=[psO_b[hg]], signal=(j == 3))

                def new_unit(hg):
                    i = ui[0]
                    ui[0] += 1
                    sS, sP = i % 2, i % 4
                    sv = psS[sS][:].rearrange("p (j q) -> p j q", j=4)
                    for j in range(4):
                        h = hg * 4 + j
                        hp, po = h // 2, (h % 2) * 64
                        P.op("pe", lambda e, j=j, hp=hp, h=h: e.matmul(
                            sv[0:64, j, 0:64], lhsT=kT[:, hp, 512:576], rhs=qT[:, h, 0:64],
                            start=(j == 0), stop=False, skip_group_check=True),
                            reads=[kT_b[hp][1], qT_b[hp]], writes=[psS_b[sS]], signal=False)
                    for j in range(4):
                        h = hg * 4 + j
                        P.op("pe", lambda e, j=j, h=h: e.matmul(
                            sv[0:64, j, 0:64], lhsT=ident[0:64, 0:64], rhs=biasN[:, h, :], start=False, stop=(j == 3),
                            skip_group_check=True), reads=[b_bias, b_const], writes=[psS_b[sS]], signal=(j == 3))
                    pview = pT[:, sP, 0:256].rearrange("p (j q) -> p j q", j=4)
                    P.op("act", lambda e: e.activation(out=pview[0:64, :, :], in_=sv[0:64, :, 0:64], func=AF.Exp,
                                                       scale=0.125), reads=[psS_b[sS]], writes=[pT_b[sP]])
                    ov = psO[hg][:, 0:260].rearrange("p (h d) -> p h d", h=4)
                    for j in range(4):
                        h = hg * 4 + j
                        P.op("pe", lambda e, j=j, h=h: e.matmul(
                            ov[0:64, j, :], lhsT=pview[0:64, j, :], rhs=vext[0:64, 4, h, 0:65], start=False,
                            stop=(j == 3), skip_group_check=True),
                            reads=[pT_b[sP], vext_b[4]], writes=[psO_b[hg]], signal=(j == 3))
                    finish_group(0, hg, hg, 64)

                for pr in range(2):
                    cs = (2 * pr, 2 * pr + 1)
                    if pr == 1:
                        for c in cs:
                            lru_front(c)
                        for c in cs:
                            lru_chainA(c)
                        for c in cs:
                            lru_sqrt(c)
                    for c in cs:
                        lru_chainB(c)
                        lru_ssl(c)
                lru_finish()
                for sq_ in range(NS):
                    load_cache(sq_)
                    for hg in range(2):
                        for cb in range(4):
                            cache_unit(sq_, hg, cb)
                for hg in range(2):
                    new_unit(hg)
                finish_stats([0], 64)
                finish_apply([0], 64)

            if dbg is not None and dbg.endswith("_att"):
                P.mute = True
            for b in range(nb):
                a0, a1 = nextA(), nextA()
                for k in range(8):
                    sl = cur_slot[B_OUT[k // 4]]
                    wv = wring[:, sl, :].rearrange("p (k c) -> p k c", k=4)
                    for half, a in enumerate((a0, a1)):
                        P.op("pe", lambda e, a=a, k=k, wv=wv, b=b, half=half: e.matmul(
                            psA[a][0:nr, :], lhsT=actT[:, k, b * 128:b * 128 + nr],
                            rhs=wv[:, k % 4, half * 512:(half + 1) * 512], start=(k == 0), stop=(k == 7)),
                            reads=[ring_b[sl], actT_b[b]], writes=[psA_b[a]], signal=(k == 7))
                for half, a in enumerate((a0, a1)):
                    P.op("dve", lambda e, a=a, b=b, half=half: e.scalar_tensor_tensor(
                        out=X[0:nr, b, half * 512:(half + 1) * 512], in0=psA[a][0:nr, :],
                        scalar=st[0:nr, RL + b:RL + b + 1], in1=X[0:nr, b, half * 512:(half + 1) * 512],
                        op0=ALU.mult, op1=ALU.add), reads=[psA_b[a], stb("rl")], writes=[X_b[b]])
                norm_transpose(T, b, X[0:nr, b, :], X_b[b], SS2, R2, "n2", phase="stats")
            for blk in B_OUT:
                consumed(T, blk)
            if dbg is not None and dbg.endswith("_out"):
                P.mute = True
            for b in range(nb):
                norm_transpose(T, b, X[0:nr, b, :], X_b[b], SS2, R2, "n2", phase="apply")
            for f in range(NF):
                i = f // 2
                sl = cur_slot[B_GU[i]]
                wv = wring[:, sl, :].rearrange("p (a k c) -> p a k c", a=4, k=8)
                ag, au = nextA(), nextA()
                for gu, a in enumerate((ag, au)):
                    for k in range(8):
                        P.op("pe", lambda e, a=a, k=k, wv=wv, gu=gu, f=f: e.matmul(
                            psA[a][:, TS], lhsT=wv[:, (f % 2) * 2 + gu, k, :], rhs=actT[:, k, TS], start=(k == 0),
                            stop=(k == 7)), reads=[ring_b[sl]] + actT_b[0:nb], writes=[psA_b[a]], signal=(k == 7))
                tq = f % 2
                P.op("act", lambda e, ag=ag, tq=tq: e.activation(out=scrF[:, tq, TS], in_=psA[ag][:, TS], func=AF.Tanh,
                                                                 scale=0.5), reads=[psA_b[ag]], writes=[scrF_b[tq]])
                P.op("dve", lambda e, ag=ag, tq=tq: e.scalar_tensor_tensor(
                    out=scrF[:, 2 + tq, TS], in0=scrF[:, tq, TS], scalar=1.0, in1=psA[ag][:, TS], op0=ALU.add,
                    op1=ALU.mult), reads=[scrF_b[tq], psA_b[ag]], writes=[scrF_b[2 + tq]])
                als = [hT_b[f]]
                P.op("dve", lambda e, au=au, tq=tq, f=f: e.scalar_tensor_tensor(
                    out=hTv(f)[:, TS], in0=scrF[:, 2 + tq, TS], scalar=0.5, in1=psA[au][:, TS], op0=ALU.mult,
                    op1=ALU.mult), reads=[scrF_b[2 + tq], psA_b[au]], writes=als)
                if f % 2 == 1:
                    consumed(T, B_GU[i])
            if dbg is not None and dbg.endswith("_ffn"):
                P.mute = True
            for b in range(nb):
                a0, a1 = nextA(), nextA()
                for kf in range(NF):
                    sl = cur_slot[B_DN[kf // 4]]
                    wv = wring[:, sl, :].rearrange("p (k c) -> p k c", k=4)
                    for half, a in enumerate((a0, a1)):
                        P.op("pe", lambda e, a=a, kf=kf, wv=wv, b=b, half=half: e.matmul(
                            psA[a][0:nr, :], lhsT=hTv(kf)[:, b * 128:b * 128 + nr],
                            rhs=wv[:, kf % 4, half * 512:(half + 1) * 512], start=(kf == 0), stop=(kf == NF - 1)),
                            reads=[ring_b[sl], hT_b[kf]], writes=[psA_b[a]], signal=(kf == NF - 1))
                    if b == nb - 1 and (kf % 4 == 3 or kf == NF - 1):
                        consumed(T, B_DN[kf // 4])
                if nxt is not None and b < nxt["nb"] and OVERLAP_N1:
                    qn = nxt["xinq"][b]
                    norm_transpose(nxt, b, xin[0:nxt["nr"], qn, :], xin_b[qn], SS1, R1, "n1")
                    nxt["n1_done"] = True
                for half, a in enumerate((a0, a1)):
                    P.op("dve", lambda e, a=a, b=b, half=half: e.tensor_tensor(
                        out=X[0:nr, b, half * 512:(half + 1) * 512], in0=psA[a][0:nr, :],
                        in1=X[0:nr, b, half * 512:(half + 1) * 512], op=ALU.add), reads=[psA_b[a]], writes=[X_b[b]])
                P.op("act", lambda e, b=b: e.activation(out=junk[0:nr, :], in_=X[0:nr, b, :], func=AF.Square,
                                                        accum_out=st[0:nr, SS3 + b:SS3 + b + 1]),
                     reads=[X_b[b]], writes=[b_junk, stb("n3ss%d" % b)])
                pool_rsqrt(R3 + b, SS3 + b, 1, 1.0 / D, nr, [stb("n3ss%d" % b)], [stb("n3r%d" % b)])
                P.op("dve", lambda e, b=b: e.scalar_tensor_tensor(
                    out=X[0:nr, b, :], in0=X[0:nr, b, :], scalar=st[0:nr, R3 + b:R3 + b + 1], in1=gfin[0:nr, :],
                    op0=ALU.mult, op1=ALU.mult), reads=[stb("n3r%d" % b), b_const], writes=[X_b[b]])
                r0 = row0 + b * 128
                P.dma("pool", ch_y[b], ydst[r0:r0 + nr, :], X[0:nr, b, :], reads=[X_b[b]])

        tiles = []
        xq = [0]
        for s in range(NSEQ):
            for t in range(NT):
                half = t % 2
                tiles.append(dict(
                    kind="p", s=s, t=t, ntok=512, nb=4, nr=128, nseg=1, L=512, first=(t == 0), last=(t == NT - 1),
                    row0=s * SEQ + t * 512, xsrc=xp, ydst=y_prompt, woff=half * 512, whalf=half,
                    vblk=[(4 * t + b) % 8 for b in range(4)], xinq=[0, 1, 2, 3], xq=xq, y2={}, yaq=0, gates={},
                    kvdst={"k": [prompt_k[s][b * 128:(b + 1) * 128, :] for b in range(4)],
                           "v": [prompt_v[s][b * 128:(b + 1) * 128, :] for b in range(4)]},
                    convdst=[prompt_conv[s]], lrudst=[prompt_lru[s]]))
        nr_s = NS * DSEQ
        tiles.append(dict(
            kind="s", s=0, t=0, ntok=nr_s, nb=1, nr=nr_s, nseg=NS, L=DSEQ, first=True, last=True, row0=0, xsrc=xs,
            ydst=y_sample, woff=512, whalf=1, vblk=[4], xinq=[0], xq=xq, y2={}, yaq=0, gates={},
            kvdst={"k": [sample_k[:, :]], "v": [sample_v[:, :]]},
            convdst=[sample_conv[sg] for sg in range(NS)], lrudst=[sample_lru[sg] for sg in range(NS)]))
        for i, T in enumerate(tiles):
            T["next"] = tiles[i + 1] if i + 1 < len(tiles) else None
            seq = B_IN + [B_INV] + ([B_INK] if T["last"] else []) + B_OUT + B_GU + B_DN
            T["lpos"] = {blk: len(load_seq) + j for j, blk in enumerate(seq)}
            load_seq.extend(seq)
        if not (dbg is not None and dbg.startswith("setup")):
            emit_loads_upto(NSLOT)
        T0 = tiles[0]
        for b in range(T0["nb"] if not (dbg is not None and dbg.startswith("setup")) else 0):
            q = T0["xinq"][b]
            r0 = T0["row0"] + b * 128
            P.dma("sp", ch_xin[q], xin[0:T0["nr"], q, :], T0["xsrc"][r0:r0 + T0["nr"], :], writes=[xin_b[q]])
        if dbg is not None and dbg.startswith("setup"):
            tiles = []
        if dbg is not None and dbg.startswith("p1"):
            tiles = tiles[:1]
        for T in tiles:
            if T["kind"] == "s":
                for sg in range(NS):
                    for c in range(4):
                        P.dma("sp", ch_state, upre[:, c, sg * 19:sg * 19 + 3],
                              state_conv[sg][:, c * 128:(c + 1) * 128].rearrange("j p -> p j"),
                              writes=[upre_b[c]], allow_slow_non_contiguous=True)
                for c in range(4):
                    P.dma("sp", ch_state, hprev[:, c, 0:NS], state_lru[:, c * 128:(c + 1) * 128].rearrange("s p -> p s"),
                          writes=[hprev_b[c]], allow_slow_non_contiguous=True)
                P.seal(ch_state, upre_b + hprev_b)
            run_tile(T)
        P.mute = False
        for e in ("pool",):
            P.wait_all(e, store_chans + ch_ring + ch_xin + [ch_X, ch_state, ch_ck, ch_cv])
        P.flush(block)
    return nc


_CACHE = {}


def kernel(**inputs):
    NC = 8
    B, SEQ, _ = inputs["x_prompt"].shape
    DB, DSEQ, _ = inputs["x_sample"].shape
    NSEQ = B // NC
    NS = DB // NC
    key = (NSEQ, SEQ, NS, DSEQ)
    if key not in _CACHE:
        _CACHE[key] = build(NSEQ, SEQ, NS, DSEQ)
    nc = _CACHE[key]
    f = lambda a: np.ascontiguousarray(a, dtype=np.float32)
    shared = {}
    for n in ["norm_mix", "w_in", "conv_w", "conv_b", "lru_wa", "lru_ba", "lru_wx", "lru_bx", "lru_lambda",
              "rel_bias", "norm_lru_out", "norm_attn_out", "w_out", "norm_ffn", "w_gate", "w_up", "w_down"]:
        shared[n] = f(inputs[n][0])
    shared["norm_final"] = f(inputs["norm_final"])
    in_maps = []
    for c in range(NC):
        m = dict(shared)
        m["x_prompt"] = f(inputs["x_prompt"][c * NSEQ:(c + 1) * NSEQ]).reshape(NSEQ * SEQ, D)
        m["x_sample"] = f(inputs["x_sample"][c * NS:(c + 1) * NS]).reshape(NS * DSEQ, D)
        m["state_conv"] = f(inputs["state_conv"][0, c * NS:(c + 1) * NS])
        m["state_lru"] = f(inputs["state_lru"][0, c * NS:(c + 1) * NS])
        m["cache_k"] = f(inputs["cache_k"][0, c * NS:(c + 1) * NS]).reshape(NS, 512, 512)
        m["cache_v"] = f(inputs["cache_v"][0, c * NS:(c + 1) * NS]).reshape(NS, 512, 512)
        in_maps.append(m)
    res = run_bass_kernel_spmd(nc, in_maps, core_ids=list(range(NC)))
    R = res.results
    cat = lambda n: np.concatenate([np.asarray(r[n], dtype=np.float32) for r in R], axis=0)
    y_prompt = cat("y_prompt").reshape(B, SEQ, D)
    y_sample = cat("y_sample").reshape(DB, DSEQ, D)
    keep = min(512, SEQ)
    outs = (
        y_prompt, y_sample,
        cat("prompt_conv").reshape(1, B, 3, LW), cat("prompt_lru").reshape(1, B, LW),
        cat("prompt_k").reshape(1, B, keep, 8, 64), cat("prompt_v").reshape(1, B, keep, 8, 64),
        cat("sample_conv").reshape(1, DB, 3, LW), cat("sample_lru").reshape(1, DB, LW),
        cat("sample_k").reshape(1, DB, DSEQ, 8, 64), cat("sample_v").reshape(1, DB, DSEQ, 8, 64),
    )
    return outs
```

```python
import numpy as np
from contextlib import ExitStack
import concourse.bass as bass
import concourse.mybir as mybir
from concourse.bass_utils import run_bass_kernel_spmd

F32 = mybir.dt.float32
BF16 = mybir.dt.bfloat16
AF = mybir.ActivationFunctionType
ALU = mybir.AluOpType

D = 1024
KC = 8
LW = 512
DFF = 2816
NF = 22
INC = 2560
EPS = 1e-6
NEG = -30000.0
NCD = dict(allow_slow_non_contiguous=True)
C0 = 0.7978845608028654
C1 = 0.7978845608028654 * 0.044715
B_IN = [0, 1, 2, 3]
B_INV = 4
B_INK = 5
B_OUT = [6, 7]
B_GU = list(range(8, 19))
B_DN = list(range(19, 25))
NBLK = 25
NSLOT = 6
OVERLAP_N1 = True


class Buf:
    def __init__(self, name=""):
        self.name = name
        self.w = {}
        self.r = {}


class Chan:
    def __init__(self, sem):
        self.sem = sem
        self.count = 0


class Prog:
    ENG = ("pe", "act", "dve", "pool", "sp")

    def __init__(self, nc, es):
        self.nc = nc
        self.streams = {e: [] for e in self.ENG}
        self.sem = {e: es.enter_context(nc.semaphore("sem_" + e)) for e in self.ENG if e != "sp"}
        self.cnt = {e: 0 for e in self.ENG}
        self.waited = {e: {} for e in self.ENG}
        self.semid = {}
        self.es = es
        self.nchan = 0

    def chan(self):
        self.nchan += 1
        return Chan(self.es.enter_context(self.nc.semaphore("ch%d" % self.nchan)))

    def _key(self, sem):
        k = id(sem)
        self.semid[k] = sem
        return k

    def _deps(self, eng, reads, writes, skip=None):
        deps = {}
        for b in reads:
            for k, v in b.w.items():
                deps[k] = max(deps.get(k, 0), v)
        for b in writes:
            for k, v in b.w.items():
                deps[k] = max(deps.get(k, 0), v)
            for k, v in b.r.items():
                deps[k] = max(deps.get(k, 0), v)
        pek = self._key(self.sem["pe"])
        for k, v in deps.items():
            if eng == "pe" and k == pek:
                continue
            if skip is not None and k == skip[0] and v < skip[1]:
                continue
            if self.waited[eng].get(k, 0) >= v:
                continue
            self.waited[eng][k] = v
            sem = self.semid[k]
            self.streams[eng].append(lambda e, s=sem, vv=v: e.wait_ge(s, vv))

    def _mark(self, k, v, reads, writes):
        for b in reads:
            b.r[k] = max(b.r.get(k, 0), v)
        for b in writes:
            b.w = {k: v}
            b.r = {}

    mute = False

    def op(self, eng, fn, reads=(), writes=(), signal=True):
        if self.mute:
            return
        reads = [b for b in reads if b is not None]
        writes = [b for b in writes if b is not None]
        self._deps(eng, reads, writes)
        sem = self.sem[eng]
        k = self._key(sem)
        if signal:
            self.cnt[eng] += 1
            v = self.cnt[eng]
            self.streams[eng].append(lambda e, f=fn, s=sem: f(e).then_inc(s, 1))
        else:
            v = self.cnt[eng] + 1
            self.streams[eng].append(lambda e, f=fn: f(e))
        self._mark(k, v, reads, writes)

    def dma(self, q, ch, out, in_, reads=(), writes=(), **kw):
        if self.mute:
            return
        reads = [b for b in reads if b is not None]
        writes = [b for b in writes if b is not None]
        self._deps(q, reads, writes, skip=(self._key(ch.sem), ch.count))
        ch.count += 16
        sem = ch.sem
        self.streams[q].append(
            lambda e, o=out, i=in_, s=sem, kw=kw: e.dma_start(out=o, in_=i, **kw).then_inc(s, 16))
        self._mark(self._key(sem), ch.count, reads, writes)

    def seal(self, ch, bufs):
        k = self._key(ch.sem)
        for b in bufs:
            if k in b.w:
                b.w[k] = ch.count

    def wait_all(self, eng, chans):
        for ch in chans:
            if ch.count:
                k = self._key(ch.sem)
                if self.waited[eng].get(k, 0) < ch.count:
                    self.waited[eng][k] = ch.count
                    self.streams[eng].append(lambda e, s=ch.sem, v=ch.count: e.wait_ge(s, v))

    def flush(self, block):
        m = {"sp": block.sync, "act": block.scalar, "dve": block.vector, "pool": block.gpsimd,
             "pe": block.tensor}
        for e in self.ENG:
            lst = self.streams[e]
            if not lst:
                continue
            self.streams[e] = []

            def body(eng, lst=lst):
                for f in lst:
                    f(eng)
            m[e](body)


def build(NSEQ, SEQ, NS, DSEQ=16, dbg=None):
    NT = SEQ // 512
    nc = bass.Bass("TRN2", target_bir_lowering=False)
    dt = lambda n, s, k="ExternalInput", d=F32: nc.dram_tensor(n, s, d, kind=k).ap()
    xp = dt("x_prompt", [NSEQ * SEQ, D])
    xs = dt("x_sample", [NS * DSEQ, D])
    state_conv = dt("state_conv", [NS, 3, LW])
    state_lru = dt("state_lru", [NS, LW])
    cache_k = dt("cache_k", [NS, 512, 512])
    cache_v = dt("cache_v", [NS, 512, 512])
    norm_mix = dt("norm_mix", [D])
    w_in = dt("w_in", [D, INC])
    conv_w = dt("conv_w", [4, LW])
    conv_b = dt("conv_b", [LW])
    lru_wa = dt("lru_wa", [8, 64, 64])
    lru_ba = dt("lru_ba", [LW])
    lru_wx = dt("lru_wx", [8, 64, 64])
    lru_bx = dt("lru_bx", [LW])
    lru_lambda = dt("lru_lambda", [LW])
    rel_bias = dt("rel_bias", [8, 257])
    norm_lru_out = dt("norm_lru_out", [LW])
    norm_attn_out = dt("norm_attn_out", [LW])
    w_out = dt("w_out", [D, D])
    norm_ffn = dt("norm_ffn", [D])
    w_gate = dt("w_gate", [D, DFF])
    w_up = dt("w_up", [D, DFF])
    w_down = dt("w_down", [DFF, D])
    norm_final = dt("norm_final", [D])
    O = "ExternalOutput"
    y_prompt = dt("y_prompt", [NSEQ * SEQ, D], O)
    y_sample = dt("y_sample", [NS * DSEQ, D], O)
    prompt_conv = dt("prompt_conv", [NSEQ, 3, LW], O)
    prompt_lru = dt("prompt_lru", [NSEQ, LW], O)
    prompt_k = dt("prompt_k", [NSEQ, 512, 512], O)
    prompt_v = dt("prompt_v", [NSEQ, 512, 512], O)
    sample_conv = dt("sample_conv", [NS, 3, LW], O)
    sample_lru = dt("sample_lru", [NS, LW], O)
    sample_k = dt("sample_k", [NS * DSEQ, 512], O)
    sample_v = dt("sample_v", [NS * DSEQ, 512], O)
    wsc = dt("wsc", [NBLK, 128, 4096], "Internal", BF16)
    ext = dt("ext", [8, 640], "Internal")
    ext2 = dt("ext2", [8, 128, 640], "Internal")

    with ExitStack() as es:
        P = Prog(nc, es)
        sb = lambda n, s, d=F32: es.enter_context(nc.sbuf_tensor(n, s, d))
        wring = sb("wring", [128, NSLOT, 4096], BF16)
        X = sb("X", [128, 4, D])
        xin = sb("xin", [128, 4, D])
        actT = sb("actT", [128, 8, 512], BF16)
        upre = sb("upre", [128, 4, 515])
        gbuf = sb("gbuf", [128, 4, 512])
        qT = sb("qT", [128, 8, 512], BF16)
        kT = sb("kT", [128, 4, 1024], BF16)
        vext = sb("vext", [128, 8, 8, 66], BF16)
        big = sb("big", [128, NF * 512], BF16)
        ucb = sb("ucb", [128, 2, 512], BF16)
        pT = sb("pT", [128, 4, 512], BF16)
        ya = sb("ya", [128, 4, 512])
        yab = sb("yab", [128, 2, 512], BF16)
        xsb = sb("xsb", [128, 2, D], BF16)
        junk = sb("junk", [128, D], BF16)
        biasT = sb("biasT", [128, 3, 8, 128], BF16)
        biasN = sb("biasN", [64, 8, 64], BF16)
        scrF = sb("scrF", [128, 4, 512])
        gfin = sb("gfin", [128, D])
        gw = sb("gw", [128, 2, 4, 128], BF16)
        ident = sb("ident", [128, 128], BF16)
        ones = sb("ones", [128, 2], BF16)
        cst = sb("cst", [128, 64])
        st = sb("st", [128, 96])
        hprev = sb("hprev", [128, 4, 4])
        hprev_s = sb("hprev_s", [128, 4, 4])
        upre_s = sb("upre_s", [128, 4, 80])
        gains = sb("gains", [128, 3, 8])
        pp = lambda n: es.enter_context(nc.psum_tensor(n, [128, 512], F32))
        psA = [pp("psA%d" % i) for i in range(4)]
        psS = [pp("psS%d" % i) for i in range(2)]
        psO = [pp("psO%d" % i) for i in range(2)]

        CW = lambda j, c: cst[:, j * 4 + c: j * 4 + c + 1]
        CB = lambda c: cst[:, 16 + c: 17 + c]
        HBA = lambda c: cst[:, 20 + c: 21 + c]
        HBX = lambda c: cst[:, 24 + c: 25 + c]
        C8 = lambda c: cst[:, 28 + c: 29 + c]
        HC8 = lambda c: cst[:, 32 + c: 33 + c]
        LAM = cst[:, 36:40]
        EM05 = cst[:, 40:41]
        EP05 = cst[:, 41:42]
        CBH = cst[:, 48:56]
        b_cst = Buf("cst")
        b_gains = Buf("gains")

        ring_b = [Buf("ring%d" % i) for i in range(NSLOT)]
        X_b = [Buf() for _ in range(4)]
        xin_b = [Buf() for _ in range(4)]
        actT_b = [Buf() for _ in range(4)]
        upre_b = [Buf() for _ in range(4)]
        gbuf_b = [Buf() for _ in range(4)]
        qT_b = [Buf() for _ in range(4)]
        kT_b = [[Buf(), Buf()] for _ in range(4)]
        vext_b = [Buf() for _ in range(8)]
        hT_b = [Buf() for _ in range(NF)]
        ucb_b = [Buf(), Buf()]
        pT_b = [Buf() for _ in range(4)]
        ya_b = [Buf() for _ in range(4)]
        yab_b = [Buf(), Buf()]
        xsb_b = [Buf(), Buf()]
        b_junk = Buf()
        scrF_b = [Buf() for _ in range(4)]
        psA_b = [Buf() for _ in range(4)]
        psS_b = [Buf() for _ in range(2)]
        psO_b = [Buf() for _ in range(2)]
        wsc_b = [Buf() for _ in range(NBLK)]
        b_bias = Buf()
        b_const = Buf()
        hprev_b = [Buf() for _ in range(4)]
        hprev_sb = [Buf() for _ in range(4)]
        upre_sb = [Buf() for _ in range(4)]
        st_b = {}

        def stb(name):
            if name not in st_b:
                st_b[name] = Buf(name)
            return st_b[name]
        SS1, R1, SS2, R2, SS3, R3, SSA, SSL, RL, SCA, TMP, RS = (0, 4, 8, 12, 16, 20, 24, 28, 32, 36, 40, 48)

        def tmpv(q, k):
            i = q * 5 + k
            return big[:, i * 1024:(i + 1) * 1024].bitcast(F32)

        def tmpb(q, k):
            i = q * 5 + k
            return [hT_b[2 * i], hT_b[2 * i + 1]]

        def y2v(i):
            return big[:, (20 + i) * 512:(21 + i) * 512]

        def y2b(i):
            return [hT_b[20 + i]]
        hTv = lambda f: big[:, f * 512:(f + 1) * 512]

        rrA = [0]

        def nextA():
            i = rrA[0] % 4
            rrA[0] += 1
            return i
        ch_misc = P.chan()
        ch_ring = [P.chan() for _ in range(NSLOT)]
        ch_xin = [P.chan() for _ in range(4)]
        ch_X = P.chan()
        ch_y = [P.chan() for _ in range(4)]
        ch_kv = [P.chan() for _ in range(4)]
        ch_conv = [P.chan() for _ in range(4)]
        ch_lru = [P.chan() for _ in range(4)]
        ch_state = P.chan()
        ch_ck = P.chan()
        ch_cv = P.chan()
        ch_e = [P.chan() for _ in range(3)]
        store_chans = ch_y + ch_kv + ch_conv + ch_lru

        with ExitStack() as es1:
            sb1 = lambda n, s, d=F32: es1.enter_context(nc.sbuf_tensor(n, s, d))
            stage = wring[:, 0:4, :].rearrange("p (q a) c -> p q (a c)", q=2).bitcast(F32)
            obf = wring[:, 4:6, :]
            gst = ya[:, 0:2, :].rearrange("p a (j c) -> p a j c", j=4)
            identf = sb1("identf", [128, 128])
            E8 = sb1("E8", [8, 640])
            B34 = scrF[:].rearrange("p a c -> p (a c)").rearrange("p (i h c) -> p i h c", i=2, h=8)
            BNf = sb1("BNf", [64, 8, 64])
            stage_b = [Buf(), Buf()]
            obf_b = [Buf(), Buf()]
            b_gst, b_identf, b_E8, b_B34, b_BNf, b_ext, b_ext2 = (Buf() for _ in range(7))
            ch_stg = [P.chan(), P.chan()]
            ch_obf = [P.chan(), P.chan()]
            block = es1.enter_context(nc.Block())

            cstage = sb1("cstage", [64, 128])
            gstage = sb1("gstage", [24, 128])
            b_cstage, b_gstage = Buf(), Buf()
            P.op("pool", lambda e: e.memset(cstage[:], 0.0), writes=[b_cstage])
            P.dma("sp", ch_misc, gstage[0:8, :], norm_mix.rearrange("(k p) -> k p", p=128), writes=[b_gstage])
            P.dma("sp", ch_misc, gstage[8:16, :], norm_ffn.rearrange("(k p) -> k p", p=128), writes=[b_gstage])
            P.dma("sp", ch_misc, gstage[16:20, :], norm_lru_out.rearrange("(k p) -> k p", p=128), writes=[b_gstage])
            P.dma("sp", ch_misc, gstage[20:24, :], norm_attn_out.rearrange("(k p) -> k p", p=128), writes=[b_gstage])
            P.dma("sp", ch_misc, cstage[0:16, :], conv_w.rearrange("j (c p) -> (j c) p", p=128), writes=[b_cstage])
            for row, src in [(16, conv_b), (20, lru_ba), (24, lru_bx), (36, lru_lambda)]:
                P.dma("sp", ch_misc, cstage[row:row + 4, :], src.rearrange("(c p) -> c p", p=128), writes=[b_cstage])
            P.dma("sp", ch_misc, CBH, bass.AP(tensor=rel_bias.tensor, offset=256, ap=[[0, 128], [257, 8]]),
                  writes=[b_cst], **NCD)
            P.dma("sp", ch_misc, gfin[:], bass.AP(tensor=norm_final.tensor, offset=0, ap=[[0, 128], [1, D]]),
                  writes=[b_const])
            P.op("pool", lambda e: e.memset(gst[:], 0.0), writes=[b_gst])
            for g, src in enumerate([lru_wa, lru_wx]):
                for n in range(8):
                    j, hf = n // 2, n % 2
                    P.dma("sp", ch_misc, gst[hf * 64:(hf + 1) * 64, g, j, hf * 64:(hf + 1) * 64], src[n],
                          writes=[b_gst])
            P.dma("sp", ch_misc, E8[:, 0:257], rel_bias, writes=[b_E8])
            P.seal(ch_misc, [b_gains, b_cst, b_const, b_gst, b_E8, b_cstage, b_gstage])
            P.op("pool", lambda e: e.memset(identf[:], 0.0), writes=[b_identf])
            P.op("pool", lambda e: e.affine_select(out=identf[:], in_=identf[:], pattern=[[-1, 128]],
                                                   compare_op=ALU.not_equal, fill=1.0, base=0,
                                                   channel_multiplier=1), writes=[b_identf])
            P.op("dve", lambda e: e.tensor_copy(out=ident[:], in_=identf[:]), reads=[b_identf], writes=[b_const])
            P.op("pe", lambda e: e.transpose(out=psA[0][:, 0:64], in_=cstage[:, :], identity=identf[0:64, 0:64]),
                 reads=[b_cstage, b_identf], writes=[psA_b[0]])
            P.op("pe", lambda e: e.transpose(out=psA[1][:, 0:24], in_=gstage[:, :], identity=identf[0:24, 0:24]),
                 reads=[b_gstage, b_identf], writes=[psA_b[1]])
            P.op("act", lambda e: e.activation(out=cst[:, 0:40], in_=psA[0][:, 0:40], func=AF.Copy),
                 reads=[psA_b[0]], writes=[b_cst])
            P.op("act", lambda e: e.activation(out=gains[:].rearrange("p a k -> p (a k)"), in_=psA[1][:, 0:24],
                                               func=AF.Copy), reads=[psA_b[1]], writes=[b_gains])
            P.op("pool", lambda e: e.memset(cst[:, 40:41], -0.5), writes=[b_cst])
            P.op("pool", lambda e: e.memset(cst[:, 41:42], 0.5), writes=[b_cst])
            P.op("dve", lambda e: e.tensor_scalar(out=cst[:, 20:28], in0=cst[:, 20:28], scalar1=0.5, scalar2=None,
                                                  op0=ALU.mult), reads=[], writes=[b_cst])
            P.op("act", lambda e: e.activation(out=cst[:, 44:48], in_=LAM, func=AF.Exp, scale=-1.0), writes=[b_cst])
            P.op("act", lambda e: e.activation(out=cst[:, 44:48], in_=cst[:, 44:48], func=AF.Ln, bias=1.0),
                 writes=[b_cst])
            P.op("dve", lambda e: e.tensor_scalar(out=cst[:, 28:32], in0=cst[:, 44:48], scalar1=-8.0, scalar2=None,
                                                  op0=ALU.mult), writes=[b_cst])
            P.op("dve", lambda e: e.tensor_scalar(out=cst[:, 32:36], in0=cst[:, 44:48], scalar1=-4.0, scalar2=None,
                                                  op0=ALU.mult), writes=[b_cst])
            P.op("dve", lambda e: e.tensor_copy(out=gw[:], in_=gst[:]), reads=[b_gst], writes=[b_const])
            P.op("pool", lambda e: e.memset(ones[:], 1.0), writes=[b_const])
            for blk in range(8):
                P.op("pool", lambda e, blk=blk: e.memset(vext[:, blk, :, 64:65], 1.0), writes=[vext_b[blk]])
            P.op("pool", lambda e: e.memset(qT[:], 0.0), writes=qT_b)
            P.op("pool", lambda e: e.memset(hprev[:], 0.0), writes=hprev_b)
            P.op("pool", lambda e: e.memset(upre[:], 0.0), writes=upre_b)
            SKIPB = dbg is not None and "nobias" in dbg
            SKIPW = dbg is not None and "noweights" in dbg
            P.mute = SKIPB
            P.op("dve", lambda e: e.tensor_copy(out=E8[:, 257:640], in_=E8[:, 256:257].to_broadcast([8, 383])),
                 writes=[b_E8])
            P.dma("sp", ch_e[0], ext, E8[:], reads=[b_E8], writes=[b_ext])
            P.dma("sp", ch_e[1], ext2, bass.AP(tensor=ext.tensor, offset=0, ap=[[640, 8], [0, 128], [1, 640]]),
                  reads=[b_ext], writes=[b_ext2])
            for i, base in enumerate([256, 128]):
                P.dma("sp", ch_e[2], B34[:, i, :, :],
                      bass.AP(tensor=ext2.tensor, offset=base, ap=[[639, 128], [81920, 8], [1, 128]]),
                      reads=[b_ext2], writes=[b_B34])
            P.op("pool", lambda e: e.memset(BNf[:], NEG), writes=[b_BNf])
            for s in range(4):
                P.dma("sp", ch_e[2], BNf[16 * s:16 * s + 16, :, 16 * s:16 * s + 16],
                      bass.AP(tensor=ext2.tensor, offset=128, ap=[[639, 16], [81920, 8], [1, 16]]),
                      reads=[b_ext2], writes=[b_BNf])
            P.seal(ch_e[2], [b_B34, b_BNf])
            for i in range(2):
                P.op("dve", lambda e, i=i: e.tensor_tensor(
                    out=B34[:, i, :, :], in0=B34[:, i, :, :],
                    in1=CBH.unsqueeze(2).to_broadcast([128, 8, 128]), op=ALU.subtract),
                    reads=[b_cst], writes=[b_B34])
                P.op("dve", lambda e, i=i: e.tensor_scalar(out=biasT[:, 1 + i, :, :], in0=B34[:, i, :, :], scalar1=8.0,
                                                          scalar2=None, op0=ALU.mult), reads=[b_B34], writes=[b_bias])
            P.op("pool", lambda e: e.memset(biasT[:, 0, :, :], 0.0), writes=[b_bias])
            P.op("pool", lambda e: e.memset(biasT[0:64, 0, :, 64:128], NEG), writes=[b_bias])
            P.op("pool", lambda e: e.memset(biasT[64:128, 2, :, 0:64], NEG), writes=[b_bias])
            P.op("dve", lambda e: e.tensor_tensor(out=BNf[:], in0=BNf[:],
                                                  in1=CBH[0:64, :].unsqueeze(2).to_broadcast([64, 8, 64]),
                                                  op=ALU.subtract), reads=[b_cst], writes=[b_BNf])
            P.op("dve", lambda e: e.tensor_scalar(out=biasN[:], in0=BNf[:], scalar1=8.0, scalar2=None, op0=ALU.mult),
                 reads=[b_BNf], writes=[b_bias])

            P.mute = SKIPW
            w_in_r = w_in.rearrange("(k p) c -> p k c", p=128)
            w_out_r = w_out.rearrange("(k p) c -> p k c", p=128)
            w_g_r = w_gate.rearrange("(k p) c -> p k c", p=128)
            w_u_r = w_up.rearrange("(k p) c -> p k c", p=128)
            w_d_r = w_down.rearrange("(k p) c -> p k c", p=128)
            eng_rr = ["act", "dve", "act", "dve", "act"]
            cnt = [0]

            def conv_block(blk, loads, gi, views, plain=False, used=4096):
                q = blk % 2
                for dst, src in loads:
                    P.dma("sp", ch_stg[q], dst, src, writes=[stage_b[q]])
                if plain:
                    eng = eng_rr[cnt[0] % 5]
                    cnt[0] += 1
                    if eng == "act":
                        P.op(eng, lambda e: e.activation(out=obf[:, q, 0:used], in_=stage[:, q, 0:used], func=AF.Copy),
                             reads=[stage_b[q]], writes=[obf_b[q]])
                    else:
                        P.op(eng, lambda e: e.tensor_copy(out=obf[:, q, 0:used], in_=stage[:, q, 0:used]),
                             reads=[stage_b[q]], writes=[obf_b[q]])
                else:
                    for k, (iv, ov, gk) in enumerate(views):
                        eng = eng_rr[cnt[0] % 5]
                        cnt[0] += 1
                        gsc = gains[:, gi, gk:gk + 1]
                        if eng == "act":
                            P.op(eng, lambda e, iv=iv, ov=ov, gsc=gsc: e.activation(out=ov, in_=iv, func=AF.Copy,
                                                                                 scale=gsc),
                                 reads=[stage_b[q], b_gains], writes=[obf_b[q]])
                        else:
                            P.op(eng, lambda e, iv=iv, ov=ov, gsc=gsc: e.tensor_scalar(out=ov, in0=iv, scalar1=gsc,
                                                                                    scalar2=None, op0=ALU.mult),
                                 reads=[stage_b[q], b_gains], writes=[obf_b[q]])
                P.dma("pool", ch_obf[q], wsc[blk][:, 0:used], obf[:, q, 0:used], reads=[obf_b[q]], writes=[wsc_b[blk]])

            def v5(t, q, a, b):
                return t[:, q, :].rearrange("p (a b c) -> p a b c", a=a, b=b)
            for i in range(4):
                blk = B_IN[i]
                q = blk % 2
                sv = stage[:, q, :].rearrange("p (k a c) -> p k a c", k=8, a=4)
                ov = v5(obf, q, 4, 8)
                loads = [(stage[:, q, :].rearrange("p (k c) -> p k c", k=8), w_in_r[:, :, i * 512:(i + 1) * 512])]
                conv_block(blk, loads, 0, [(sv[:, k, :, :], ov[:, :, k, :], k) for k in range(8)])
            for blk, c0 in [(B_INV, 2048), (B_INK, 1536)]:
                q = blk % 2
                sv = stage[:, q, :].rearrange("p (k c) -> p k c", k=8)
                ov = obf[:, q, :].rearrange("p (k c) -> p k c", k=8)
                conv_block(blk, [(sv, w_in_r[:, :, c0:c0 + 512])], 0,
                           [(sv[:, k, :], ov[:, k, :], k) for k in range(8)])
            for j in range(2):
                blk = B_OUT[j]
                q = blk % 2
                sv = stage[:, q, :].rearrange("p (k c) -> p k c", k=4)
                ov = obf[:, q, :].rearrange("p (k c) -> p k c", k=4)
                conv_block(blk, [(sv, w_out_r[:, 4 * j:4 * j + 4, :])], 2,
                           [(sv[:, k, :], ov[:, k, :], 4 * j + k) for k in range(4)])
            for i in range(11):
                blk = B_GU[i]
                q = blk % 2
                sv = stage[:, q, :].rearrange("p (g k f c) -> p g k f c", g=2, k=8, f=2)
                ov = obf[:, q, :].rearrange("p (f g k c) -> p f g k c", f=2, g=2, k=8)
                sl = stage[:, q, :].rearrange("p (g k c) -> p g k c", g=2, k=8)
                loads = [(sl[:, 0, :, :], w_g_r[:, :, 2 * i * 128:(2 * i + 2) * 128]),
                         (sl[:, 1, :, :], w_u_r[:, :, 2 * i * 128:(2 * i + 2) * 128])]
                conv_block(blk, loads, 1, [(sv[:, :, k, :, :].rearrange("p g f c -> p f g c"), ov[:, :, :, k, :], k)
                                           for k in range(8)])
            for i in range(6):
                blk = B_DN[i]
                q = blk % 2
                nk = 4 if i < 5 else 2
                sv = stage[:, q, 0:nk * 1024].rearrange("p (k c) -> p k c", k=nk)
                conv_block(blk, [(sv, w_d_r[:, 4 * i:4 * i + nk, :])], None, None, plain=True, used=nk * 1024)
            P.mute = False
            for e in ("sp", "pool", "act", "dve", "pe"):
                P.wait_all(e, [ch_misc] + ch_e + ch_stg + ch_obf)
            P.flush(block)

        block = es.enter_context(nc.Block())
        load_seq = []
        next_load = [0]

        def emit_loads_upto(n):
            while next_load[0] < min(n, len(load_seq)):
                idx = next_load[0]
                blk = load_seq[idx]
                i = idx % NSLOT
                used = 2048 if blk == B_DN[5] else 4096
                P.dma("sp", ch_ring[i], wring[:, i, 0:used], wsc[blk][:, 0:used], reads=[wsc_b[blk]],
                      writes=[ring_b[i]])
                next_load[0] += 1

        def consumed(T, blk):
            emit_loads_upto(T["lpos"][blk] + NSLOT + 1)

        def pool_rsqrt(dst_col, src_col, n, scale, nrow=128, srcs=(), dsts=()):
            P.op("pool", lambda e: e.tensor_scalar(out=st[0:nrow, TMP:TMP + n], in0=st[0:nrow, src_col:src_col + n],
                                                   scalar1=scale, scalar2=EPS, op0=ALU.mult, op1=ALU.add),
                 reads=list(srcs), writes=[stb("tmp")])
            P.op("pool", lambda e: e.tensor_tensor(out=st[0:nrow, dst_col:dst_col + n], in0=st[0:nrow, TMP:TMP + n],
                                                   in1=EM05[0:nrow, :].to_broadcast([nrow, n]), op=ALU.pow),
                 reads=[stb("tmp"), b_cst], writes=list(dsts))

        def norm_transpose(T, b, src_ap, src_b, ss_col, r_col, tag, phase="both"):
            nr = T["nr"]
            if phase in ("both", "stats"):
                P.op("act", lambda e: e.activation(out=junk[0:nr, :], in_=src_ap, func=AF.Square,
                                                   accum_out=st[0:nr, ss_col + b:ss_col + b + 1]),
                     reads=[src_b], writes=[b_junk, stb(tag + "ss%d" % b)])
                pool_rsqrt(r_col + b, ss_col + b, 1, 1.0 / D, nr, [stb(tag + "ss%d" % b)], [stb(tag + "r%d" % b)])
            if phase == "stats":
                return
            q = T["xq"][0] % 2
            T["xq"][0] += 1
            P.op("dve", lambda e: e.tensor_scalar(out=xsb[0:nr, q, :], in0=src_ap, scalar1=st[0:nr, r_col + b:r_col + b + 1],
                                                  scalar2=None, op0=ALU.mult),
                 reads=[src_b, stb(tag + "r%d" % b)], writes=[xsb_b[q]])
            a = nextA()
            pv = psA[a][:].bitcast(BF16).rearrange("p (k c) -> p k c", k=8)
            for k in range(8):
                P.op("pe", lambda e, k=k: e.transpose(out=pv[:, k, 0:nr], in_=xsb[0:nr, q, k * 128:(k + 1) * 128],
                                                      identity=ident[0:nr, 0:nr]),
                     reads=[xsb_b[q], b_const], writes=[psA_b[a]], signal=(k == 7))
            eng = "act" if b % 2 == 0 else "dve"
            dst = actT[:, :, b * 128:b * 128 + nr]
            if eng == "act":
                P.op("act", lambda e: e.activation(out=dst, in_=pv[:, :, 0:nr], func=AF.Copy),
                     reads=[psA_b[a]], writes=[actT_b[b]])
            else:
                P.op("dve", lambda e: e.tensor_copy(out=dst, in_=pv[:, :, 0:nr]), reads=[psA_b[a]], writes=[actT_b[b]])

        def run_tile(T):
            kind = T["kind"]
            NTOK, nb, nr = T["ntok"], T["nb"], T["nr"]
            nseg, L = T["nseg"], T["L"]
            s_idx, t_idx = T["s"], T["t"]
            last = T["last"]
            first = T["first"]
            want_kv = last
            row0 = T["row0"]
            xsrc = T["xsrc"]
            ydst = T["ydst"]
            TS = slice(0, NTOK)
            cur_slot = {blk: idx % NSLOT for blk, idx in T["lpos"].items()}

            def cache_dma(sq_):
                cK = xin[:, 0:2, :].rearrange("p a (b c) -> p (a b) c", b=2)
                cV = xin[:, 2:4, :].rearrange("p a (b c) -> p (a b) c", b=2)
                P.dma("sp", ch_ck, cK, cache_k[sq_].rearrange("(cb p) f -> p cb f", p=128), writes=xin_b[0:2])
                P.dma("sp", ch_cv, cV, cache_v[sq_].rearrange("(cb p) f -> p cb f", p=128), writes=xin_b[2:4])
            if kind == "s":
                cache_dma(0)
            if kind == "p":
                P.dma("sp", ch_X, X[:, :, :], xsrc[row0:row0 + 512, :].rearrange("(b p) d -> p b d", p=128),
                      writes=X_b)
            else:
                P.dma("sp", ch_X, X[0:nr, 0, :], xsrc[row0:row0 + nr, :], writes=[X_b[0]])
            if not T.get("n1_done"):
                for b in range(nb):
                    q = T["xinq"][b]
                    norm_transpose(T, b, xin[0:nr, q, :], xin_b[q], SS1, R1, "n1")
            nxt = T.get("next")
            if dbg is not None and dbg.endswith("_n1"):
                P.mute = True
            UP = upre if kind == "p" else upre_s
            UPB = upre_b if kind == "p" else upre_sb
            HP = hprev if kind == "p" else hprev_s
            HPB = hprev_b if kind == "p" else hprev_sb
            u_view = lambda c: UP[:, c, 0:nseg * (3 + L)].rearrange("p (s l) -> p s l", s=nseg)
            woff = T["woff"]
            def lv(ap):
                return ap.rearrange("p (s l) -> p s l", s=nseg)

            def lru_front(c):
                q = c % 2
                T4 = tmpv(q, 3)[:, TS]
                B4 = tmpb(q, 3)
                uv = u_view(c)
                P.op("dve", lambda e: e.tensor_scalar(
                    out=lv(T4), in0=uv[:, :, 3:3 + L], scalar1=CW(3, c), scalar2=CB(c), op0=ALU.mult, op1=ALU.add),
                    reads=[UPB[c], b_cst], writes=B4)
                for j in (2, 1, 0):
                    P.op("dve", lambda e, j=j: e.scalar_tensor_tensor(
                        out=lv(T4), in0=uv[:, :, j:j + L], scalar=CW(j, c), in1=lv(T4), op0=ALU.mult, op1=ALU.add),
                        reads=[UPB[c], b_cst], writes=B4)
                if last:
                    for sg in range(nseg):
                        dstc = T["convdst"][sg][:, c * 128:(c + 1) * 128].rearrange("j p -> p j")
                        P.dma("pool", ch_conv[c], dstc, uv[:, sg, L:L + 3], reads=[UPB[c]], **NCD)
                if kind == "p" and not last:
                    P.op("pool", lambda e: e.tensor_copy(out=upre[:, c, 0:3], in_=upre[:, c, 512:515]),
                         writes=[upre_b[c]])
                elif kind == "p" and last:
                    P.op("pool", lambda e: e.memset(upre[:, c, 0:3], 0.0), writes=[upre_b[c]])
                P.op("act", lambda e: e.activation(out=ucb[:, q, TS], in_=T4, func=AF.Copy), reads=B4,
                     writes=[ucb_b[q]])
                ar, ai = nextA(), nextA()
                T["gates"][c] = (ar, ai)
                P.op("pe", lambda e: e.matmul(psA[ar][:, TS], lhsT=gw[:, 0, c, :], rhs=ucb[:, q, TS],
                                              start=True, stop=True),
                     reads=[ucb_b[q], b_const], writes=[psA_b[ar]])
                P.op("pe", lambda e: e.matmul(psA[ai][:, TS], lhsT=gw[:, 1, c, :], rhs=ucb[:, q, TS],
                                              start=True, stop=True),
                     reads=[ucb_b[q], b_const], writes=[psA_b[ai]])

            def lru_chainA(c):
                q = c % 2
                T1, T2, T3, T4, T5 = (tmpv(q, k)[:, TS] for k in range(5))
                B1, B2, B3, B4, B5 = (tmpb(q, k) for k in range(5))
                ar, ai = T["gates"][c]
                P.op("act", lambda e: e.activation(out=T5, in_=gbuf[:, c, TS], func=AF.Square),
                     reads=[gbuf_b[c]], writes=B5)
                P.op("pool", lambda e: e.tensor_scalar(out=T5, in0=T5, scalar1=C1, scalar2=C0, op0=ALU.mult,
                                                       op1=ALU.add), writes=B5)
                P.op("pool", lambda e: e.tensor_tensor(out=T5, in0=T5, in1=gbuf[:, c, TS], op=ALU.mult),
                     reads=[gbuf_b[c]], writes=B5)
                P.op("act", lambda e: e.activation(out=T1, in_=psA[ar][:, TS], func=AF.Tanh, bias=HBA(c), scale=0.5),
                     reads=[psA_b[ar], b_cst], writes=B1)
                P.op("act", lambda e: e.activation(out=T2, in_=psA[ai][:, TS], func=AF.Tanh, bias=HBX(c), scale=0.5),
                     reads=[psA_b[ai], b_cst], writes=B2)
                P.op("act", lambda e: e.activation(out=T3, in_=T1, func=AF.Exp, bias=HC8(c), scale=HC8(c)),
                     reads=B1 + [b_cst], writes=B3)
                P.op("act", lambda e: e.activation(out=T1, in_=T1, func=AF.Exp, bias=C8(c), scale=C8(c)),
                     reads=[b_cst], writes=B1)
                P.op("act", lambda e: e.activation(out=T5, in_=T5, func=AF.Tanh), writes=B5)
                P.op("dve", lambda e: e.scalar_tensor_tensor(out=T2, in0=T2, scalar=1.0, in1=T4, op0=ALU.add,
                                                             op1=ALU.mult), reads=B4, writes=B2)
                P.op("dve", lambda e: e.scalar_tensor_tensor(out=T5, in0=T5, scalar=1.0, in1=gbuf[:, c, TS],
                                                             op0=ALU.add, op1=ALU.mult), reads=[gbuf_b[c]], writes=B5)

            def lru_sqrt(c):
                q = c % 2
                T1 = tmpv(q, 0)[:, TS]
                P.op("act", lambda e: e.activation(out=T1, in_=T1, func=AF.Sqrt, bias=0.25, scale=-0.25),
                     writes=tmpb(q, 0))

            def lru_chainB(c):
                q = c % 2
                T1, T2, T3, T4, T5 = (tmpv(q, k)[:, TS] for k in range(5))
                B1, B2, B3, B4, B5 = (tmpb(q, k) for k in range(5))
                P.op("dve", lambda e: e.tensor_tensor(out=T2, in0=T2, in1=T1, op=ALU.mult), reads=B1, writes=B2)
                for sg in range(nseg):
                    init = HP[:, c, sg:sg + 1]
                    P.op("dve", lambda e, sg=sg, init=init: e.tensor_tensor_scan(
                        out=lv(T4)[:, sg, :], data0=lv(T3)[:, sg, :], data1=lv(T2)[:, sg, :], initial=init,
                        op0=ALU.mult, op1=ALU.add), reads=B3 + B2 + [HPB[c]], writes=B4)
                if last:
                    for sg in range(nseg):
                        dsth = T["lrudst"][sg][c * 128:(c + 1) * 128].rearrange("(p o) -> p o", o=1)
                        P.dma("pool", ch_lru[c], dsth, lv(T4)[:, sg, L - 1:L], reads=B4, **NCD)
                if kind == "p":
                    if not last:
                        P.op("pool", lambda e: e.tensor_copy(out=hprev[:, c, 0:1], in_=T4[:, NTOK - 1:NTOK]),
                             reads=B4, writes=[hprev_b[c]])
                    else:
                        P.op("pool", lambda e: e.memset(hprev[:, c, 0:1], 0.0), writes=[hprev_b[c]])
                P.op("dve", lambda e: e.scalar_tensor_tensor(out=actT[:, c, TS], in0=T5, scalar=0.5, in1=T4,
                                                             op0=ALU.mult, op1=ALU.mult),
                     reads=B5 + B4, writes=actT_b[0:nb])
                P.op("pool", lambda e: e.tensor_tensor(out=y2v(q)[:, TS], in0=actT[:, c, TS], in1=actT[:, c, TS],
                                                       op=ALU.mult), reads=actT_b[0:nb], writes=y2b(q))

            def lru_ssl(c):
                q = c % 2
                for b in range(nb):
                    P.op("pe", lambda e, b=b: e.matmul(
                        psO[1][0:nr, 300 + 4 * b + c:301 + 4 * b + c], lhsT=y2v(q)[:, b * 128:b * 128 + nr],
                        rhs=ones[:, 0:1], start=True, stop=True, skip_group_check=True),
                        reads=y2b(q) + [b_const], writes=[psO_b[1]], signal=(b == nb - 1))

            def lru_finish():
                P.op("dve", lambda e: e.tensor_reduce(
                    out=st[0:nr, SSL:SSL + nb], in_=psO[1][0:nr, 300:300 + 4 * nb].rearrange("p (b c) -> p b c", c=4),
                    axis=mybir.AxisListType.X, op=ALU.add), reads=[psO_b[1]], writes=[stb("ssl")])
                P.op("pool", lambda e: e.tensor_scalar(out=st[0:nr, TMP + 4:TMP + 4 + nb], in0=st[0:nr, SSL:SSL + nb],
                                                       scalar1=1.0 / LW, scalar2=EPS, op0=ALU.mult, op1=ALU.add),
                     reads=[stb("ssl")], writes=[stb("tl")])
                P.op("pool", lambda e: e.tensor_tensor(out=st[0:nr, RL:RL + nb], in0=st[0:nr, TMP + 4:TMP + 4 + nb],
                                                       in1=EM05[0:nr, :].to_broadcast([nr, nb]), op=ALU.pow),
                     reads=[stb("tl"), b_cst], writes=[stb("rl")])
                P.op("pool", lambda e: e.tensor_tensor(out=st[0:nr, 56:56 + nb], in0=st[0:nr, TMP + 4:TMP + 4 + nb],
                                                       in1=EP05[0:nr, :].to_broadcast([nr, nb]), op=ALU.pow),
                     reads=[stb("tl"), b_cst], writes=[stb("sql")])

            for m in range(16):
                if m == 8:
                    for c_ in (0, 1):
                        lru_front(c_)
                    for c_ in (0, 1):
                        lru_chainA(c_)
                    for c_ in (0, 1):
                        lru_sqrt(c_)
                blk = B_IN[m // 4]
                sl = cur_slot[blk]
                wv = wring[:, sl, :].rearrange("p (a k c) -> p a k c", a=4, k=8)
                a = nextA()
                for k in range(8):
                    P.op("pe", lambda e, a=a, wv=wv, m=m, k=k: e.matmul(psA[a][:, TS], lhsT=wv[:, m % 4, k, :],
                                                                        rhs=actT[:, k, TS], start=(k == 0),
                                                                        stop=(k == 7)),
                         reads=[ring_b[sl]] + actT_b[0:nb], writes=[psA_b[a]], signal=(k == 7))
                if m % 4 == 3:
                    consumed(T, blk)
                c = m % 4
                if m < 4:
                    dst = u_view(c)[:, :, 3:3 + L]
                    src = psA[a][:, TS].rearrange("p (s l) -> p s l", s=nseg)
                    P.op("dve", lambda e, dst=dst, src=src: e.tensor_copy(out=dst, in_=src),
                         reads=[psA_b[a]], writes=[UPB[c]])
                elif m < 8:
                    P.op("act", lambda e, a=a, c=c: e.activation(out=gbuf[:, c, TS], in_=psA[a][:, TS], func=AF.Copy),
                         reads=[psA_b[a]], writes=[gbuf_b[c]])
                elif m < 12:
                    for e2 in range(2):
                        P.op("act", lambda e, a=a, c=c, e2=e2: e.activation(
                            out=qT[e2 * 64:(e2 + 1) * 64, 2 * c + e2, TS], in_=psA[a][e2 * 64:(e2 + 1) * 64, TS],
                            func=AF.Copy), reads=[psA_b[a]], writes=[qT_b[c]])
                else:
                    P.op("dve", lambda e, a=a, c=c: e.tensor_copy(out=kT[:, c, woff:woff + NTOK], in_=psA[a][:, TS]),
                         reads=[psA_b[a]], writes=[kT_b[c][T["whalf"]]])
            if dbg is not None and dbg.endswith("_in1"):
                P.mute = True
            for which in (["v", "k"] if want_kv else ["v"]):
                blk = B_INV if which == "v" else B_INK
                sl = cur_slot[blk]
                wv = wring[:, sl, :].rearrange("p (k c) -> p k c", k=8)
                for b in range(nb):
                    a = nextA()
                    for k in range(8):
                        P.op("pe", lambda e, a=a, wv=wv, b=b, k=k: e.matmul(
                            psA[a][0:nr, :], lhsT=actT[:, k, b * 128:b * 128 + nr], rhs=wv[:, k, :],
                            start=(k == 0), stop=(k == 7)),
                            reads=[ring_b[sl], actT_b[b]], writes=[psA_b[a]], signal=(k == 7))
                    if which == "v" and not (dbg is not None and "novext" in dbg):
                        vb = T["vblk"][b]
                        if True:
                            P.op("act", lambda e, a=a, vb=vb: e.activation(
                                out=vext[0:nr, vb, :, 0:64], in_=psA[a][0:nr, :].rearrange("p (h d) -> p h d", h=8),
                                func=AF.Copy), reads=[psA_b[a]], writes=[vext_b[vb]])
                        else:
                            P.op("dve", lambda e, a=a, vb=vb: e.tensor_copy(
                                out=vext[0:nr, vb, :, 0:64], in_=psA[a][0:nr, :].rearrange("p (h d) -> p h d", h=8)),
                                reads=[psA_b[a]], writes=[vext_b[vb]])
                    if want_kv:
                        sq = (b * 2 + (0 if which == "v" else 1)) % 4
                        P.op("act", lambda e, a=a, sq=sq: e.activation(out=scrF[0:nr, sq, :], in_=psA[a][0:nr, :],
                                                                      func=AF.Copy),
                             reads=[psA_b[a]], writes=[scrF_b[sq]])
                        dst = T["kvdst"][which][b]
                        if not (dbg is not None and "nokvst" in dbg):
                            P.dma("pool", ch_kv[sq], dst, scrF[0:nr, sq, :], reads=[scrF_b[sq]])
                consumed(T, blk)
            if dbg is not None and dbg.endswith("_in"):
                P.mute = True
            if nxt is not None:
                for b in range(nxt["nb"]):
                    q = nxt["xinq"][b]
                    r0 = nxt["row0"] + b * 128
                    P.dma("sp", ch_xin[q], xin[0:nxt["nr"], q, :], nxt["xsrc"][r0:r0 + nxt["nr"], :],
                          writes=[xin_b[q]])
            def finish_group(yq, hg, po_, nrq):
                P.op("dve", lambda e: e.reciprocal(out=st[0:nrq, RS + hg * 4:RS + hg * 4 + 4],
                                                   in_=psO[po_][0:nrq, 0:260].rearrange("p (h d) -> p h d", h=4)[:, :, 64]),
                     reads=[psO_b[po_]], writes=[stb("rs%d" % hg)])
                P.op("dve", lambda e: e.tensor_tensor(
                    out=ya[0:nrq, yq, hg * 256:(hg + 1) * 256].rearrange("p (h d) -> p h d", h=4),
                    in0=psO[po_][0:nrq, 0:260].rearrange("p (h d) -> p h d", h=4)[:, :, 0:64],
                    in1=st[0:nrq, RS + hg * 4:RS + hg * 4 + 4].unsqueeze(2).to_broadcast([nrq, 4, 64]), op=ALU.mult),
                    reads=[psO_b[po_], stb("rs%d" % hg)], writes=[ya_b[yq]])

            def finish_stats(blocks, nrq):
                for b in blocks:
                    bq = b % 2
                    P.op("act", lambda e, b=b, bq=bq: e.activation(out=yab[0:nrq, bq, :], in_=ya[0:nrq, b, :],
                                                                   func=AF.Square,
                                                                   accum_out=st[0:nrq, SSA + b:SSA + b + 1]),
                         reads=[ya_b[b]], writes=[yab_b[bq], stb("ssa%d" % b)])
                nbk = len(blocks)
                b0 = blocks[0]
                P.op("pool", lambda e: e.tensor_scalar(out=st[0:nrq, TMP:TMP + nbk], in0=st[0:nrq, SSA + b0:SSA + b0 + nbk],
                                                       scalar1=1.0 / LW, scalar2=EPS, op0=ALU.mult, op1=ALU.add),
                     reads=[stb("ssa%d" % b) for b in blocks], writes=[stb("tmp")])
                P.op("pool", lambda e: e.tensor_tensor(out=st[0:nrq, SCA + b0:SCA + b0 + nbk], in0=st[0:nrq, TMP:TMP + nbk],
                                                       in1=EM05[0:nrq, :].to_broadcast([nrq, nbk]), op=ALU.pow),
                     reads=[stb("tmp"), b_cst], writes=[stb("sca%d" % b) for b in blocks])
                P.op("pool", lambda e: e.tensor_tensor(out=st[0:nrq, SCA + b0:SCA + b0 + nbk],
                                                       in0=st[0:nrq, SCA + b0:SCA + b0 + nbk],
                                                       in1=st[0:nrq, 56 + b0:56 + b0 + nbk], op=ALU.mult),
                     reads=[stb("sql")], writes=[stb("sca%d" % b) for b in blocks])

            def finish_apply(blocks, nrq):
                for b in blocks:
                    bq = b % 2
                    P.op("dve", lambda e, b=b, bq=bq: e.tensor_scalar(
                        out=yab[0:nrq, bq, :], in0=ya[0:nrq, b, :], scalar1=st[0:nrq, SCA + b:SCA + b + 1], scalar2=None,
                        op0=ALU.mult), reads=[ya_b[b], stb("sca%d" % b)], writes=[yab_b[bq]])
                    a = nextA()
                    pv = psA[a][:].bitcast(BF16).rearrange("p (k c) -> p k c", k=8)
                    for hp in range(4):
                        P.op("pe", lambda e, hp=hp, pv=pv, bq=bq: e.transpose(
                            out=pv[:, hp, 0:nrq], in_=yab[0:nrq, bq, hp * 128:(hp + 1) * 128],
                            identity=ident[0:nrq, 0:nrq]),
                            reads=[yab_b[bq], b_const], writes=[psA_b[a]], signal=(hp == 3))
                    P.op("act" if b % 2 == 0 else "dve", (lambda e, b=b, pv=pv: e.activation(
                        out=actT[:, 4:8, b * 128:b * 128 + nrq], in_=pv[:, 0:4, 0:nrq], func=AF.Copy)) if b % 2 == 0 else
                        (lambda e, b=b, pv=pv: e.tensor_copy(out=actT[:, 4:8, b * 128:b * 128 + nrq], in_=pv[:, 0:4, 0:nrq])),
                        reads=[psA_b[a]], writes=[actT_b[b]])

            if kind == "p":
                units = []
                for m in range(4):
                    for hg in range(2):
                        kbs = [kb for kb in range(5) if 4 * t_idx + m - 4 + kb >= 0]
                        for kb in kbs:
                            units.append((m, hg, kb, kb == kbs[0], kb == kbs[-1]))

                def qk(u, i):
                    m, hg, kb, fst, lst = u
                    sS = i % 2
                    ablk = (4 * t_idx + m - 4 + kb) % 8
                    kcol = ablk * 128
                    khalf = ablk // 4
                    biased = kb in (0, 3, 4)
                    kbi = {0: 0, 3: 1, 4: 2}.get(kb, 0)
                    sv = psS[sS][:].rearrange("p (j q) -> p j q", j=4)
                    for j in range(4):
                        h = hg * 4 + j
                        hp, po = h // 2, (h % 2) * 64
                        P.op("pe", lambda e, j=j, hp=hp, h=h: e.matmul(
                            sv[:, j, :], lhsT=kT[:, hp, kcol:kcol + 128],
                            rhs=qT[:, h, m * 128:(m + 1) * 128], start=(j == 0),
                            stop=(not biased and j == 3), skip_group_check=True),
                            reads=[kT_b[hp][khalf], qT_b[hp]], writes=[psS_b[sS]],
                            signal=(not biased and j == 3))
                    if biased:
                        for j in range(4):
                            h = hg * 4 + j
                            P.op("pe", lambda e, j=j, h=h: e.matmul(sv[:, j, :], lhsT=ident[:], rhs=biasT[:, kbi, h, :],
                                                                    start=False, stop=(j == 3), skip_group_check=True),
                                 reads=[b_bias, b_const], writes=[psS_b[sS]], signal=(j == 3))
                    sP = i % 4
                    P.op("act", lambda e: e.activation(out=pT[:, sP, :], in_=psS[sS][:], func=AF.Exp, scale=0.125),
                         reads=[psS_b[sS]], writes=[pT_b[sP]])

                def pv_(u, i):
                    m, hg, kb, fst, lst = u
                    sP = i % 4
                    ablk = (4 * t_idx + m - 4 + kb) % 8
                    ov = psO[hg][:, 0:260].rearrange("p (h d) -> p h d", h=4)
                    for j in range(4):
                        h = hg * 4 + j
                        P.op("pe", lambda e, j=j, h=h: e.matmul(
                            ov[:, j, :], lhsT=pT[:, sP, j * 128:(j + 1) * 128], rhs=vext[:, ablk, h, 0:65],
                            start=(fst and j == 0), stop=(lst and j == 3), skip_group_check=True),
                            reads=[pT_b[sP], vext_b[ablk]], writes=[psO_b[hg]], signal=(j == 3))
                    if lst:
                        finish_group(m, hg, hg, 128)
                for pr in range(2):
                    ms = (2 * pr, 2 * pr + 1)
                    if pr == 1:
                        for m in ms:
                            lru_front(m)
                        for m in ms:
                            lru_chainA(m)
                        for m in ms:
                            lru_sqrt(m)
                    for m in ms:
                        lru_chainB(m)
                        um = [(i, u) for i, u in enumerate(units) if u[0] == m]
                        n = len(um)
                        hooks = {}
                        if m == 3:
                            def h_a():
                                lru_ssl(3)
                                lru_finish()

                            def h_b():
                                finish_stats([0, 1, 2], 128)
                            hooks = {min(3, n - 1): h_a, min(5, n - 1) + (1 if n - 1 <= 3 else 0): h_b}
                        for x in range(min(2, n)):
                            qk(um[x][1], um[x][0])
                        for x in range(n):
                            pv_(um[x][1], um[x][0])
                            if x + 2 < n:
                                qk(um[x + 2][1], um[x + 2][0])
                            if x in hooks:
                                hooks.pop(x)()
                        for x in sorted(hooks):
                            hooks[x]()
                        if m < 3:
                            lru_ssl(m)
                finish_stats([3], 128)
                finish_apply([0, 1, 2, 3], 128)
            else:
                cstK = xin[:, 0:2, :].rearrange("p a (b c) -> p (a b) c", b=2)
                cstV = xin[:, 2:4, :].rearrange("p a (b c) -> p (a b) c", b=2)
                ui = [0]
                fstO = [True, True]

                def load_cache(sq_):
                    for half in range(2):
                        qx = T["xq"][0] % 2
                        T["xq"][0] += 1
                        xv = xsb[:, qx, :].rearrange("p (cb f) -> p cb f", cb=2)
                        P.op("dve", lambda e, xv=xv, half=half: e.tensor_copy(out=xv, in_=cstK[:, 2 * half:2 * half + 2, :]),
                             reads=xin_b[0:2], writes=[xsb_b[qx]])
                        a = nextA()
                        pv = psA[a][:].bitcast(BF16).rearrange("p (k c) -> p k c", k=8)
                        for cbl in range(2):
                            for hp in range(4):
                                P.op("pe", lambda e, cbl=cbl, hp=hp, xv=xv, pv=pv: e.transpose(
                                    out=pv[:, cbl * 4 + hp, :], in_=xv[:, cbl, hp * 128:(hp + 1) * 128], identity=ident[:]),
                                    reads=[xsb_b[qx], b_const], writes=[psA_b[a]], signal=(cbl == 1 and hp == 3))
                        for cbl in range(2):
                            cb = 2 * half + cbl
                            P.op("act", lambda e, cb=cb, cbl=cbl, pv=pv: e.activation(
                                out=kT[:, :, cb * 128:(cb + 1) * 128], in_=pv[:, cbl * 4:cbl * 4 + 4, :], func=AF.Copy),
                                reads=[psA_b[a]], writes=[kT_b[hp][0] for hp in range(4)])
                    for cb in range(4):
                        P.op("act", lambda e, cb=cb: e.activation(
                            out=vext[:, cb, :, 0:64], in_=cstV[:, cb, :].rearrange("p (h d) -> p h d", h=8),
                            func=AF.Copy), reads=xin_b[2:4], writes=[vext_b[cb]])

                def cache_unit(sq_, hg, cb):
                    i = ui[0]
                    ui[0] += 1
                    sS, sP = i % 2, i % 4
                    sv = psS[sS][:].rearrange("p (j q) -> p j q", j=4)
                    biased = cb == 3
                    for j in range(4):
                        h = hg * 4 + j
                        hp, po = h // 2, (h % 2) * 64
                        P.op("pe", lambda e, j=j, hp=hp, h=h: e.matmul(
                            sv[:, j, 0:16], lhsT=kT[:, hp, cb * 128:(cb + 1) * 128],
                            rhs=qT[:, h, sq_ * 16:sq_ * 16 + 16], start=(j == 0),
                            stop=(not biased and j == 3), skip_group_check=True),
                            reads=[kT_b[hp][0], qT_b[hp]], writes=[psS_b[sS]], signal=(not biased and j == 3))
                    if biased:
                        for j in range(4):
                            h = hg * 4 + j
                            P.op("pe", lambda e, j=j, h=h: e.matmul(
                                sv[:, j, 0:16], lhsT=ident[:], rhs=biasT[:, 1, h, 0:16], start=False,
                                stop=(j == 3), skip_group_check=True),
                                reads=[b_bias, b_const], writes=[psS_b[sS]], signal=(j == 3))
                    pview = pT[:, sP, 0:256].rearrange("p (j q) -> p j q", j=4)
                    P.op("pool", lambda e: e.memset(pT[:, sP, 0:256], 0.0), writes=[pT_b[sP]])
                    P.op("act", lambda e: e.activation(out=pview[:, :, sq_ * 16:sq_ * 16 + 16], in_=sv[:, :, 0:16],
                                                       func=AF.Exp, scale=0.125),
                         reads=[psS_b[sS]], writes=[pT_b[sP]])
                    ov = psO[hg][:, 0:260].rearrange("p (h d) -> p h d", h=4)
                    f0 = fstO[hg]
                    fstO[hg] = False
                    for j in range(4):
                        h = hg * 4 + j
                        P.op("pe", lambda e, j=j, h=h: e.matmul(
                            ov[0:64, j, :], lhsT=pview[:, j, :], rhs=vext[:, cb, h, 0:65],
                            start=(f0 and j == 0), stop=False, skip_group_check=True),
                            reads=[pT_b[sP], vext_b[cb]], writes=[psO_b[hg]], signal=(j == 3))

                def new_unit(hg):
                    i = ui[0]
                    ui[0] += 1
                    sS, sP = i % 2, i % 4
                    sv = psS[sS][:].rearrange("p (j q) -> p j q", j=4)
                    for j in range(4):
                        h = hg * 4 + j
                        hp, po = h // 2, (h % 2) * 64
                        P.op("pe", lambda e, j=j, hp=hp, h=h: e.matmul(
                            sv[0:64, j, 0:64], lhsT=kT[:, hp, 512:576], rhs=qT[:, h, 0:64],
                            start=(j == 0), stop=False, skip_group_check=True),
                            reads=[kT_b[hp][1], qT_b[hp]], writes=[psS_b[sS]], signal=False)
                    for j in range(4):
                        h = hg * 4 + j
                        P.op("pe", lambda e, j=j, h=h: e.matmul(
                            sv[0:64, j, 0:64], lhsT=ident[0:64, 0:64], rhs=biasN[:, h, :], start=False, stop=(j == 3),
                            skip_group_check=True), reads=[b_bias, b_const], writes=[psS_b[sS]], signal=(j == 3))
                    pview = pT[:, sP, 0:256].rearrange("p (j q) -> p j q", j=4)
                    P.op("act", lambda e: e.activation(out=pview[0:64, :, :], in_=sv[0:64, :, 0:64], func=AF.Exp,
                                                       scale=0.125), reads=[psS_b[sS]], writes=[pT_b[sP]])
                    ov = psO[hg][:, 0:260].rearrange("p (h d) -> p h d", h=4)
                    for j in range(4):
                        h = hg * 4 + j
                        P.op("pe", lambda e, j=j, h=h: e.matmul(
                            ov[0:64, j, :], lhsT=pview[0:64, j, :], rhs=vext[0:64, 4, h, 0:65], start=False,
                            stop=(j == 3), skip_group_check=True),
                            reads=[pT_b[sP], vext_b[4]], writes=[psO_b[hg]], signal=(j == 3))
                    finish_group(0, hg, hg, 64)

                for pr in range(2):
                    cs = (2 * pr, 2 * pr + 1)
                    if pr == 1:
                        for c in cs:
                            lru_front(c)
                        for c in cs:
                            lru_chainA(c)
                        for c in cs:
                            lru_sqrt(c)
                    for c in cs:
                        lru_chainB(c)
                        lru_ssl(c)
                lru_finish()
                for sq_ in range(NS):
                    load_cache(sq_)
                    if sq_ + 1 < NS:
                        cache_dma(sq_ + 1)
                    for hg in range(2):
                        for cb in range(4):
                            cache_unit(sq_, hg, cb)
                for hg in range(2):
                    new_unit(hg)
                finish_stats([0], 64)
                finish_apply([0], 64)

            if dbg is not None and dbg.endswith("_att"):
                P.mute = True
            for b in range(nb):
                a0, a1 = nextA(), nextA()
                for k in range(8):
                    sl = cur_slot[B_OUT[k // 4]]
                    wv = wring[:, sl, :].rearrange("p (k c) -> p k c", k=4)
                    for half, a in enumerate((a0, a1)):
                        P.op("pe", lambda e, a=a, k=k, wv=wv, b=b, half=half: e.matmul(
                            psA[a][0:nr, :], lhsT=actT[:, k, b * 128:b * 128 + nr],
                            rhs=wv[:, k % 4, half * 512:(half + 1) * 512], start=(k == 0), stop=(k == 7)),
                            reads=[ring_b[sl], actT_b[b]], writes=[psA_b[a]], signal=(k == 7))
                for half, a in enumerate((a0, a1)):
                    P.op("dve", lambda e, a=a, b=b, half=half: e.scalar_tensor_tensor(
                        out=X[0:nr, b, half * 512:(half + 1) * 512], in0=psA[a][0:nr, :],
                        scalar=st[0:nr, RL + b:RL + b + 1], in1=X[0:nr, b, half * 512:(half + 1) * 512],
                        op0=ALU.mult, op1=ALU.add), reads=[psA_b[a], stb("rl")], writes=[X_b[b]])
                norm_transpose(T, b, X[0:nr, b, :], X_b[b], SS2, R2, "n2", phase="stats")
            for blk in B_OUT:
                consumed(T, blk)
            if dbg is not None and dbg.endswith("_out"):
                P.mute = True
            for b in range(nb):
                norm_transpose(T, b, X[0:nr, b, :], X_b[b], SS2, R2, "n2", phase="apply")
            for f in range(NF):
                i = f // 2
                sl = cur_slot[B_GU[i]]
                wv = wring[:, sl, :].rearrange("p (a k c) -> p a k c", a=4, k=8)
                ag, au = nextA(), nextA()
                for gu, a in enumerate((ag, au)):
                    for k in range(8):
                        P.op("pe", lambda e, a=a, k=k, wv=wv, gu=gu, f=f: e.matmul(
                            psA[a][:, TS], lhsT=wv[:, (f % 2) * 2 + gu, k, :], rhs=actT[:, k, TS], start=(k == 0),
                            stop=(k == 7)), reads=[ring_b[sl]] + actT_b[0:nb], writes=[psA_b[a]], signal=(k == 7))
                tq = f % 2
                P.op("act", lambda e, ag=ag, tq=tq: e.activation(out=scrF[:, tq, TS], in_=psA[ag][:, TS], func=AF.Tanh,
                                                                 scale=0.5), reads=[psA_b[ag]], writes=[scrF_b[tq]])
                P.op("dve", lambda e, ag=ag, tq=tq: e.scalar_tensor_tensor(
                    out=scrF[:, 2 + tq, TS], in0=scrF[:, tq, TS], scalar=1.0, in1=psA[ag][:, TS], op0=ALU.add,
                    op1=ALU.mult), reads=[scrF_b[tq], psA_b[ag]], writes=[scrF_b[2 + tq]])
                als = [hT_b[f]]
                P.op("dve", lambda e, au=au, tq=tq, f=f: e.scalar_tensor_tensor(
                    out=hTv(f)[:, TS], in0=scrF[:, 2 + tq, TS], scalar=0.5, in1=psA[au][:, TS], op0=ALU.mult,
                    op1=ALU.mult), reads=[scrF_b[2 + tq], psA_b[au]], writes=als)
                if f % 2 == 1:
                    consumed(T, B_GU[i])
            if dbg is not None and dbg.endswith("_ffn"):
                P.mute = True
            for b in range(nb):
                a0, a1 = nextA(), nextA()
                for kf in range(NF):
                    sl = cur_slot[B_DN[kf // 4]]
                    wv = wring[:, sl, :].rearrange("p (k c) -> p k c", k=4)
                    for half, a in enumerate((a0, a1)):
                        P.op("pe", lambda e, a=a, kf=kf, wv=wv, b=b, half=half: e.matmul(
                            psA[a][0:nr, :], lhsT=hTv(kf)[:, b * 128:b * 128 + nr],
                            rhs=wv[:, kf % 4, half * 512:(half + 1) * 512], start=(kf == 0), stop=(kf == NF - 1)),
                            reads=[ring_b[sl], hT_b[kf]], writes=[psA_b[a]], signal=(kf == NF - 1))
                    if b == nb - 1 and (kf % 4 == 3 or kf == NF - 1):
                        consumed(T, B_DN[kf // 4])
                if nxt is not None and b < nxt["nb"] and OVERLAP_N1:
                    qn = nxt["xinq"][b]
                    norm_transpose(nxt, b, xin[0:nxt["nr"], qn, :], xin_b[qn], SS1, R1, "n1")
                    nxt["n1_done"] = True
                for half, a in enumerate((a0, a1)):
                    P.op("dve", lambda e, a=a, b=b, half=half: e.tensor_tensor(
                        out=X[0:nr, b, half * 512:(half + 1) * 512], in0=psA[a][0:nr, :],
                        in1=X[0:nr, b, half * 512:(half + 1) * 512], op=ALU.add), reads=[psA_b[a]], writes=[X_b[b]])
                P.op("act", lambda e, b=b: e.activation(out=junk[0:nr, :], in_=X[0:nr, b, :], func=AF.Square,
                                                        accum_out=st[0:nr, SS3 + b:SS3 + b + 1]),
                     reads=[X_b[b]], writes=[b_junk, stb("n3ss%d" % b)])
                pool_rsqrt(R3 + b, SS3 + b, 1, 1.0 / D, nr, [stb("n3ss%d" % b)], [stb("n3r%d" % b)])
                P.op("dve", lambda e, b=b: e.scalar_tensor_tensor(
                    out=X[0:nr, b, :], in0=X[0:nr, b, :], scalar=st[0:nr, R3 + b:R3 + b + 1], in1=gfin[0:nr, :],
                    op0=ALU.mult, op1=ALU.mult), reads=[stb("n3r%d" % b), b_const], writes=[X_b[b]])
                r0 = row0 + b * 128
                P.dma("pool", ch_y[b], ydst[r0:r0 + nr, :], X[0:nr, b, :], reads=[X_b[b]])

        tiles = []
        xq = [0]
        for s in range(NSEQ):
            for t in range(NT):
                half = t % 2
                tiles.append(dict(
                    kind="p", s=s, t=t, ntok=512, nb=4, nr=128, nseg=1, L=512, first=(t == 0), last=(t == NT - 1),
                    row0=s * SEQ + t * 512, xsrc=xp, ydst=y_prompt, woff=half * 512, whalf=half,
                    vblk=[(4 * t + b) % 8 for b in range(4)], xinq=[0, 1, 2, 3], xq=xq, y2={}, yaq=0, gates={},
                    kvdst={"k": [prompt_k[s][b * 128:(b + 1) * 128, :] for b in range(4)],
                           "v": [prompt_v[s][b * 128:(b + 1) * 128, :] for b in range(4)]},
                    convdst=[prompt_conv[s]], lrudst=[prompt_lru[s]]))
        nr_s = NS * DSEQ
        tiles.append(dict(
            kind="s", s=0, t=0, ntok=nr_s, nb=1, nr=nr_s, nseg=NS, L=DSEQ, first=True, last=True, row0=0, xsrc=xs,
            ydst=y_sample, woff=512, whalf=1, vblk=[4], xinq=[0], xq=xq, y2={}, yaq=0, gates={},
            kvdst={"k": [sample_k[:, :]], "v": [sample_v[:, :]]},
            convdst=[sample_conv[sg] for sg in range(NS)], lrudst=[sample_lru[sg] for sg in range(NS)]))
        for i, T in enumerate(tiles):
            T["next"] = tiles[i + 1] if i + 1 < len(tiles) else None
            seq = B_IN + [B_INV] + ([B_INK] if T["last"] else []) + B_OUT + B_GU + B_DN
            T["lpos"] = {blk: len(load_seq) + j for j, blk in enumerate(seq)}
            load_seq.extend(seq)
        if not (dbg is not None and dbg.startswith("setup")):
            emit_loads_upto(NSLOT)
        T0 = tiles[0]
        for b in range(T0["nb"] if not (dbg is not None and dbg.startswith("setup")) else 0):
            q = T0["xinq"][b]
            r0 = T0["row0"] + b * 128
            P.dma("sp", ch_xin[q], xin[0:T0["nr"], q, :], T0["xsrc"][r0:r0 + T0["nr"], :], writes=[xin_b[q]])
        if dbg is not None and dbg.startswith("setup"):
            tiles = []
        if dbg is not None and dbg.startswith("p1"):
            tiles = tiles[:1]
        if any(T["kind"] == "s" for T in tiles):
            P.op("pool", lambda e: e.memset(upre_s[:], 0.0), writes=upre_sb)
            for sg in range(NS):
                for c in range(4):
                    P.dma("sp", ch_state, upre_s[:, c, sg * 19:sg * 19 + 3],
                          state_conv[sg][:, c * 128:(c + 1) * 128].rearrange("j p -> p j"),
                          writes=[upre_sb[c]], allow_slow_non_contiguous=True)
            for c in range(4):
                P.dma("sp", ch_state, hprev_s[:, c, 0:NS], state_lru[:, c * 128:(c + 1) * 128].rearrange("s p -> p s"),
                      writes=[hprev_sb[c]], allow_slow_non_contiguous=True)
            P.seal(ch_state, upre_sb + hprev_sb)
        for T in tiles:
            run_tile(T)
        P.mute = False
        for e in ("pool",):
            P.wait_all(e, store_chans + ch_ring + ch_xin + [ch_X, ch_state, ch_ck, ch_cv])
        P.flush(block)
    return nc


_CACHE = {}


def kernel(**inputs):
    NC = 8
    B, SEQ, _ = inputs["x_prompt"].shape
    DB, DSEQ, _ = inputs["x_sample"].shape
    NSEQ = B // NC
    NS = DB // NC
    key = (NSEQ, SEQ, NS, DSEQ)
    if key not in _CACHE:
        _CACHE[key] = build(NSEQ, SEQ, NS, DSEQ)
    nc = _CACHE[key]
    f = lambda a: np.ascontiguousarray(a, dtype=np.float32)
    shared = {}
    for n in ["norm_mix", "w_in", "conv_w", "conv_b", "lru_wa", "lru_ba", "lru_wx", "lru_bx", "lru_lambda",
              "rel_bias", "norm_lru_out", "norm_attn_out", "w_out", "norm_ffn", "w_gate", "w_up", "w_down"]:
        shared[n] = f(inputs[n][0])
    shared["norm_final"] = f(inputs["norm_final"])
    in_maps = []
    for c in range(NC):
        m = dict(shared)
        m["x_prompt"] = f(inputs["x_prompt"][c * NSEQ:(c + 1) * NSEQ]).reshape(NSEQ * SEQ, D)
        m["x_sample"] = f(inputs["x_sample"][c * NS:(c + 1) * NS]).reshape(NS * DSEQ, D)
        m["state_conv"] = f(inputs["state_conv"][0, c * NS:(c + 1) * NS])
        m["state_lru"] = f(inputs["state_lru"][0, c * NS:(c + 1) * NS])
        m["cache_k"] = f(inputs["cache_k"][0, c * NS:(c + 1) * NS]).reshape(NS, 512, 512)
        m["cache_v"] = f(inputs["cache_v"][0, c * NS:(c + 1) * NS]).reshape(NS, 512, 512)
        in_maps.append(m)
    res = run_bass_kernel_spmd(nc, in_maps, core_ids=list(range(NC)))
    R = res.results
    cat = lambda n: np.concatenate([np.asarray(r[n], dtype=np.float32) for r in R], axis=0)
    y_prompt = cat("y_prompt").reshape(B, SEQ, D)
    y_sample = cat("y_sample").reshape(DB, DSEQ, D)
    keep = min(512, SEQ)
    outs = (
        y_prompt, y_sample,
        cat("prompt_conv").reshape(1, B, 3, LW), cat("prompt_lru").reshape(1, B, LW),
        cat("prompt_k").reshape(1, B, keep, 8, 64), cat("prompt_v").reshape(1, B, keep, 8, 64),
        cat("sample_conv").reshape(1, DB, 3, LW), cat("sample_lru").reshape(1, DB, LW),
        cat("sample_k").reshape(1, DB, DSEQ, 8, 64), cat("sample_v").reshape(1, DB, DSEQ, 8, 64),
    )
    return outs
```

```python
import numpy as np
from contextlib import ExitStack
import concourse.bass as bass
import concourse.mybir as mybir
from concourse.bass_utils import run_bass_kernel_spmd

F32 = mybir.dt.float32
BF16 = mybir.dt.bfloat16
AF = mybir.ActivationFunctionType
ALU = mybir.AluOpType

D = 1024
KC = 8
LW = 512
DFF = 2816
NF = 22
INC = 2560
EPS = 1e-6
NEG = -30000.0
NCD = dict(allow_slow_non_contiguous=True)
C0 = 0.7978845608028654
C1 = 0.7978845608028654 * 0.044715
B_IN = [0, 1, 2, 3]
B_INV = 4
B_INK = 5
B_OUT = [6, 7]
B_GU = list(range(8, 19))
B_DN = list(range(19, 25))
NBLK = 25
NSLOT = 6
OVERLAP_N1 = True


class Buf:
    def __init__(self, name=""):
        self.name = name
        self.w = {}
        self.r = {}


class Chan:
    def __init__(self, sem):
        self.sem = sem
        self.count = 0


class Prog:
    ENG = ("pe", "act", "dve", "pool", "sp")

    def __init__(self, nc, es):
        self.nc = nc
        self.streams = {e: [] for e in self.ENG}
        self.sem = {e: es.enter_context(nc.semaphore("sem_" + e)) for e in self.ENG if e != "sp"}
        self.cnt = {e: 0 for e in self.ENG}
        self.waited = {e: {} for e in self.ENG}
        self.semid = {}
        self.es = es
        self.nchan = 0

    def chan(self):
        self.nchan += 1
        return Chan(self.es.enter_context(self.nc.semaphore("ch%d" % self.nchan)))

    def _key(self, sem):
        k = id(sem)
        self.semid[k] = sem
        return k

    def _deps(self, eng, reads, writes, skip=None):
        deps = {}
        for b in reads:
            for k, v in b.w.items():
                deps[k] = max(deps.get(k, 0), v)
        for b in writes:
            for k, v in b.w.items():
                deps[k] = max(deps.get(k, 0), v)
            for k, v in b.r.items():
                deps[k] = max(deps.get(k, 0), v)
        pek = self._key(self.sem["pe"])
        for k, v in deps.items():
            if eng == "pe" and k == pek:
                continue
            if skip is not None and k == skip[0] and v < skip[1]:
                continue
            if self.waited[eng].get(k, 0) >= v:
                continue
            self.waited[eng][k] = v
            sem = self.semid[k]
            self.streams[eng].append(lambda e, s=sem, vv=v: e.wait_ge(s, vv))

    def _mark(self, k, v, reads, writes):
        for b in reads:
            b.r[k] = max(b.r.get(k, 0), v)
        for b in writes:
            b.w = {k: v}
            b.r = {}

    mute = False

    def op(self, eng, fn, reads=(), writes=(), signal=True):
        if self.mute:
            return
        reads = [b for b in reads if b is not None]
        writes = [b for b in writes if b is not None]
        self._deps(eng, reads, writes)
        sem = self.sem[eng]
        k = self._key(sem)
        if signal:
            self.cnt[eng] += 1
            v = self.cnt[eng]
            self.streams[eng].append(lambda e, f=fn, s=sem: f(e).then_inc(s, 1))
        else:
            v = self.cnt[eng] + 1
            self.streams[eng].append(lambda e, f=fn: f(e))
        self._mark(k, v, reads, writes)

    def dma(self, q, ch, out, in_, reads=(), writes=(), **kw):
        if self.mute:
            return
        reads = [b for b in reads if b is not None]
        writes = [b for b in writes if b is not None]
        self._deps(q, reads, writes, skip=(self._key(ch.sem), ch.count))
        ch.count += 16
        sem = ch.sem
        self.streams[q].append(
            lambda e, o=out, i=in_, s=sem, kw=kw: e.dma_start(out=o, in_=i, **kw).then_inc(s, 16))
        self._mark(self._key(sem), ch.count, reads, writes)

    def seal(self, ch, bufs):
        k = self._key(ch.sem)
        for b in bufs:
            if k in b.w:
                b.w[k] = ch.count

    def wait_all(self, eng, chans):
        for ch in chans:
            if ch.count:
                k = self._key(ch.sem)
                if self.waited[eng].get(k, 0) < ch.count:
                    self.waited[eng][k] = ch.count
                    self.streams[eng].append(lambda e, s=ch.sem, v=ch.count: e.wait_ge(s, v))

    def flush(self, block):
        m = {"sp": block.sync, "act": block.scalar, "dve": block.vector, "pool": block.gpsimd,
             "pe": block.tensor}
        for e in self.ENG:
            lst = self.streams[e]
            if not lst:
                continue
            self.streams[e] = []

            def body(eng, lst=lst):
                for f in lst:
                    f(eng)
            m[e](body)


def build(NSEQ, SEQ, NS, DSEQ=16, dbg=None):
    NT = SEQ // 512
    nc = bass.Bass("TRN2", target_bir_lowering=False)
    dt = lambda n, s, k="ExternalInput", d=F32: nc.dram_tensor(n, s, d, kind=k).ap()
    xp = dt("x_prompt", [NSEQ * SEQ, D])
    xs = dt("x_sample", [NS * DSEQ, D])
    state_conv = dt("state_conv", [NS, 3, LW])
    state_lru = dt("state_lru", [NS, LW])
    cache_k = dt("cache_k", [NS, 512, 512])
    cache_v = dt("cache_v", [NS, 512, 512])
    norm_mix = dt("norm_mix", [D])
    w_in = dt("w_in", [D, INC])
    conv_w = dt("conv_w", [4, LW])
    conv_b = dt("conv_b", [LW])
    lru_wa = dt("lru_wa", [8, 64, 64])
    lru_ba = dt("lru_ba", [LW])
    lru_wx = dt("lru_wx", [8, 64, 64])
    lru_bx = dt("lru_bx", [LW])
    lru_lambda = dt("lru_lambda", [LW])
    rel_bias = dt("rel_bias", [8, 257])
    norm_lru_out = dt("norm_lru_out", [LW])
    norm_attn_out = dt("norm_attn_out", [LW])
    w_out = dt("w_out", [D, D])
    norm_ffn = dt("norm_ffn", [D])
    w_gate = dt("w_gate", [D, DFF])
    w_up = dt("w_up", [D, DFF])
    w_down = dt("w_down", [DFF, D])
    norm_final = dt("norm_final", [D])
    O = "ExternalOutput"
    y_prompt = dt("y_prompt", [NSEQ * SEQ, D], O)
    y_sample = dt("y_sample", [NS * DSEQ, D], O)
    prompt_conv = dt("prompt_conv", [NSEQ, 3, LW], O)
    prompt_lru = dt("prompt_lru", [NSEQ, LW], O)
    prompt_k = dt("prompt_k", [NSEQ, 512, 512], O)
    prompt_v = dt("prompt_v", [NSEQ, 512, 512], O)
    sample_conv = dt("sample_conv", [NS, 3, LW], O)
    sample_lru = dt("sample_lru", [NS, LW], O)
    sample_k = dt("sample_k", [NS * DSEQ, 512], O)
    sample_v = dt("sample_v", [NS * DSEQ, 512], O)
    wsc = dt("wsc", [NBLK, 128, 4096], "Internal", BF16)
    ext = dt("ext", [8, 640], "Internal")
    ext2 = dt("ext2", [8, 128, 640], "Internal")

    with ExitStack() as es:
        P = Prog(nc, es)
        sb = lambda n, s, d=F32: es.enter_context(nc.sbuf_tensor(n, s, d))
        wring = sb("wring", [128, NSLOT, 4096], BF16)
        X = sb("X", [128, 4, D])
        xin = sb("xin", [128, 4, D])
        actT = sb("actT", [128, 8, 512], BF16)
        upre = sb("upre", [128, 4, 515])
        gbuf = sb("gbuf", [128, 4, 512])
        qT = sb("qT", [128, 8, 512], BF16)
        kT = sb("kT", [128, 4, 1024], BF16)
        vext = sb("vext", [128, 8, 8, 66], BF16)
        big = sb("big", [128, NF * 512], BF16)
        ucb = sb("ucb", [128, 2, 512], BF16)
        pT = sb("pT", [128, 4, 512], BF16)
        ya = sb("ya", [128, 4, 512])
        yab = sb("yab", [128, 2, 512], BF16)
        xsb = sb("xsb", [128, 2, D], BF16)
        junk = sb("junk", [128, D], BF16)
        biasT = sb("biasT", [128, 3, 8, 128], BF16)
        biasN = sb("biasN", [64, 8, 64], BF16)
        scrF = sb("scrF", [128, 4, 512])
        gfin = sb("gfin", [128, D])
        gw = sb("gw", [128, 2, 4, 128], BF16)
        ident = sb("ident", [128, 128], BF16)
        ones = sb("ones", [128, 2], BF16)
        cst = sb("cst", [128, 64])
        st = sb("st", [128, 96])
        hprev = sb("hprev", [128, 4, 4])
        hprev_s = sb("hprev_s", [128, 4, 4])
        upre_s = sb("upre_s", [128, 4, 80])
        gains = sb("gains", [128, 3, 8])
        pp = lambda n: es.enter_context(nc.psum_tensor(n, [128, 512], F32))
        psA = [pp("psA%d" % i) for i in range(4)]
        psS = [pp("psS%d" % i) for i in range(2)]
        psO = [pp("psO%d" % i) for i in range(2)]

        CW = lambda j, c: cst[:, j * 4 + c: j * 4 + c + 1]
        CB = lambda c: cst[:, 16 + c: 17 + c]
        HBA = lambda c: cst[:, 20 + c: 21 + c]
        HBX = lambda c: cst[:, 24 + c: 25 + c]
        C8 = lambda c: cst[:, 28 + c: 29 + c]
        HC8 = lambda c: cst[:, 32 + c: 33 + c]
        LAM = cst[:, 36:40]
        EM05 = cst[:, 40:41]
        EP05 = cst[:, 41:42]
        CBH = cst[:, 48:56]
        b_cst = Buf("cst")
        b_gains = Buf("gains")

        ring_b = [Buf("ring%d" % i) for i in range(NSLOT)]
        X_b = [Buf() for _ in range(4)]
        xin_b = [Buf() for _ in range(4)]
        actT_b = [Buf() for _ in range(4)]
        upre_b = [Buf() for _ in range(4)]
        gbuf_b = [Buf() for _ in range(4)]
        qT_b = [Buf() for _ in range(4)]
        kT_b = [[Buf(), Buf()] for _ in range(4)]
        vext_b = [Buf() for _ in range(8)]
        hT_b = [Buf() for _ in range(NF)]
        ucb_b = [Buf(), Buf()]
        pT_b = [Buf() for _ in range(4)]
        ya_b = [Buf() for _ in range(4)]
        yab_b = [Buf(), Buf()]
        xsb_b = [Buf(), Buf()]
        b_junk = Buf()
        scrF_b = [Buf() for _ in range(4)]
        psA_b = [Buf() for _ in range(4)]
        psS_b = [Buf() for _ in range(2)]
        psO_b = [Buf() for _ in range(2)]
        wsc_b = [Buf() for _ in range(NBLK)]
        b_bias = Buf()
        b_const = Buf()
        hprev_b = [Buf() for _ in range(4)]
        hprev_sb = [Buf() for _ in range(4)]
        upre_sb = [Buf() for _ in range(4)]
        st_b = {}

        def stb(name):
            if name not in st_b:
                st_b[name] = Buf(name)
            return st_b[name]
        SS1, R1, SS2, R2, SS3, R3, SSA, SSL, RL, SCA, TMP, RS = (0, 4, 8, 12, 16, 20, 24, 28, 32, 36, 40, 48)

        def tmpv(q, k):
            i = q * 5 + k
            return big[:, i * 1024:(i + 1) * 1024].bitcast(F32)

        def tmpb(q, k):
            i = q * 5 + k
            return [hT_b[2 * i], hT_b[2 * i + 1]]

        def y2v(i):
            return big[:, (20 + i) * 512:(21 + i) * 512]

        def y2b(i):
            return [hT_b[20 + i]]
        hTv = lambda f: big[:, f * 512:(f + 1) * 512]

        rrA = [0]

        def nextA():
            i = rrA[0] % 4
            rrA[0] += 1
            return i
        ch_misc = P.chan()
        ch_ring = [P.chan() for _ in range(NSLOT)]
        ch_xin = [P.chan() for _ in range(4)]
        ch_X = P.chan()
        ch_y = [P.chan() for _ in range(4)]
        ch_kv = [P.chan() for _ in range(4)]
        ch_conv = [P.chan() for _ in range(4)]
        ch_lru = [P.chan() for _ in range(4)]
        ch_state = P.chan()
        ch_ck = P.chan()
        ch_cv = P.chan()
        ch_e = [P.chan() for _ in range(3)]
        store_chans = ch_y + ch_kv + ch_conv + ch_lru

        with ExitStack() as es1:
            sb1 = lambda n, s, d=F32: es1.enter_context(nc.sbuf_tensor(n, s, d))
            stage = wring[:, 0:4, :].rearrange("p (q a) c -> p q (a c)", q=2).bitcast(F32)
            obf = wring[:, 4:6, :]
            gst = ya[:, 0:2, :].rearrange("p a (j c) -> p a j c", j=4)
            identf = sb1("identf", [128, 128])
            E8 = sb1("E8", [8, 640])
            B34 = scrF[:].rearrange("p a c -> p (a c)").rearrange("p (i h c) -> p i h c", i=2, h=8)
            BNf = sb1("BNf", [64, 8, 64])
            stage_b = [Buf(), Buf()]
            obf_b = [Buf(), Buf()]
            b_gst, b_identf, b_E8, b_B34, b_BNf, b_ext, b_ext2 = (Buf() for _ in range(7))
            ch_stg = [P.chan(), P.chan()]
            ch_obf = [P.chan(), P.chan()]
            block = es1.enter_context(nc.Block())

            cstage = sb1("cstage", [64, 128])
            gstage = sb1("gstage", [24, 128])
            b_cstage, b_gstage = Buf(), Buf()
            P.op("pool", lambda e: e.memset(cstage[:], 0.0), writes=[b_cstage])
            P.dma("sp", ch_misc, gstage[0:8, :], norm_mix.rearrange("(k p) -> k p", p=128), writes=[b_gstage])
            P.dma("sp", ch_misc, gstage[8:16, :], norm_ffn.rearrange("(k p) -> k p", p=128), writes=[b_gstage])
            P.dma("sp", ch_misc, gstage[16:20, :], norm_lru_out.rearrange("(k p) -> k p", p=128), writes=[b_gstage])
            P.dma("sp", ch_misc, gstage[20:24, :], norm_attn_out.rearrange("(k p) -> k p", p=128), writes=[b_gstage])
            P.dma("sp", ch_misc, cstage[0:16, :], conv_w.rearrange("j (c p) -> (j c) p", p=128), writes=[b_cstage])
            for row, src in [(16, conv_b), (20, lru_ba), (24, lru_bx), (36, lru_lambda)]:
                P.dma("sp", ch_misc, cstage[row:row + 4, :], src.rearrange("(c p) -> c p", p=128), writes=[b_cstage])
            P.dma("sp", ch_misc, CBH, bass.AP(tensor=rel_bias.tensor, offset=256, ap=[[0, 128], [257, 8]]),
                  writes=[b_cst], **NCD)
            P.dma("sp", ch_misc, gfin[:], bass.AP(tensor=norm_final.tensor, offset=0, ap=[[0, 128], [1, D]]),
                  writes=[b_const])
            P.op("pool", lambda e: e.memset(gst[:], 0.0), writes=[b_gst])
            for g, src in enumerate([lru_wa, lru_wx]):
                for n in range(8):
                    j, hf = n // 2, n % 2
                    P.dma("sp", ch_misc, gst[hf * 64:(hf + 1) * 64, g, j, hf * 64:(hf + 1) * 64], src[n],
                          writes=[b_gst])
            P.dma("sp", ch_misc, E8[:, 0:257], rel_bias, writes=[b_E8])
            P.seal(ch_misc, [b_gains, b_cst, b_const, b_gst, b_E8, b_cstage, b_gstage])
            P.op("pool", lambda e: e.memset(identf[:], 0.0), writes=[b_identf])
            P.op("pool", lambda e: e.affine_select(out=identf[:], in_=identf[:], pattern=[[-1, 128]],
                                                   compare_op=ALU.not_equal, fill=1.0, base=0,
                                                   channel_multiplier=1), writes=[b_identf])
            P.op("dve", lambda e: e.tensor_copy(out=ident[:], in_=identf[:]), reads=[b_identf], writes=[b_const])
            P.op("pe", lambda e: e.transpose(out=psA[0][:, 0:64], in_=cstage[:, :], identity=identf[0:64, 0:64]),
                 reads=[b_cstage, b_identf], writes=[psA_b[0]])
            P.op("pe", lambda e: e.transpose(out=psA[1][:, 0:24], in_=gstage[:, :], identity=identf[0:24, 0:24]),
                 reads=[b_gstage, b_identf], writes=[psA_b[1]])
            P.op("act", lambda e: e.activation(out=cst[:, 0:40], in_=psA[0][:, 0:40], func=AF.Copy),
                 reads=[psA_b[0]], writes=[b_cst])
            P.op("act", lambda e: e.activation(out=gains[:].rearrange("p a k -> p (a k)"), in_=psA[1][:, 0:24],
                                               func=AF.Copy), reads=[psA_b[1]], writes=[b_gains])
            P.op("pool", lambda e: e.memset(cst[:, 40:41], -0.5), writes=[b_cst])
            P.op("pool", lambda e: e.memset(cst[:, 41:42], 0.5), writes=[b_cst])
            P.op("dve", lambda e: e.tensor_scalar(out=cst[:, 20:28], in0=cst[:, 20:28], scalar1=0.5, scalar2=None,
                                                  op0=ALU.mult), reads=[], writes=[b_cst])
            P.op("act", lambda e: e.activation(out=cst[:, 44:48], in_=LAM, func=AF.Exp, scale=-1.0), writes=[b_cst])
            P.op("act", lambda e: e.activation(out=cst[:, 44:48], in_=cst[:, 44:48], func=AF.Ln, bias=1.0),
                 writes=[b_cst])
            P.op("dve", lambda e: e.tensor_scalar(out=cst[:, 28:32], in0=cst[:, 44:48], scalar1=-8.0, scalar2=None,
                                                  op0=ALU.mult), writes=[b_cst])
            P.op("dve", lambda e: e.tensor_scalar(out=cst[:, 32:36], in0=cst[:, 44:48], scalar1=-4.0, scalar2=None,
                                                  op0=ALU.mult), writes=[b_cst])
            P.op("dve", lambda e: e.tensor_copy(out=gw[:], in_=gst[:]), reads=[b_gst], writes=[b_const])
            P.op("pool", lambda e: e.memset(ones[:], 1.0), writes=[b_const])
            for blk in range(8):
                P.op("pool", lambda e, blk=blk: e.memset(vext[:, blk, :, 64:65], 1.0), writes=[vext_b[blk]])
            P.op("pool", lambda e: e.memset(qT[:], 0.0), writes=qT_b)
            P.op("pool", lambda e: e.memset(hprev[:], 0.0), writes=hprev_b)
            P.op("pool", lambda e: e.memset(upre[:], 0.0), writes=upre_b)
            SKIPB = dbg is not None and "nobias" in dbg
            SKIPW = dbg is not None and "noweights" in dbg
            P.mute = SKIPB
            P.op("dve", lambda e: e.tensor_copy(out=E8[:, 257:640], in_=E8[:, 256:257].to_broadcast([8, 383])),
                 writes=[b_E8])
            P.dma("sp", ch_e[0], ext, E8[:], reads=[b_E8], writes=[b_ext])
            P.dma("sp", ch_e[1], ext2, bass.AP(tensor=ext.tensor, offset=0, ap=[[640, 8], [0, 128], [1, 640]]),
                  reads=[b_ext], writes=[b_ext2])
            for i, base in enumerate([256, 128]):
                P.dma("sp", ch_e[2], B34[:, i, :, :],
                      bass.AP(tensor=ext2.tensor, offset=base, ap=[[639, 128], [81920, 8], [1, 128]]),
                      reads=[b_ext2], writes=[b_B34])
            P.op("pool", lambda e: e.memset(BNf[:], NEG), writes=[b_BNf])
            for s in range(4):
                P.dma("sp", ch_e[2], BNf[16 * s:16 * s + 16, :, 16 * s:16 * s + 16],
                      bass.AP(tensor=ext2.tensor, offset=128, ap=[[639, 16], [81920, 8], [1, 16]]),
                      reads=[b_ext2], writes=[b_BNf])
            P.seal(ch_e[2], [b_B34, b_BNf])
            for i in range(2):
                P.op("dve", lambda e, i=i: e.tensor_tensor(
                    out=B34[:, i, :, :], in0=B34[:, i, :, :],
                    in1=CBH.unsqueeze(2).to_broadcast([128, 8, 128]), op=ALU.subtract),
                    reads=[b_cst], writes=[b_B34])
                P.op("dve", lambda e, i=i: e.tensor_scalar(out=biasT[:, 1 + i, :, :], in0=B34[:, i, :, :], scalar1=8.0,
                                                          scalar2=None, op0=ALU.mult), reads=[b_B34], writes=[b_bias])
            P.op("pool", lambda e: e.memset(biasT[:, 0, :, :], 0.0), writes=[b_bias])
            P.op("pool", lambda e: e.memset(biasT[0:64, 0, :, 64:128], NEG), writes=[b_bias])
            P.op("pool", lambda e: e.memset(biasT[64:128, 2, :, 0:64], NEG), writes=[b_bias])
            P.op("dve", lambda e: e.tensor_tensor(out=BNf[:], in0=BNf[:],
                                                  in1=CBH[0:64, :].unsqueeze(2).to_broadcast([64, 8, 64]),
                                                  op=ALU.subtract), reads=[b_cst], writes=[b_BNf])
            P.op("dve", lambda e: e.tensor_scalar(out=biasN[:], in0=BNf[:], scalar1=8.0, scalar2=None, op0=ALU.mult),
                 reads=[b_BNf], writes=[b_bias])

            P.mute = SKIPW
            w_in_r = w_in.rearrange("(k p) c -> p k c", p=128)
            w_out_r = w_out.rearrange("(k p) c -> p k c", p=128)
            w_g_r = w_gate.rearrange("(k p) c -> p k c", p=128)
            w_u_r = w_up.rearrange("(k p) c -> p k c", p=128)
            w_d_r = w_down.rearrange("(k p) c -> p k c", p=128)
            eng_rr = ["act", "dve", "act", "dve", "act"]
            cnt = [0]

            def conv_block(blk, loads, gi, views, plain=False, used=4096):
                q = blk % 2
                for dst, src in loads:
                    P.dma("sp", ch_stg[q], dst, src, writes=[stage_b[q]])
                if plain:
                    eng = eng_rr[cnt[0] % 5]
                    cnt[0] += 1
                    if eng == "act":
                        P.op(eng, lambda e: e.activation(out=obf[:, q, 0:used], in_=stage[:, q, 0:used], func=AF.Copy),
                             reads=[stage_b[q]], writes=[obf_b[q]])
                    else:
                        P.op(eng, lambda e: e.tensor_copy(out=obf[:, q, 0:used], in_=stage[:, q, 0:used]),
                             reads=[stage_b[q]], writes=[obf_b[q]])
                else:
                    for k, (iv, ov, gk) in enumerate(views):
                        eng = eng_rr[cnt[0] % 5]
                        cnt[0] += 1
                        gsc = gains[:, gi, gk:gk + 1]
                        if eng == "act":
                            P.op(eng, lambda e, iv=iv, ov=ov, gsc=gsc: e.activation(out=ov, in_=iv, func=AF.Copy,
                                                                                 scale=gsc),
                                 reads=[stage_b[q], b_gains], writes=[obf_b[q]])
                        else:
                            P.op(eng, lambda e, iv=iv, ov=ov, gsc=gsc: e.tensor_scalar(out=ov, in0=iv, scalar1=gsc,
                                                                                    scalar2=None, op0=ALU.mult),
                                 reads=[stage_b[q], b_gains], writes=[obf_b[q]])
                P.dma("pool", ch_obf[q], wsc[blk][:, 0:used], obf[:, q, 0:used], reads=[obf_b[q]], writes=[wsc_b[blk]])

            def v5(t, q, a, b):
                return t[:, q, :].rearrange("p (a b c) -> p a b c", a=a, b=b)
            for i in range(4):
                blk = B_IN[i]
                q = blk % 2
                sv = stage[:, q, :].rearrange("p (k a c) -> p k a c", k=8, a=4)
                ov = v5(obf, q, 4, 8)
                loads = [(stage[:, q, :].rearrange("p (k c) -> p k c", k=8), w_in_r[:, :, i * 512:(i + 1) * 512])]
                conv_block(blk, loads, 0, [(sv[:, k, :, :], ov[:, :, k, :], k) for k in range(8)])
            for blk, c0 in [(B_INV, 2048), (B_INK, 1536)]:
                q = blk % 2
                sv = stage[:, q, :].rearrange("p (k c) -> p k c", k=8)
                ov = obf[:, q, :].rearrange("p (k c) -> p k c", k=8)
                conv_block(blk, [(sv, w_in_r[:, :, c0:c0 + 512])], 0,
                           [(sv[:, k, :], ov[:, k, :], k) for k in range(8)])
            for j in range(2):
                blk = B_OUT[j]
                q = blk % 2
                sv = stage[:, q, :].rearrange("p (k c) -> p k c", k=4)
                ov = obf[:, q, :].rearrange("p (k c) -> p k c", k=4)
                conv_block(blk, [(sv, w_out_r[:, 4 * j:4 * j + 4, :])], 2,
                           [(sv[:, k, :], ov[:, k, :], 4 * j + k) for k in range(4)])
            for i in range(11):
                blk = B_GU[i]
                q = blk % 2
                sv = stage[:, q, :].rearrange("p (g k f c) -> p g k f c", g=2, k=8, f=2)
                ov = obf[:, q, :].rearrange("p (f g k c) -> p f g k c", f=2, g=2, k=8)
                sl = stage[:, q, :].rearrange("p (g k c) -> p g k c", g=2, k=8)
                loads = [(sl[:, 0, :, :], w_g_r[:, :, 2 * i * 128:(2 * i + 2) * 128]),
                         (sl[:, 1, :, :], w_u_r[:, :, 2 * i * 128:(2 * i + 2) * 128])]
                conv_block(blk, loads, 1, [(sv[:, :, k, :, :].rearrange("p g f c -> p f g c"), ov[:, :, :, k, :], k)
                                           for k in range(8)])
            for i in range(6):
                blk = B_DN[i]
                q = blk % 2
                nk = 4 if i < 5 else 2
                sv = stage[:, q, 0:nk * 1024].rearrange("p (k c) -> p k c", k=nk)
                conv_block(blk, [(sv, w_d_r[:, 4 * i:4 * i + nk, :])], None, None, plain=True, used=nk * 1024)
            P.mute = False
            for e in ("sp", "pool", "act", "dve", "pe"):
                P.wait_all(e, [ch_misc] + ch_e + ch_stg + ch_obf)
            P.flush(block)

        block = es.enter_context(nc.Block())
        load_seq = []
        next_load = [0]

        def emit_loads_upto(n):
            while next_load[0] < min(n, len(load_seq)):
                idx = next_load[0]
                blk = load_seq[idx]
                i = idx % NSLOT
                used = 2048 if blk == B_DN[5] else 4096
                P.dma("sp", ch_ring[i], wring[:, i, 0:used], wsc[blk][:, 0:used], reads=[wsc_b[blk]],
                      writes=[ring_b[i]])
                next_load[0] += 1

        def consumed(T, blk):
            emit_loads_upto(T["lpos"][blk] + NSLOT + 1)

        def pool_rsqrt(dst_col, src_col, n, scale, nrow=128, srcs=(), dsts=()):
            P.op("pool", lambda e: e.tensor_scalar(out=st[0:nrow, TMP:TMP + n], in0=st[0:nrow, src_col:src_col + n],
                                                   scalar1=scale, scalar2=EPS, op0=ALU.mult, op1=ALU.add),
                 reads=list(srcs), writes=[stb("tmp")])
            P.op("pool", lambda e: e.tensor_tensor(out=st[0:nrow, dst_col:dst_col + n], in0=st[0:nrow, TMP:TMP + n],
                                                   in1=EM05[0:nrow, :].to_broadcast([nrow, n]), op=ALU.pow),
                 reads=[stb("tmp"), b_cst], writes=list(dsts))

        def norm_transpose(T, b, src_ap, src_b, ss_col, r_col, tag, phase="both"):
            nr = T["nr"]
            if phase in ("both", "stats"):
                P.op("act", lambda e: e.activation(out=junk[0:nr, :], in_=src_ap, func=AF.Square,
                                                   accum_out=st[0:nr, ss_col + b:ss_col + b + 1]),
                     reads=[src_b], writes=[b_junk, stb(tag + "ss%d" % b)])
                pool_rsqrt(r_col + b, ss_col + b, 1, 1.0 / D, nr, [stb(tag + "ss%d" % b)], [stb(tag + "r%d" % b)])
            if phase == "stats":
                return
            q = T["xq"][0] % 2
            T["xq"][0] += 1
            P.op("dve", lambda e: e.tensor_scalar(out=xsb[0:nr, q, :], in0=src_ap, scalar1=st[0:nr, r_col + b:r_col + b + 1],
                                                  scalar2=None, op0=ALU.mult),
                 reads=[src_b, stb(tag + "r%d" % b)], writes=[xsb_b[q]])
            a = nextA()
            pv = psA[a][:].bitcast(BF16).rearrange("p (k c) -> p k c", k=8)
            for k in range(8):
                P.op("pe", lambda e, k=k: e.transpose(out=pv[:, k, 0:nr], in_=xsb[0:nr, q, k * 128:(k + 1) * 128],
                                                      identity=ident[0:nr, 0:nr]),
                     reads=[xsb_b[q], b_const], writes=[psA_b[a]], signal=(k == 7))
            eng = "act" if b % 2 == 0 else "dve"
            dst = actT[:, :, b * 128:b * 128 + nr]
            if eng == "act":
                P.op("act", lambda e: e.activation(out=dst, in_=pv[:, :, 0:nr], func=AF.Copy),
                     reads=[psA_b[a]], writes=[actT_b[b]])
            else:
                P.op("dve", lambda e: e.tensor_copy(out=dst, in_=pv[:, :, 0:nr]), reads=[psA_b[a]], writes=[actT_b[b]])

        def run_tile(T):
            kind = T["kind"]
            NTOK, nb, nr = T["ntok"], T["nb"], T["nr"]
            nseg, L = T["nseg"], T["L"]
            s_idx, t_idx = T["s"], T["t"]
            last = T["last"]
            first = T["first"]
            want_kv = last
            row0 = T["row0"]
            xsrc = T["xsrc"]
            ydst = T["ydst"]
            TS = slice(0, NTOK)
            cur_slot = {blk: idx % NSLOT for blk, idx in T["lpos"].items()}

            def cache_dma(sq_):
                cK = xin[:, 0:2, :].rearrange("p a (b c) -> p (a b) c", b=2)
                cV = xin[:, 2:4, :].rearrange("p a (b c) -> p (a b) c", b=2)
                P.dma("sp", ch_ck, cK, cache_k[sq_].rearrange("(cb p) f -> p cb f", p=128), writes=xin_b[0:2])
                P.dma("sp", ch_cv, cV, cache_v[sq_].rearrange("(cb p) f -> p cb f", p=128), writes=xin_b[2:4])
            if kind == "s":
                cache_dma(0)
            if kind == "p":
                P.dma("sp", ch_X, X[:, :, :], xsrc[row0:row0 + 512, :].rearrange("(b p) d -> p b d", p=128),
                      writes=X_b)
            else:
                P.dma("sp", ch_X, X[0:nr, 0, :], xsrc[row0:row0 + nr, :], writes=[X_b[0]])
            if not T.get("n1_done"):
                for b in range(nb):
                    q = T["xinq"][b]
                    norm_transpose(T, b, xin[0:nr, q, :], xin_b[q], SS1, R1, "n1")
            nxt = T.get("next")
            if dbg is not None and dbg.endswith("_n1"):
                P.mute = True
            UP = upre if kind == "p" else upre_s
            UPB = upre_b if kind == "p" else upre_sb
            HP = hprev if kind == "p" else hprev_s
            HPB = hprev_b if kind == "p" else hprev_sb
            u_view = lambda c: UP[:, c, 0:nseg * (3 + L)].rearrange("p (s l) -> p s l", s=nseg)
            woff = T["woff"]
            def lv(ap):
                return ap.rearrange("p (s l) -> p s l", s=nseg)

            def lru_front(c):
                q = c % 2
                T4 = tmpv(q, 3)[:, TS]
                B4 = tmpb(q, 3)
                uv = u_view(c)
                P.op("dve", lambda e: e.tensor_scalar(
                    out=lv(T4), in0=uv[:, :, 3:3 + L], scalar1=CW(3, c), scalar2=CB(c), op0=ALU.mult, op1=ALU.add),
                    reads=[UPB[c], b_cst], writes=B4)
                for j in (2, 1, 0):
                    P.op("dve", lambda e, j=j: e.scalar_tensor_tensor(
                        out=lv(T4), in0=uv[:, :, j:j + L], scalar=CW(j, c), in1=lv(T4), op0=ALU.mult, op1=ALU.add),
                        reads=[UPB[c], b_cst], writes=B4)
                if last:
                    for sg in range(nseg):
                        dstc = T["convdst"][sg][:, c * 128:(c + 1) * 128].rearrange("j p -> p j")
                        P.dma("pool", ch_conv[c], dstc, uv[:, sg, L:L + 3], reads=[UPB[c]], **NCD)
                if kind == "p" and not last:
                    P.op("pool", lambda e: e.tensor_copy(out=upre[:, c, 0:3], in_=upre[:, c, 512:515]),
                         writes=[upre_b[c]])
                elif kind == "p" and last:
                    P.op("pool", lambda e: e.memset(upre[:, c, 0:3], 0.0), writes=[upre_b[c]])
                P.op("act", lambda e: e.activation(out=ucb[:, q, TS], in_=T4, func=AF.Copy), reads=B4,
                     writes=[ucb_b[q]])
                ar, ai = nextA(), nextA()
                T["gates"][c] = (ar, ai)
                P.op("pe", lambda e: e.matmul(psA[ar][:, TS], lhsT=gw[:, 0, c, :], rhs=ucb[:, q, TS],
                                              start=True, stop=True),
                     reads=[ucb_b[q], b_const], writes=[psA_b[ar]])
                P.op("pe", lambda e: e.matmul(psA[ai][:, TS], lhsT=gw[:, 1, c, :], rhs=ucb[:, q, TS],
                                              start=True, stop=True),
                     reads=[ucb_b[q], b_const], writes=[psA_b[ai]])

            def lru_chainA(c):
                q = c % 2
                T1, T2, T3, T4, T5 = (tmpv(q, k)[:, TS] for k in range(5))
                B1, B2, B3, B4, B5 = (tmpb(q, k) for k in range(5))
                ar, ai = T["gates"][c]
                P.op("act", lambda e: e.activation(out=T5, in_=gbuf[:, c, TS], func=AF.Square),
                     reads=[gbuf_b[c]], writes=B5)
                P.op("pool", lambda e: e.tensor_scalar(out=T5, in0=T5, scalar1=C1, scalar2=C0, op0=ALU.mult,
                                                       op1=ALU.add), writes=B5)
                P.op("pool", lambda e: e.tensor_tensor(out=T5, in0=T5, in1=gbuf[:, c, TS], op=ALU.mult),
                     reads=[gbuf_b[c]], writes=B5)
                P.op("act", lambda e: e.activation(out=T1, in_=psA[ar][:, TS], func=AF.Tanh, bias=HBA(c), scale=0.5),
                     reads=[psA_b[ar], b_cst], writes=B1)
                P.op("act", lambda e: e.activation(out=T2, in_=psA[ai][:, TS], func=AF.Tanh, bias=HBX(c), scale=0.5),
                     reads=[psA_b[ai], b_cst], writes=B2)
                P.op("act", lambda e: e.activation(out=T3, in_=T1, func=AF.Exp, bias=HC8(c), scale=HC8(c)),
                     reads=B1 + [b_cst], writes=B3)
                P.op("act", lambda e: e.activation(out=T1, in_=T1, func=AF.Exp, bias=C8(c), scale=C8(c)),
                     reads=[b_cst], writes=B1)
                P.op("act", lambda e: e.activation(out=T5, in_=T5, func=AF.Tanh), writes=B5)
                P.op("dve", lambda e: e.scalar_tensor_tensor(out=T2, in0=T2, scalar=1.0, in1=T4, op0=ALU.add,
                                                             op1=ALU.mult), reads=B4, writes=B2)
                P.op("dve", lambda e: e.scalar_tensor_tensor(out=T5, in0=T5, scalar=1.0, in1=gbuf[:, c, TS],
                                                             op0=ALU.add, op1=ALU.mult), reads=[gbuf_b[c]], writes=B5)

            def lru_sqrt(c):
                q = c % 2
                T1 = tmpv(q, 0)[:, TS]
                P.op("pool", lambda e: e.tensor_scalar(out=T1, in0=T1, scalar1=1.0, scalar2=0.0, op0=ALU.min, op1=ALU.max),
                     writes=tmpb(q, 0))
                P.op("act", lambda e: e.activation(out=T1, in_=T1, func=AF.Sqrt, bias=0.25, scale=-0.25),
                     writes=tmpb(q, 0))

            def lru_chainB(c):
                q = c % 2
                T1, T2, T3, T4, T5 = (tmpv(q, k)[:, TS] for k in range(5))
                B1, B2, B3, B4, B5 = (tmpb(q, k) for k in range(5))
                P.op("dve", lambda e: e.tensor_tensor(out=T2, in0=T2, in1=T1, op=ALU.mult), reads=B1, writes=B2)
                for sg in range(nseg):
                    init = HP[:, c, sg:sg + 1]
                    P.op("dve", lambda e, sg=sg, init=init: e.tensor_tensor_scan(
                        out=lv(T4)[:, sg, :], data0=lv(T3)[:, sg, :], data1=lv(T2)[:, sg, :], initial=init,
                        op0=ALU.mult, op1=ALU.add), reads=B3 + B2 + [HPB[c]], writes=B4)
                if last:
                    for sg in range(nseg):
                        dsth = T["lrudst"][sg][c * 128:(c + 1) * 128].rearrange("(p o) -> p o", o=1)
                        P.dma("pool", ch_lru[c], dsth, lv(T4)[:, sg, L - 1:L], reads=B4, **NCD)
                if kind == "p":
                    if not last:
                        P.op("pool", lambda e: e.tensor_copy(out=hprev[:, c, 0:1], in_=T4[:, NTOK - 1:NTOK]),
                             reads=B4, writes=[hprev_b[c]])
                    else:
                        P.op("pool", lambda e: e.memset(hprev[:, c, 0:1], 0.0), writes=[hprev_b[c]])
                P.op("dve", lambda e: e.scalar_tensor_tensor(out=actT[:, c, TS], in0=T5, scalar=0.5, in1=T4,
                                                             op0=ALU.mult, op1=ALU.mult),
                     reads=B5 + B4, writes=actT_b[0:nb])
                P.op("pool", lambda e: e.tensor_tensor(out=y2v(q)[:, TS], in0=actT[:, c, TS], in1=actT[:, c, TS],
                                                       op=ALU.mult), reads=actT_b[0:nb], writes=y2b(q))

            def lru_ssl(c):
                q = c % 2
                for b in range(nb):
                    P.op("pe", lambda e, b=b: e.matmul(
                        psO[1][0:nr, 300 + 4 * b + c:301 + 4 * b + c], lhsT=y2v(q)[:, b * 128:b * 128 + nr],
                        rhs=ones[:, 0:1], start=True, stop=True, skip_group_check=True),
                        reads=y2b(q) + [b_const], writes=[psO_b[1]], signal=(b == nb - 1))

            def lru_finish():
                P.op("dve", lambda e: e.tensor_reduce(
                    out=st[0:nr, SSL:SSL + nb], in_=psO[1][0:nr, 300:300 + 4 * nb].rearrange("p (b c) -> p b c", c=4),
                    axis=mybir.AxisListType.X, op=ALU.add), reads=[psO_b[1]], writes=[stb("ssl")])
                P.op("pool", lambda e: e.tensor_scalar(out=st[0:nr, TMP + 4:TMP + 4 + nb], in0=st[0:nr, SSL:SSL + nb],
                                                       scalar1=1.0 / LW, scalar2=EPS, op0=ALU.mult, op1=ALU.add),
                     reads=[stb("ssl")], writes=[stb("tl")])
                P.op("pool", lambda e: e.tensor_tensor(out=st[0:nr, RL:RL + nb], in0=st[0:nr, TMP + 4:TMP + 4 + nb],
                                                       in1=EM05[0:nr, :].to_broadcast([nr, nb]), op=ALU.pow),
                     reads=[stb("tl"), b_cst], writes=[stb("rl")])
                P.op("pool", lambda e: e.tensor_tensor(out=st[0:nr, 56:56 + nb], in0=st[0:nr, TMP + 4:TMP + 4 + nb],
                                                       in1=EP05[0:nr, :].to_broadcast([nr, nb]), op=ALU.pow),
                     reads=[stb("tl"), b_cst], writes=[stb("sql")])

            for m in range(16):
                if m == 8:
                    for c_ in (0, 1):
                        lru_front(c_)
                    for c_ in (0, 1):
                        lru_chainA(c_)
                    for c_ in (0, 1):
                        lru_sqrt(c_)
                blk = B_IN[m // 4]
                sl = cur_slot[blk]
                wv = wring[:, sl, :].rearrange("p (a k c) -> p a k c", a=4, k=8)
                a = nextA()
                for k in range(8):
                    P.op("pe", lambda e, a=a, wv=wv, m=m, k=k: e.matmul(psA[a][:, TS], lhsT=wv[:, m % 4, k, :],
                                                                        rhs=actT[:, k, TS], start=(k == 0),
                                                                        stop=(k == 7)),
                         reads=[ring_b[sl]] + actT_b[0:nb], writes=[psA_b[a]], signal=(k == 7))
                if m % 4 == 3:
                    consumed(T, blk)
                c = m % 4
                if m < 4:
                    dst = u_view(c)[:, :, 3:3 + L]
                    src = psA[a][:, TS].rearrange("p (s l) -> p s l", s=nseg)
                    P.op("dve", lambda e, dst=dst, src=src: e.tensor_copy(out=dst, in_=src),
                         reads=[psA_b[a]], writes=[UPB[c]])
                elif m < 8:
                    P.op("act", lambda e, a=a, c=c: e.activation(out=gbuf[:, c, TS], in_=psA[a][:, TS], func=AF.Copy),
                         reads=[psA_b[a]], writes=[gbuf_b[c]])
                elif m < 12:
                    for e2 in range(2):
                        P.op("act", lambda e, a=a, c=c, e2=e2: e.activation(
                            out=qT[e2 * 64:(e2 + 1) * 64, 2 * c + e2, TS], in_=psA[a][e2 * 64:(e2 + 1) * 64, TS],
                            func=AF.Copy), reads=[psA_b[a]], writes=[qT_b[c]])
                else:
                    P.op("dve", lambda e, a=a, c=c: e.tensor_copy(out=kT[:, c, woff:woff + NTOK], in_=psA[a][:, TS]),
                         reads=[psA_b[a]], writes=[kT_b[c][T["whalf"]]])
            if dbg is not None and dbg.endswith("_in1"):
                P.mute = True
            for which in (["v", "k"] if want_kv else ["v"]):
                blk = B_INV if which == "v" else B_INK
                sl = cur_slot[blk]
                wv = wring[:, sl, :].rearrange("p (k c) -> p k c", k=8)
                for b in range(nb):
                    a = nextA()
                    for k in range(8):
                        P.op("pe", lambda e, a=a, wv=wv, b=b, k=k: e.matmul(
                            psA[a][0:nr, :], lhsT=actT[:, k, b * 128:b * 128 + nr], rhs=wv[:, k, :],
                            start=(k == 0), stop=(k == 7)),
                            reads=[ring_b[sl], actT_b[b]], writes=[psA_b[a]], signal=(k == 7))
                    if which == "v" and not (dbg is not None and "novext" in dbg):
                        vb = T["vblk"][b]
                        if True:
                            P.op("act", lambda e, a=a, vb=vb: e.activation(
                                out=vext[0:nr, vb, :, 0:64], in_=psA[a][0:nr, :].rearrange("p (h d) -> p h d", h=8),
                                func=AF.Copy), reads=[psA_b[a]], writes=[vext_b[vb]])
                        else:
                            P.op("dve", lambda e, a=a, vb=vb: e.tensor_copy(
                                out=vext[0:nr, vb, :, 0:64], in_=psA[a][0:nr, :].rearrange("p (h d) -> p h d", h=8)),
                                reads=[psA_b[a]], writes=[vext_b[vb]])
                    if want_kv:
                        sq = (b * 2 + (0 if which == "v" else 1)) % 4
                        P.op("act", lambda e, a=a, sq=sq: e.activation(out=scrF[0:nr, sq, :], in_=psA[a][0:nr, :],
                                                                      func=AF.Copy),
                             reads=[psA_b[a]], writes=[scrF_b[sq]])
                        dst = T["kvdst"][which][b]
                        if not (dbg is not None and "nokvst" in dbg):
                            P.dma("pool", ch_kv[sq], dst, scrF[0:nr, sq, :], reads=[scrF_b[sq]])
                consumed(T, blk)
            if dbg is not None and dbg.endswith("_in"):
                P.mute = True
            if nxt is not None:
                for b in range(nxt["nb"]):
                    q = nxt["xinq"][b]
                    r0 = nxt["row0"] + b * 128
                    P.dma("sp", ch_xin[q], xin[0:nxt["nr"], q, :], nxt["xsrc"][r0:r0 + nxt["nr"], :],
                          writes=[xin_b[q]])
            def finish_group(yq, hg, po_, nrq):
                P.op("dve", lambda e: e.reciprocal(out=st[0:nrq, RS + hg * 4:RS + hg * 4 + 4],
                                                   in_=psO[po_][0:nrq, 0:260].rearrange("p (h d) -> p h d", h=4)[:, :, 64]),
                     reads=[psO_b[po_]], writes=[stb("rs%d" % hg)])
                P.op("dve", lambda e: e.tensor_tensor(
                    out=ya[0:nrq, yq, hg * 256:(hg + 1) * 256].rearrange("p (h d) -> p h d", h=4),
                    in0=psO[po_][0:nrq, 0:260].rearrange("p (h d) -> p h d", h=4)[:, :, 0:64],
                    in1=st[0:nrq, RS + hg * 4:RS + hg * 4 + 4].unsqueeze(2).to_broadcast([nrq, 4, 64]), op=ALU.mult),
                    reads=[psO_b[po_], stb("rs%d" % hg)], writes=[ya_b[yq]])

            def finish_stats(blocks, nrq):
                for b in blocks:
                    bq = b % 2
                    P.op("act", lambda e, b=b, bq=bq: e.activation(out=yab[0:nrq, bq, :], in_=ya[0:nrq, b, :],
                                                                   func=AF.Square,
                                                                   accum_out=st[0:nrq, SSA + b:SSA + b + 1]),
                         reads=[ya_b[b]], writes=[yab_b[bq], stb("ssa%d" % b)])
                nbk = len(blocks)
                b0 = blocks[0]
                P.op("pool", lambda e: e.tensor_scalar(out=st[0:nrq, TMP:TMP + nbk], in0=st[0:nrq, SSA + b0:SSA + b0 + nbk],
                                                       scalar1=1.0 / LW, scalar2=EPS, op0=ALU.mult, op1=ALU.add),
                     reads=[stb("ssa%d" % b) for b in blocks], writes=[stb("tmp")])
                P.op("pool", lambda e: e.tensor_tensor(out=st[0:nrq, SCA + b0:SCA + b0 + nbk], in0=st[0:nrq, TMP:TMP + nbk],
                                                       in1=EM05[0:nrq, :].to_broadcast([nrq, nbk]), op=ALU.pow),
                     reads=[stb("tmp"), b_cst], writes=[stb("sca%d" % b) for b in blocks])
                P.op("pool", lambda e: e.tensor_tensor(out=st[0:nrq, SCA + b0:SCA + b0 + nbk],
                                                       in0=st[0:nrq, SCA + b0:SCA + b0 + nbk],
                                                       in1=st[0:nrq, 56 + b0:56 + b0 + nbk], op=ALU.mult),
                     reads=[stb("sql")], writes=[stb("sca%d" % b) for b in blocks])

            def finish_apply(blocks, nrq):
                for b in blocks:
                    bq = b % 2
                    P.op("dve", lambda e, b=b, bq=bq: e.tensor_scalar(
                        out=yab[0:nrq, bq, :], in0=ya[0:nrq, b, :], scalar1=st[0:nrq, SCA + b:SCA + b + 1], scalar2=None,
                        op0=ALU.mult), reads=[ya_b[b], stb("sca%d" % b)], writes=[yab_b[bq]])
                    a = nextA()
                    pv = psA[a][:].bitcast(BF16).rearrange("p (k c) -> p k c", k=8)
                    for hp in range(4):
                        P.op("pe", lambda e, hp=hp, pv=pv, bq=bq: e.transpose(
                            out=pv[:, hp, 0:nrq], in_=yab[0:nrq, bq, hp * 128:(hp + 1) * 128],
                            identity=ident[0:nrq, 0:nrq]),
                            reads=[yab_b[bq], b_const], writes=[psA_b[a]], signal=(hp == 3))
                    P.op("act" if b % 2 == 0 else "dve", (lambda e, b=b, pv=pv: e.activation(
                        out=actT[:, 4:8, b * 128:b * 128 + nrq], in_=pv[:, 0:4, 0:nrq], func=AF.Copy)) if b % 2 == 0 else
                        (lambda e, b=b, pv=pv: e.tensor_copy(out=actT[:, 4:8, b * 128:b * 128 + nrq], in_=pv[:, 0:4, 0:nrq])),
                        reads=[psA_b[a]], writes=[actT_b[b]])

            if kind == "p":
                units = []
                for m in range(4):
                    for hg in range(2):
                        kbs = [kb for kb in range(5) if 4 * t_idx + m - 4 + kb >= 0]
                        for kb in kbs:
                            units.append((m, hg, kb, kb == kbs[0], kb == kbs[-1]))

                def qk(u, i):
                    m, hg, kb, fst, lst = u
                    sS = i % 2
                    ablk = (4 * t_idx + m - 4 + kb) % 8
                    kcol = ablk * 128
                    khalf = ablk // 4
                    biased = kb in (0, 3, 4)
                    kbi = {0: 0, 3: 1, 4: 2}.get(kb, 0)
                    sv = psS[sS][:].rearrange("p (j q) -> p j q", j=4)
                    for j in range(4):
                        h = hg * 4 + j
                        hp, po = h // 2, (h % 2) * 64
                        P.op("pe", lambda e, j=j, hp=hp, h=h: e.matmul(
                            sv[:, j, :], lhsT=kT[:, hp, kcol:kcol + 128],
                            rhs=qT[:, h, m * 128:(m + 1) * 128], start=(j == 0),
                            stop=(not biased and j == 3), skip_group_check=True),
                            reads=[kT_b[hp][khalf], qT_b[hp]], writes=[psS_b[sS]],
                            signal=(not biased and j == 3))
                    if biased:
                        for j in range(4):
                            h = hg * 4 + j
                            P.op("pe", lambda e, j=j, h=h: e.matmul(sv[:, j, :], lhsT=ident[:], rhs=biasT[:, kbi, h, :],
                                                                    start=False, stop=(j == 3), skip_group_check=True),
                                 reads=[b_bias, b_const], writes=[psS_b[sS]], signal=(j == 3))
                    sP = i % 4
                    P.op("act", lambda e: e.activation(out=pT[:, sP, :], in_=psS[sS][:], func=AF.Exp, scale=0.125),
                         reads=[psS_b[sS]], writes=[pT_b[sP]])

                def pv_(u, i):
                    m, hg, kb, fst, lst = u
                    sP = i % 4
                    ablk = (4 * t_idx + m - 4 + kb) % 8
                    ov = psO[hg][:, 0:260].rearrange("p (h d) -> p h d", h=4)
                    for j in range(4):
                        h = hg * 4 + j
                        P.op("pe", lambda e, j=j, h=h: e.matmul(
                            ov[:, j, :], lhsT=pT[:, sP, j * 128:(j + 1) * 128], rhs=vext[:, ablk, h, 0:65],
                            start=(fst and j == 0), stop=(lst and j == 3), skip_group_check=True),
                            reads=[pT_b[sP], vext_b[ablk]], writes=[psO_b[hg]], signal=(j == 3))
                    if lst:
                        finish_group(m, hg, hg, 128)
                for pr in range(2):
                    ms = (2 * pr, 2 * pr + 1)
                    if pr == 1:
                        for m in ms:
                            lru_front(m)
                        for m in ms:
                            lru_chainA(m)
                        for m in ms:
                            lru_sqrt(m)
                    for m in ms:
                        lru_chainB(m)
                        um = [(i, u) for i, u in enumerate(units) if u[0] == m]
                        n = len(um)
                        hooks = {}
                        if m == 3:
                            def h_a():
                                lru_ssl(3)
                                lru_finish()

                            def h_b():
                                finish_stats([0, 1, 2], 128)
                            hooks = {min(3, n - 1): h_a, min(5, n - 1) + (1 if n - 1 <= 3 else 0): h_b}
                        for x in range(min(2, n)):
                            qk(um[x][1], um[x][0])
                        for x in range(n):
                            pv_(um[x][1], um[x][0])
                            if x + 2 < n:
                                qk(um[x + 2][1], um[x + 2][0])
                            if x in hooks:
                                hooks.pop(x)()
                        for x in sorted(hooks):
                            hooks[x]()
                        if m < 3:
                            lru_ssl(m)
                finish_stats([3], 128)
                finish_apply([0, 1, 2, 3], 128)
            else:
                cstK = xin[:, 0:2, :].rearrange("p a (b c) -> p (a b) c", b=2)
                cstV = xin[:, 2:4, :].rearrange("p a (b c) -> p (a b) c", b=2)
                ui = [0]
                fstO = [True, True]

                def load_cache(sq_):
                    for half in range(2):
                        qx = T["xq"][0] % 2
                        T["xq"][0] += 1
                        xv = xsb[:, qx, :].rearrange("p (cb f) -> p cb f", cb=2)
                        P.op("dve", lambda e, xv=xv, half=half: e.tensor_copy(out=xv, in_=cstK[:, 2 * half:2 * half + 2, :]),
                             reads=xin_b[0:2], writes=[xsb_b[qx]])
                        a = nextA()
                        pv = psA[a][:].bitcast(BF16).rearrange("p (k c) -> p k c", k=8)
                        for cbl in range(2):
                            for hp in range(4):
                                P.op("pe", lambda e, cbl=cbl, hp=hp, xv=xv, pv=pv: e.transpose(
                                    out=pv[:, cbl * 4 + hp, :], in_=xv[:, cbl, hp * 128:(hp + 1) * 128], identity=ident[:]),
                                    reads=[xsb_b[qx], b_const], writes=[psA_b[a]], signal=(cbl == 1 and hp == 3))
                        for cbl in range(2):
                            cb = 2 * half + cbl
                            P.op("act", lambda e, cb=cb, cbl=cbl, pv=pv: e.activation(
                                out=kT[:, :, cb * 128:(cb + 1) * 128], in_=pv[:, cbl * 4:cbl * 4 + 4, :], func=AF.Copy),
                                reads=[psA_b[a]], writes=[kT_b[hp][0] for hp in range(4)])
                    for cb in range(4):
                        P.op("act", lambda e, cb=cb: e.activation(
                            out=vext[:, cb, :, 0:64], in_=cstV[:, cb, :].rearrange("p (h d) -> p h d", h=8),
                            func=AF.Copy), reads=xin_b[2:4], writes=[vext_b[cb]])

                def cache_unit(sq_, hg, cb):
                    i = ui[0]
                    ui[0] += 1
                    sS, sP = i % 2, i % 4
                    sv = psS[sS][:].rearrange("p (j q) -> p j q", j=4)
                    biased = cb == 3
                    for j in range(4):
                        h = hg * 4 + j
                        hp, po = h // 2, (h % 2) * 64
                        P.op("pe", lambda e, j=j, hp=hp, h=h: e.matmul(
                            sv[:, j, 0:16], lhsT=kT[:, hp, cb * 128:(cb + 1) * 128],
                            rhs=qT[:, h, sq_ * 16:sq_ * 16 + 16], start=(j == 0),
                            stop=(not biased and j == 3), skip_group_check=True),
                            reads=[kT_b[hp][0], qT_b[hp]], writes=[psS_b[sS]], signal=(not biased and j == 3))
                    if biased:
                        for j in range(4):
                            h = hg * 4 + j
                            P.op("pe", lambda e, j=j, h=h: e.matmul(
                                sv[:, j, 0:16], lhsT=ident[:], rhs=biasT[:, 1, h, 0:16], start=False,
                                stop=(j == 3), skip_group_check=True),
                                reads=[b_bias, b_const], writes=[psS_b[sS]], signal=(j == 3))
                    pview = pT[:, sP, 0:256].rearrange("p (j q) -> p j q", j=4)
                    P.op("pool", lambda e: e.memset(pT[:, sP, 0:256], 0.0), writes=[pT_b[sP]])
                    P.op("act", lambda e: e.activation(out=pview[:, :, sq_ * 16:sq_ * 16 + 16], in_=sv[:, :, 0:16],
                                                       func=AF.Exp, scale=0.125),
                         reads=[psS_b[sS]], writes=[pT_b[sP]])
                    ov = psO[hg][:, 0:260].rearrange("p (h d) -> p h d", h=4)
                    f0 = fstO[hg]
                    fstO[hg] = False
                    for j in range(4):
                        h = hg * 4 + j
                        P.op("pe", lambda e, j=j, h=h: e.matmul(
                            ov[0:64, j, :], lhsT=pview[:, j, :], rhs=vext[:, cb, h, 0:65],
                            start=(f0 and j == 0), stop=False, skip_group_check=True),
                            reads=[pT_b[sP], vext_b[cb]], writes=[psO_b[hg]], signal=(j == 3))

                def new_unit(hg):
                    i = ui[0]
                    ui[0] += 1
                    sS, sP = i % 2, i % 4
                    sv = psS[sS][:].rearrange("p (j q) -> p j q", j=4)
                    for j in range(4):
                        h = hg * 4 + j
                        hp, po = h // 2, (h % 2) * 64
                        P.op("pe", lambda e, j=j, hp=hp, h=h: e.matmul(
                            sv[0:64, j, 0:64], lhsT=kT[:, hp, 512:576], rhs=qT[:, h, 0:64],
                            start=(j == 0), stop=False, skip_group_check=True),
                            reads=[kT_b[hp][1], qT_b[hp]], writes=[psS_b[sS]], signal=False)
                    for j in range(4):
                        h = hg * 4 + j
                        P.op("pe", lambda e, j=j, h=h: e.matmul(
                            sv[0:64, j, 0:64], lhsT=ident[0:64, 0:64], rhs=biasN[:, h, :], start=False, stop=(j == 3),
                            skip_group_check=True), reads=[b_bias, b_const], writes=[psS_b[sS]], signal=(j == 3))
                    pview = pT[:, sP, 0:256].rearrange("p (j q) -> p j q", j=4)
                    P.op("act", lambda e: e.activation(out=pview[0:64, :, :], in_=sv[0:64, :, 0:64], func=AF.Exp,
                                                       scale=0.125), reads=[psS_b[sS]], writes=[pT_b[sP]])
                    ov = psO[hg][:, 0:260].rearrange("p (h d) -> p h d", h=4)
                    for j in range(4):
                        h = hg * 4 + j
                        P.op("pe", lambda e, j=j, h=h: e.matmul(
                            ov[0:64, j, :], lhsT=pview[0:64, j, :], rhs=vext[0:64, 4, h, 0:65], start=False,
                            stop=(j == 3), skip_group_check=True),
                            reads=[pT_b[sP], vext_b[4]], writes=[psO_b[hg]], signal=(j == 3))
                    finish_group(0, hg, hg, 64)

                for pr in range(2):
                    cs = (2 * pr, 2 * pr + 1)
                    if pr == 1:
                        for c in cs:
                            lru_front(c)
                        for c in cs:
                            lru_chainA(c)
                        for c in cs:
                            lru_sqrt(c)
                    for c in cs:
                        lru_chainB(c)
                        lru_ssl(c)
                lru_finish()
                for sq_ in range(NS):
                    load_cache(sq_)
                    if sq_ + 1 < NS:
                        cache_dma(sq_ + 1)
                    for hg in range(2):
                        for cb in range(4):
                            cache_unit(sq_, hg, cb)
                for hg in range(2):
                    new_unit(hg)
                finish_stats([0], 64)
                finish_apply([0], 64)

            if dbg is not None and dbg.endswith("_att"):
                P.mute = True
            for b in range(nb):
                a0, a1 = nextA(), nextA()
                for k in range(8):
                    sl = cur_slot[B_OUT[k // 4]]
                    wv = wring[:, sl, :].rearrange("p (k c) -> p k c", k=4)
                    for half, a in enumerate((a0, a1)):
                        P.op("pe", lambda e, a=a, k=k, wv=wv, b=b, half=half: e.matmul(
                            psA[a][0:nr, :], lhsT=actT[:, k, b * 128:b * 128 + nr],
                            rhs=wv[:, k % 4, half * 512:(half + 1) * 512], start=(k == 0), stop=(k == 7)),
                            reads=[ring_b[sl], actT_b[b]], writes=[psA_b[a]], signal=(k == 7))
                for half, a in enumerate((a0, a1)):
                    P.op("dve", lambda e, a=a, b=b, half=half: e.scalar_tensor_tensor(
                        out=X[0:nr, b, half * 512:(half + 1) * 512], in0=psA[a][0:nr, :],
                        scalar=st[0:nr, RL + b:RL + b + 1], in1=X[0:nr, b, half * 512:(half + 1) * 512],
                        op0=ALU.mult, op1=ALU.add), reads=[psA_b[a], stb("rl")], writes=[X_b[b]])
                norm_transpose(T, b, X[0:nr, b, :], X_b[b], SS2, R2, "n2", phase="stats")
            for blk in B_OUT:
                consumed(T, blk)
            if dbg is not None and dbg.endswith("_out"):
                P.mute = True
            for b in range(nb):
                norm_transpose(T, b, X[0:nr, b, :], X_b[b], SS2, R2, "n2", phase="apply")
            for f in range(NF):
                i = f // 2
                sl = cur_slot[B_GU[i]]
                wv = wring[:, sl, :].rearrange("p (a k c) -> p a k c", a=4, k=8)
                ag, au = nextA(), nextA()
                for gu, a in enumerate((ag, au)):
                    for k in range(8):
                        P.op("pe", lambda e, a=a, k=k, wv=wv, gu=gu, f=f: e.matmul(
                            psA[a][:, TS], lhsT=wv[:, (f % 2) * 2 + gu, k, :], rhs=actT[:, k, TS], start=(k == 0),
                            stop=(k == 7)), reads=[ring_b[sl]] + actT_b[0:nb], writes=[psA_b[a]], signal=(k == 7))
                tq = f % 2
                P.op("act", lambda e, ag=ag, tq=tq: e.activation(out=scrF[:, tq, TS], in_=psA[ag][:, TS], func=AF.Tanh,
                                                                 scale=0.5), reads=[psA_b[ag]], writes=[scrF_b[tq]])
                P.op("dve", lambda e, ag=ag, tq=tq: e.scalar_tensor_tensor(
                    out=scrF[:, 2 + tq, TS], in0=scrF[:, tq, TS], scalar=1.0, in1=psA[ag][:, TS], op0=ALU.add,
                    op1=ALU.mult), reads=[scrF_b[tq], psA_b[ag]], writes=[scrF_b[2 + tq]])
                als = [hT_b[f]]
                P.op("dve", lambda e, au=au, tq=tq, f=f: e.scalar_tensor_tensor(
                    out=hTv(f)[:, TS], in0=scrF[:, 2 + tq, TS], scalar=0.5, in1=psA[au][:, TS], op0=ALU.mult,
                    op1=ALU.mult), reads=[scrF_b[2 + tq], psA_b[au]], writes=als)
                if f % 2 == 1:
                    consumed(T, B_GU[i])
            if dbg is not None and dbg.endswith("_ffn"):
                P.mute = True
            for b in range(nb):
                a0, a1 = nextA(), nextA()
                for kf in range(NF):
                    sl = cur_slot[B_DN[kf // 4]]
                    wv = wring[:, sl, :].rearrange("p (k c) -> p k c", k=4)
                    for half, a in enumerate((a0, a1)):
                        P.op("pe", lambda e, a=a, kf=kf, wv=wv, b=b, half=half: e.matmul(
                            psA[a][0:nr, :], lhsT=hTv(kf)[:, b * 128:b * 128 + nr],
                            rhs=wv[:, kf % 4, half * 512:(half + 1) * 512], start=(kf == 0), stop=(kf == NF - 1)),
                            reads=[ring_b[sl], hT_b[kf]], writes=[psA_b[a]], signal=(kf == NF - 1))
                    if b == nb - 1 and (kf % 4 == 3 or kf == NF - 1):
                        consumed(T, B_DN[kf // 4])
                if nxt is not None and b < nxt["nb"] and OVERLAP_N1:
                    qn = nxt["xinq"][b]
                    norm_transpose(nxt, b, xin[0:nxt["nr"], qn, :], xin_b[qn], SS1, R1, "n1")
                    nxt["n1_done"] = True
                for half, a in enumerate((a0, a1)):
                    P.op("dve", lambda e, a=a, b=b, half=half: e.tensor_tensor(
                        out=X[0:nr, b, half * 512:(half + 1) * 512], in0=psA[a][0:nr, :],
                        in1=X[0:nr, b, half * 512:(half + 1) * 512], op=ALU.add), reads=[psA_b[a]], writes=[X_b[b]])
                P.op("act", lambda e, b=b: e.activation(out=junk[0:nr, :], in_=X[0:nr, b, :], func=AF.Square,
                                                        accum_out=st[0:nr, SS3 + b:SS3 + b + 1]),
                     reads=[X_b[b]], writes=[b_junk, stb("n3ss%d" % b)])
                pool_rsqrt(R3 + b, SS3 + b, 1, 1.0 / D, nr, [stb("n3ss%d" % b)], [stb("n3r%d" % b)])
                P.op("dve", lambda e, b=b: e.scalar_tensor_tensor(
                    out=X[0:nr, b, :], in0=X[0:nr, b, :], scalar=st[0:nr, R3 + b:R3 + b + 1], in1=gfin[0:nr, :],
                    op0=ALU.mult, op1=ALU.mult), reads=[stb("n3r%d" % b), b_const], writes=[X_b[b]])
                r0 = row0 + b * 128
                P.dma("pool", ch_y[b], ydst[r0:r0 + nr, :], X[0:nr, b, :], reads=[X_b[b]])

        tiles = []
        xq = [0]
        for s in range(NSEQ):
            for t in range(NT):
                half = t % 2
                tiles.append(dict(
                    kind="p", s=s, t=t, ntok=512, nb=4, nr=128, nseg=1, L=512, first=(t == 0), last=(t == NT - 1),
                    row0=s * SEQ + t * 512, xsrc=xp, ydst=y_prompt, woff=half * 512, whalf=half,
                    vblk=[(4 * t + b) % 8 for b in range(4)], xinq=[0, 1, 2, 3], xq=xq, y2={}, yaq=0, gates={},
                    kvdst={"k": [prompt_k[s][b * 128:(b + 1) * 128, :] for b in range(4)],
                           "v": [prompt_v[s][b * 128:(b + 1) * 128, :] for b in range(4)]},
                    convdst=[prompt_conv[s]], lrudst=[prompt_lru[s]]))
        nr_s = NS * DSEQ
        tiles.append(dict(
            kind="s", s=0, t=0, ntok=nr_s, nb=1, nr=nr_s, nseg=NS, L=DSEQ, first=True, last=True, row0=0, xsrc=xs,
            ydst=y_sample, woff=512, whalf=1, vblk=[4], xinq=[0], xq=xq, y2={}, yaq=0, gates={},
            kvdst={"k": [sample_k[:, :]], "v": [sample_v[:, :]]},
            convdst=[sample_conv[sg] for sg in range(NS)], lrudst=[sample_lru[sg] for sg in range(NS)]))
        for i, T in enumerate(tiles):
            T["next"] = tiles[i + 1] if i + 1 < len(tiles) else None
            seq = B_IN + [B_INV] + ([B_INK] if T["last"] else []) + B_OUT + B_GU + B_DN
            T["lpos"] = {blk: len(load_seq) + j for j, blk in enumerate(seq)}
            load_seq.extend(seq)
        if not (dbg is not None and dbg.startswith("setup")):
            emit_loads_upto(NSLOT)
        T0 = tiles[0]
        for b in range(T0["nb"] if not (dbg is not None and dbg.startswith("setup")) else 0):
            q = T0["xinq"][b]
            r0 = T0["row0"] + b * 128
            P.dma("sp", ch_xin[q], xin[0:T0["nr"], q, :], T0["xsrc"][r0:r0 + T0["nr"], :], writes=[xin_b[q]])
        if dbg is not None and dbg.startswith("setup"):
            tiles = []
        if dbg is not None and dbg.startswith("p1"):
            tiles = tiles[:1]
        if any(T["kind"] == "s" for T in tiles):
            P.op("pool", lambda e: e.memset(upre_s[:], 0.0), writes=upre_sb)
            for sg in range(NS):
                for c in range(4):
                    P.dma("sp", ch_state, upre_s[:, c, sg * 19:sg * 19 + 3],
                          state_conv[sg][:, c * 128:(c + 1) * 128].rearrange("j p -> p j"),
                          writes=[upre_sb[c]], allow_slow_non_contiguous=True)
            for c in range(4):
                P.dma("sp", ch_state, hprev_s[:, c, 0:NS], state_lru[:, c * 128:(c + 1) * 128].rearrange("s p -> p s"),
                      writes=[hprev_sb[c]], allow_slow_non_contiguous=True)
            P.seal(ch_state, upre_sb + hprev_sb)
        for T in tiles:
            run_tile(T)
        P.mute = False
        for e in ("pool",):
            P.wait_all(e, store_chans + ch_ring + ch_xin + [ch_X, ch_state, ch_ck, ch_cv])
        P.flush(block)
    return nc


_CACHE = {}


def kernel(**inputs):
    NC = 8
    B, SEQ, _ = inputs["x_prompt"].shape
    DB, DSEQ, _ = inputs["x_sample"].shape
    NSEQ = B // NC
    NS = DB // NC
    key = (NSEQ, SEQ, NS, DSEQ)
    if key not in _CACHE:
        _CACHE[key] = build(NSEQ, SEQ, NS, DSEQ)
    nc = _CACHE[key]
    f = lambda a: np.ascontiguousarray(a, dtype=np.float32)
    shared = {}
    for n in ["norm_mix", "w_in", "conv_w", "conv_b", "lru_wa", "lru_ba", "lru_wx", "lru_bx", "lru_lambda",
              "rel_bias", "norm_lru_out", "norm_attn_out", "w_out", "norm_ffn", "w_gate", "w_up", "w_down"]:
        shared[n] = f(inputs[n][0])
    shared["norm_final"] = f(inputs["norm_final"])
    in_maps = []
    for c in range(NC):
        m = dict(shared)
        m["x_prompt"] = f(inputs["x_prompt"][c * NSEQ:(c + 1) * NSEQ]).reshape(NSEQ * SEQ, D)
        m["x_sample"] = f(inputs["x_sample"][c * NS:(c + 1) * NS]).reshape(NS * DSEQ, D)
        m["state_conv"] = f(inputs["state_conv"][0, c * NS:(c + 1) * NS])
        m["state_lru"] = f(inputs["state_lru"][0, c * NS:(c + 1) * NS])
        m["cache_k"] = f(inputs["cache_k"][0, c * NS:(c + 1) * NS]).reshape(NS, 512, 512)
        m["cache_v"] = f(inputs["cache_v"][0, c * NS:(c + 1) * NS]).reshape(NS, 512, 512)
        in_maps.append(m)
    res = run_bass_kernel_spmd(nc, in_maps, core_ids=list(range(NC)))
    R = res.results
    cat = lambda n: np.concatenate([np.asarray(r[n], dtype=np.float32) for r in R], axis=0)
    y_prompt = cat("y_prompt").reshape(B, SEQ, D)
    y_sample = cat("y_sample").reshape(DB, DSEQ, D)
    keep = min(512, SEQ)
    outs = (
        y_prompt, y_sample,
        cat("prompt_conv").reshape(1, B, 3, LW), cat("prompt_lru").reshape(1, B, LW),
        cat("prompt_k").reshape(1, B, keep, 8, 64), cat("prompt_v").reshape(1, B, keep, 8, 64),
        cat("sample_conv").reshape(1, DB, 3, LW), cat("sample_lru").reshape(1, DB, LW),
        cat("sample_k").reshape(1, DB, DSEQ, 8, 64), cat("sample_v").reshape(1, DB, DSEQ, 8, 64),
    )
    return outs
```
